# Optimizing a Trainium2 kernel written in Bass

```python
import jax, jax.numpy as jnp
from jax import lax
import numpy as np

D_MODEL = 1024
BATCH = 8
SEQ = 4096
DEPTH = 4

CHUNK = 64
PLE_DIM = 256
EPS = 1e-6
A_HEADS = 4
A_DK = 128
A_DV = 128
A_WIDTH = A_HEADS * A_DV
B_HEADS = 8
B_HD = 64
B_WIDTH = B_HEADS * B_HD
B_LEFT_CHUNKS = 8
B_BAND = (B_LEFT_CHUNKS + 1) * CHUNK
MAX_REL = 128
N_REL = 2 * MAX_REL + 1
D_FF = ((8 * D_MODEL // 3 + 255) // 256) * 256
IN_SIZES = (A_HEADS * A_DK, A_HEADS * A_DK, A_WIDTH, A_WIDTH,
            B_WIDTH, B_WIDTH, B_WIDTH,
            D_MODEL, D_MODEL)
D_IN = sum(IN_SIZES)

kernel_name = "hybrid_hgrn2_bandattn_streaming_trunk"


def rmsnorm(x, g):
    x32 = x.astype(jnp.float32)
    y = x32 * lax.rsqrt(jnp.mean(x32 * x32, axis=-1, keepdims=True) + EPS)
    return (y * g.astype(jnp.float32)).astype(x.dtype)


def head_rms(t, g):
    t32 = t.astype(jnp.float32)
    return t32 * lax.rsqrt(jnp.mean(t32 * t32, axis=-1, keepdims=True) + EPS) * g.astype(jnp.float32)


def hgrn2_mixer(q, f_logit, i, g, lb, onorm_g):
    f32 = jnp.float32
    bsz, seq, _ = q.shape
    nc = seq // CHUNK
    lb32 = lb.astype(f32)
    logf = jnp.logaddexp(jnp.log(lb32), jnp.log1p(-lb32) + jax.nn.log_sigmoid(f_logit.astype(f32)))
    k = 1.0 - jnp.exp(logf)

    def heads(t, d):
        return t.astype(f32).reshape(bsz, nc, CHUNK, A_HEADS, d).transpose(1, 0, 3, 2, 4)

    qc = heads(q, A_DK) * (A_DK ** -0.5)
    kc = heads(k, A_DK)
    lfc = heads(logf, A_DK)
    ic = heads(i, A_DV)
    causal = jnp.tril(jnp.ones((CHUNK, CHUNK), dtype=bool))

    def step(state, inp):
        qt, kt, lft, it = inp
        b = jnp.cumsum(lft, axis=-2)
        o_inter = jnp.einsum('bhtk,bhkv->bhtv', qt * jnp.exp(b), state)
        diff = b[..., :, None, :] - b[..., None, :, :]
        dec = jnp.exp(jnp.where(causal[:, :, None], diff, -jnp.inf))
        attn = jnp.einsum('bhtk,bhsk,bhtsk->bhts', qt, kt, dec)
        o = o_inter + jnp.einsum('bhts,bhsv->bhtv', attn, it)
        b_last = b[..., -1:, :]
        new_state = (jnp.exp(b_last[..., 0, :])[..., None] * state
                     + jnp.einsum('bhsk,bhsv->bhkv', kt * jnp.exp(b_last - b), it))
        return new_state, o

    s0 = jnp.zeros((bsz, A_HEADS, A_DK, A_DV), f32)
    _, o = lax.scan(step, s0, (qc, kc, lfc, ic))
    o = o.transpose(1, 0, 3, 2, 4).reshape(bsz, seq, A_HEADS, A_DV)
    o = head_rms(o, onorm_g).reshape(bsz, seq, A_WIDTH) * jax.nn.silu(g.astype(f32))
    return o.astype(q.dtype)


def band_attention(q, k, v, qn_g, kn_g, rel_bias):
    f32 = jnp.float32
    bsz, seq, _ = q.shape
    nc = seq // CHUNK
    qh = head_rms(q.reshape(bsz, seq, B_HEADS, B_HD), qn_g)
    kh = head_rms(k.reshape(bsz, seq, B_HEADS, B_HD), kn_g)
    vh = v.reshape(bsz, seq, B_HEADS, B_HD).astype(f32)
    pad = B_LEFT_CHUNKS * CHUNK
    kp = jnp.pad(kh, ((0, 0), (pad, 0), (0, 0), (0, 0)))
    vp = jnp.pad(vh, ((0, 0), (pad, 0), (0, 0), (0, 0)))
    band_idx = (jnp.arange(nc) * CHUNK)[:, None] + jnp.arange(B_BAND)[None, :]
    kb = kp[:, band_idx]
    vb = vp[:, band_idx]
    qc = qh.reshape(bsz, nc, CHUNK, B_HEADS, B_HD)
    scores = jnp.einsum('bnqhd,bnkhd->bnhqk', qc, kb) * (B_HD ** -0.5)
    rel = (pad + jnp.arange(CHUNK))[:, None] - jnp.arange(B_BAND)[None, :]
    bias = rel_bias.astype(f32)[:, jnp.clip(rel, -MAX_REL, MAX_REL) + MAX_REL]
    valid = band_idx >= pad
    scores = jnp.where(valid[None, :, None, None, :], scores + bias[None, None], -1e30)
    probs = jax.nn.softmax(scores, axis=-1)
    o = jnp.einsum('bnhqk,bnkhd->bnqhd', probs, vb)
    return o.reshape(bsz, seq, B_WIDTH).astype(q.dtype)


def setup_inputs(seed: int = 0) -> dict:
    key = jax.random.key(seed)
    ks = jax.random.split(key, 24)
    f32 = jnp.float32

    def nrm(k, shape, scale):
        return jax.random.normal(k, shape, f32) * scale

    def gain(k, shape):
        return 1.0 + 0.02 * jax.random.normal(k, shape, f32)

    return {
        "x": nrm(ks[0], (BATCH, SEQ, D_MODEL), 1.0),
        "p": nrm(ks[1], (DEPTH, BATCH, SEQ, PLE_DIM), 1.0),
        "norm_mix_g": gain(ks[2], (DEPTH, D_MODEL)),
        "w_in": nrm(ks[3], (DEPTH, D_MODEL, D_IN), D_MODEL ** -0.5),
        "hgrn_lb_logits": nrm(ks[4], (DEPTH, A_HEADS * A_DK), 0.5),
        "hgrn_onorm_g": gain(ks[5], (DEPTH, A_DV)),
        "attn_qnorm_g": gain(ks[6], (DEPTH, B_HD)),
        "attn_knorm_g": gain(ks[7], (DEPTH, B_HD)),
        "attn_rel_bias": nrm(ks[8], (DEPTH, B_HEADS, N_REL), 0.3),
        "w_branch_a": nrm(ks[9], (DEPTH, A_WIDTH, D_MODEL), A_WIDTH ** -0.5),
        "w_branch_b": nrm(ks[10], (DEPTH, B_WIDTH, D_MODEL), B_WIDTH ** -0.5),
        "w_out": nrm(ks[11], (DEPTH, D_MODEL, D_MODEL), 0.5 * D_MODEL ** -0.5),
        "norm_ffn_g": gain(ks[12], (DEPTH, D_MODEL)),
        "w_ffn_gate": nrm(ks[13], (DEPTH, D_MODEL, D_FF), D_MODEL ** -0.5),
        "w_ffn_up": nrm(ks[14], (DEPTH, D_MODEL, D_FF), D_MODEL ** -0.5),
        "w_ffn_down": nrm(ks[15], (DEPTH, D_FF, D_MODEL), 0.5 * D_FF ** -0.5),
        "norm_ple_g": gain(ks[16], (DEPTH, D_MODEL)),
        "w_ple_gate": nrm(ks[17], (DEPTH, D_MODEL, D_MODEL), D_MODEL ** -0.5),
        "w_ple_proj": nrm(ks[18], (DEPTH, PLE_DIM, D_MODEL), 0.5 * PLE_DIM ** -0.5),
    }


def reference(x, p, norm_mix_g, w_in, hgrn_lb_logits, hgrn_onorm_g, attn_qnorm_g, attn_knorm_g,
              attn_rel_bias, w_branch_a, w_branch_b, w_out, norm_ffn_g, w_ffn_gate, w_ffn_up,
              w_ffn_down, norm_ple_g, w_ple_gate, w_ple_proj):
    lb_soft = jax.nn.softmax(hgrn_lb_logits.astype(jnp.float32), axis=0)
    lb_cum = jnp.cumsum(lb_soft, axis=0)
    lower_bounds = lb_cum - lb_cum[0:1]
    split_at = [int(s) for s in np.cumsum(IN_SIZES)[:-1]]

    for li in range(DEPTH):
        h = rmsnorm(x, norm_mix_g[li])
        z = h @ w_in[li]
        qa, fa, ia, ga, qb, kb, vb, gate_a, gate_b = jnp.split(z, split_at, axis=-1)
        oa = hgrn2_mixer(qa, fa, ia, ga, lower_bounds[li], hgrn_onorm_g[li])
        ob = band_attention(qb, kb, vb, attn_qnorm_g[li], attn_knorm_g[li], attn_rel_bias[li])
        merged = (jax.nn.sigmoid(gate_a) * (oa @ w_branch_a[li])
                  + jax.nn.sigmoid(gate_b) * (ob @ w_branch_b[li]))
        x = x + merged @ w_out[li]
        h2 = rmsnorm(x, norm_ffn_g[li])
        x = x + (jax.nn.silu(h2 @ w_ffn_gate[li]) * (h2 @ w_ffn_up[li])) @ w_ffn_down[li]
        h3 = rmsnorm(x, norm_ple_g[li])
        x = x + jax.nn.sigmoid(h3 @ w_ple_gate[li]) * (p[li].astype(x.dtype) @ w_ple_proj[li])
    return x
```

```python
import contextlib
import numpy as np
import concourse.bass as bass
import concourse.mybir as mybir
from concourse.bass_utils import run_bass_kernel_spmd

F32 = mybir.dt.float32
BF16 = mybir.dt.bfloat16
AF = mybir.ActivationFunctionType
ALU = mybir.AluOpType

D = 1024
SEQ = 4096
DEPTH = 4
TB = 512
NCH = 8
DFF = 2816
NFF = 22
PLE = 256
EPS = 1e-6
G = 4096
NEG = -30000.0
VE = 66

GROUPS = [("q", G), ("f", G), ("ga0", G), ("ga1", G), ("gb0", G), ("gb1", G), ("g", G), ("i", G),
          ("aq", G), ("ak", G), ("av", G), ("ab0", 2560), ("ab1", 2560), ("wa", G), ("wb", G),
          ("wo0", G), ("wo1", G)]
GROUPS += [("ff%d" % i, G) for i in range(11)]
GROUPS += [("dn%d" % i, 2816) for i in range(8)]
GROUPS += [("pg0", G), ("pg1", G), ("pp", 2048)]
GOFF = {}
_o = 0
for _n, _c in GROUPS:
    GOFF[_n] = (_o, _c)
    _o += _c
WCOLS = _o

C_GMIX, C_GFFN, C_GPLE = 0, 32, 64
C_LB = 96
C_ON = 112
C_QN = 116
C_KN = 120
NCST = 124


def _lhsT_slab(W, col0, ncc, nk):
    sub = W[:, col0:col0 + ncc * 128].reshape(nk, 128, ncc, 128)
    return np.ascontiguousarray(sub.transpose(1, 2, 0, 3)).reshape(128, ncc * nk * 128)


def _host_layout(inp):
    w_in = inp["w_in"]
    wl = np.empty((DEPTH, 128, WCOLS), np.float32)
    for l in range(DEPTH):
        def put(name, arr):
            o, c = GOFF[name]
            assert arr.shape == (128, c), (name, arr.shape, c)
            wl[l, :, o:o + c] = arr
        W = w_in[l]
        put("q", _lhsT_slab(W, 0, 4, 8))
        put("f", _lhsT_slab(W, 512, 4, 8))
        put("i", _lhsT_slab(W, 1024, 4, 8))
        put("g", _lhsT_slab(W, 1536, 4, 8))
        put("aq", _lhsT_slab(W, 2048, 4, 8))
        put("ak", _lhsT_slab(W, 2560, 4, 8))
        put("av", np.ascontiguousarray(W[:, 3072:3584].reshape(8, 128, 512).transpose(1, 0, 2)).reshape(128, G))
        put("ga0", _lhsT_slab(W, 3584, 4, 8))
        put("ga1", _lhsT_slab(W, 4096, 4, 8))
        put("gb0", _lhsT_slab(W, 4608, 4, 8))
        put("gb1", _lhsT_slab(W, 5120, 4, 8))
        rb = inp["attn_rel_bias"][l]
        k = np.arange(128)[:, None, None]
        j = np.arange(5)[None, :, None]
        q = np.arange(128)[None, None, :]
        idx = np.clip(512 + q - (128 * j + k), -128, 128) + 128
        ab = rb[:, idx]
        ab = np.ascontiguousarray(ab.transpose(1, 0, 2, 3)).reshape(128, 8 * 640)
        put("ab0", ab[:, :2560])
        put("ab1", ab[:, 2560:])
        put("wa", _lhsT_slab(inp["w_branch_a"][l], 0, 8, 4))
        put("wb", _lhsT_slab(inp["w_branch_b"][l], 0, 8, 4))
        put("wo0", _lhsT_slab(inp["w_out"][l], 0, 4, 8))
        put("wo1", _lhsT_slab(inp["w_out"][l], 512, 4, 8))
        wg, wu = inp["w_ffn_gate"][l], inp["w_ffn_up"][l]
        for i in range(11):
            parts = []
            for jj in (2 * i, 2 * i + 1):
                parts.append(_lhsT_slab(wg, jj * 128, 1, 8))
                parts.append(_lhsT_slab(wu, jj * 128, 1, 8))
            put("ff%d" % i, np.concatenate(parts, axis=1))
        wd = inp["w_ffn_down"][l]
        for oc in range(8):
            put("dn%d" % oc, _lhsT_slab(wd, oc * 128, 1, 22))
        put("pg0", _lhsT_slab(inp["w_ple_gate"][l], 0, 4, 8))
        put("pg1", _lhsT_slab(inp["w_ple_gate"][l], 512, 4, 8))
        put("pp", _lhsT_slab(inp["w_ple_proj"][l], 0, 8, 2))
    cst = np.zeros((128, NCST), np.float32)
    for l in range(DEPTH):
        cst[:, C_GMIX + l * 8:C_GMIX + l * 8 + 8] = inp["norm_mix_g"][l].reshape(8, 128).T
        cst[:, C_GFFN + l * 8:C_GFFN + l * 8 + 8] = inp["norm_ffn_g"][l].reshape(8, 128).T
        cst[:, C_GPLE + l * 8:C_GPLE + l * 8 + 8] = inp["norm_ple_g"][l].reshape(8, 128).T
        cst[:, C_LB + l * 4:C_LB + l * 4 + 4] = inp["hgrn_lb_logits"][l].reshape(4, 128).T
        cst[:, C_ON + l] = inp["hgrn_onorm_g"][l]
        cst[:, C_QN + l] = np.tile(inp["attn_qnorm_g"][l], 2)
        cst[:, C_KN + l] = np.tile(inp["attn_knorm_g"][l], 2)
    return wl, cst


class Reg:
    __slots__ = ("name", "ws", "rs", "excl")

    def __init__(self, name, excl=False):
        self.name = name
        self.ws = {}
        self.rs = []
        self.excl = excl


class Op:
    __slots__ = ("eng", "fn", "deps", "sig", "signal", "dma", "chan", "is_out")

    def __init__(self, eng, fn, dma=False, chan=None):
        self.eng = eng
        self.fn = fn
        self.deps = []
        self.sig = False
        self.signal = None
        self.dma = dma
        self.chan = chan
        self.is_out = False


class Prog:
    def __init__(self, nc, es):
        self.nc = nc
        self.es = es
        self.ops = []
        self.eng_obj = {"pe": nc.tensor, "act": nc.scalar, "dve": nc.vector, "pool": nc.gpsimd, "sp": nc.sync}
        self.sems = {}
        self.chan_sems = {}

    def sem(self, name):
        if name not in self.sems:
            self.sems[name] = self.es.enter_context(self.nc.semaphore(name))
        return self.sems[name]

    def _dep(self, op, d, raw):
        if d is op:
            return
        if d.eng == op.eng and not d.dma:
            if op.dma:
                pass
            elif op.eng == "pe":
                return
            elif not raw:
                return
        d.sig = True
        op.deps.append(d)

    def add(self, eng, fn, r=(), w=(), dma=False, chan=None):
        op = Op(eng, fn, dma, chan)
        if dma:
            op.sig = True
        w = list(w) + [R for R in r if R.excl]
        r = [R for R in r if not R.excl]
        w = list(dict.fromkeys(w))
        for R in r:
            for d in R.ws.values():
                self._dep(op, d, True)
        for R in w:
            for d in R.ws.values():
                self._dep(op, d, False)
            for d in R.rs:
                self._dep(op, d, False)
        for R in r:
            R.rs.append(op)
        for R in w:
            R.ws = {eng if not dma else ("dma", id(op)): op}
            R.rs = []
        self.ops.append(op)
        return op

    def chain(self, eng, fns, r=(), w=()):
        link = Reg("chain")
        op = None
        for i, fn in enumerate(fns):
            rr = list(r) + ([link] if i > 0 else [])
            op = self.add(eng, fn, r=rr, w=list(w) + [link])
        return op

    def dma(self, queue, fn, r=(), w=(), chan="misc", nslots=4):
        if chan not in self.chan_sems:
            self.chan_sems[chan] = {"n": nslots, "i": 0, "last": [None] * nslots,
                                    "sems": [self.sem("d_%s_%d" % (chan, i)) for i in range(nslots)],
                                    "cnt": [0] * nslots}
        return self.add(queue, fn, r, w, dma=True, chan=chan)

    def emit(self):
        waited = {e: {} for e in self.eng_obj}
        cnt = {e: 0 for e in self.eng_obj}
        esem = {e: self.sem("e_" + e) for e in ("pe", "act", "dve", "pool")}
        nwait = 0

        def wait(eng, sem, val):
            nonlocal nwait
            key = id(sem)
            if waited[eng].get(key, 0) < val:
                self.eng_obj[eng].wait_ge(sem, val)
                waited[eng][key] = val
                nwait += 1

        outs = []
        for op in self.ops:
            for d in op.deps:
                s, v = d.signal
                wait(op.eng, s, v)
            if op.dma:
                ch = self.chan_sems[op.chan]
                i = ch["i"]
                ch["i"] = (i + 1) % ch["n"]
                s = ch["sems"][i]
                if ch["cnt"][i] > 0:
                    wait(op.eng, s, ch["cnt"][i])
                inst = op.fn()
                ch["cnt"][i] += 16
                inst.then_inc(s, 16)
                op.signal = (s, ch["cnt"][i])
                if op.is_out:
                    outs.append(op)
            else:
                inst = op.fn()
                if op.sig:
                    cnt[op.eng] += 1
                    inst.then_inc(esem[op.eng], 1)
                    op.signal = (esem[op.eng], cnt[op.eng])
        for op in outs:
            s, v = op.signal
            wait("pool", s, v)
        return nwait


def build(n_tiles, n_layers, debug=False):
    nc = bass.Bass("TRN2", target_bir_lowering=False)
    S = n_tiles * TB
    x_d = nc.dram_tensor("x", [S, D], F32, kind="ExternalInput").ap()
    p_d = nc.dram_tensor("p", [DEPTH, S, PLE], F32, kind="ExternalInput").ap()
    w_d = nc.dram_tensor("w", [DEPTH, 128, WCOLS], F32, kind="ExternalInput").ap()
    c_d = nc.dram_tensor("cst", [128, NCST], F32, kind="ExternalInput").ap()
    o_d = nc.dram_tensor("out", [S, D], F32, kind="ExternalOutput").ap()
    wb_d = nc.dram_tensor("wbf", [DEPTH, 128, WCOLS], BF16, kind="Internal").ap()
    kst_d = nc.dram_tensor("kst", [DEPTH, 128, 4 * TB], BF16, kind="Internal").ap()
    vst_d = nc.dram_tensor("vst", [DEPTH, 128, 4 * 8 * VE], BF16, kind="Internal").ap()
    sst_d = nc.dram_tensor("sst", [DEPTH, 128, 4 * 128], F32, kind="Internal").ap()
    dbg_d = {}

    es = contextlib.ExitStack()
    with es:
        P = Prog(nc, es)

        def sb(name, shape, dt):
            return es.enter_context(nc.sbuf_tensor("s_" + name, shape, dt))

        xT = sb("xT", [128, NCH, TB], F32)
        hT = sb("hT", [128, NCH, TB], BF16)
        NRING = 5
        ring = sb("ring", [128, NRING, G], BF16)
        Kb = sb("Kb", [128, 4, 2 * TB], BF16)
        Vb = sb("Vb", [128, 8, 8, VE], BF16)
        big = sb("big", [128, NFF, TB], BF16)
        cst = sb("cst", [128, NCST], F32)
        identb = sb("identb", [128, 128], BF16)
        identf = sb("identf", [128, 128], F32)
        onesb = sb("onesb", [128, 128], BF16)
        onesblk = sb("onesblk", [128, 128], BF16)
        mask0 = sb("mask0", [128, 128], BF16)
        mask4 = sb("mask4", [128, 128], BF16)
        cmask = sb("cmask", [64, TB], BF16)
        scanm = sb("scanm", [128, TB], BF16)
        lbt = sb("lbt", [128, 4, 4], F32)
        omlt = sb("omlt", [128, 4, 4], F32)
        lbtmp = sb("lbtmp", [128, 4, 4], F32)
        lbs = sb("lbs", [128, 4], F32)
        stg = [sb("stg%d" % i, [128, D], F32) for i in range(2)]
        pTb = sb("pTb", [128, 2, TB], BF16)
        sq = [sb("sq%d" % i, [128, TB], BF16) for i in range(2)]
        NT32 = 5
        f32t = [sb("f32t%d" % i, [128, TB], F32) for i in range(NT32)]
        sigA = [sb("sigA%d" % h, [128, TB], F32) for h in range(4)]
        qbf = [sb("qbf%d" % h, [128, TB], BF16) for h in range(4)]
        iTb = [sb("iT%d" % h, [128, TB], BF16) for h in range(4)]
        sgb16 = [sb("sg%d" % h, [128, TB], BF16) for h in range(4)]
        hL = sb("hL", [128, TB], F32)
        hB = sb("hB", [128, TB], F32)
        hE = sb("hE", [128, TB], F32)
        hF = sb("hF", [128, TB], F32)
        qeT = [sb("qeT%d" % i, [128, TB], BF16) for i in range(2)]
        keT = [sb("keT%d" % i, [128, TB], BF16) for i in range(2)]
        kdT = [sb("kdT%d" % i, [128, TB], BF16) for i in range(2)]
        Sall = [sb("Sall%d" % i, [128, 9, 128], F32) for i in range(2)]
        Sbf = [sb("Sbf%d" % i, [128, 8, 128], BF16) for i in range(2)]
        kdtok = [sb("kdtok%d" % i, [64, 8, 128], BF16) for i in range(2)]
        itok = [sb("itok%d" % i, [64, 8, 128], BF16) for i in range(2)]
        attm = [sb("attm%d" % i, [64, TB], BF16) for i in range(2)]
        oaT = sb("oaT", [128, 4, TB], BF16)
        acol = [sb("acol%d" % i, [128, 8], F32) for i in range(2)]
        qnz = sb("qnz", [128, 8, TB], BF16)
        Pt = [sb("Pt%d" % i, [128, 640], BF16) for i in range(3)]
        obt = [sb("obt%d" % i, [128, 512], BF16) for i in range(2)]
        rden = [sb("rden%d" % i, [128, 4, 1], F32) for i in range(2)]
        obT = sb("obT", [128, 4, TB], BF16)
        if debug:
            dbgbuf = sb("dbgbuf", [128, 512], F32)
            r_dbgbuf = Reg("dbgbuf")
        PSF = es.enter_context(nc.psum_tensor("PSF", [128, 6 * 512], F32))
        PSB = es.enter_context(nc.psum_tensor("PSB", [128, 2 * 1024], BF16))

        def regs(prefix, n):
            return [Reg("%s%d" % (prefix, i)) for i in range(n)]
        r_xT = regs("xT", 8)
        r_hT = regs("hT", 8)
        r_ring = regs("ring", NRING)
        r_Kprev = Reg("Kprev"); r_Kcur = regs("Kcur", 4)
        r_Vprev = Reg("Vprev"); r_Vcur = regs("Vcur", 4)
        r_big = regs("big", NFF)
        r_cst = Reg("cst"); r_const = Reg("const")
        r_stg = regs("stg", 2)
        r_pT = Reg("pT")
        r_sq = regs("sq", 2)
        r_f32t = regs("f32t", 5)
        r_sigA = regs("sigA", 4); r_qbf = regs("qbf", 4); r_iT = regs("iT", 4); r_sg = regs("sg", 4)
        r_hL, r_hB, r_hE, r_hF = Reg("hL"), Reg("hB"), Reg("hE"), Reg("hF")
        r_qeT = regs("qeT", 2); r_keT = regs("keT", 2); r_kdT = regs("kdT", 2)
        r_Sall = regs("Sall", 2); r_Sbf = regs("Sbf", 2)
        r_kdtok = regs("kdtok", 2); r_itok = regs("itok", 2); r_attm = regs("attm", 2)
        r_oaT = regs("oaT", 4)
        r_acol = regs("acol", 2)
        r_lbtmp = Reg("lbtmp")
        r_qnz = regs("qnz", 8)
        r_Pt = regs("Pt", 3); r_obt = regs("obt", 2); r_rden = regs("rden", 2)
        r_obT = regs("obT", 4)
        r_bank = [[Reg("bk%d" % b, excl=True)] for b in range(6)]
        r_bbank = [Reg("bb%d" % i, excl=True) for i in range(2)]
        r_wsrc = Reg("wsrc")
        r_wbf = [[Reg("wbf%d_%s" % (l, n)) for n, _ in GROUPS] for l in range(DEPTH)]
        r_kst = regs("kst", DEPTH); r_vst = regs("vst", DEPTH); r_sst = regs("sst", DEPTH)
        r_out = Reg("out")
        GIDX = {n: i for i, (n, _) in enumerate(GROUPS)}

        state = {"bank": 0, "bb": 0, "f32t": 0, "sq": 0, "ring": 0, "stg": 0}

        state["pool_lo"] = 0
        state["pool_n"] = 6
        state["hbank"] = 0

        def bank():
            b = state["bank"]
            state["bank"] = (b + 1) % state["pool_n"]
            return state["pool_lo"] + b

        def hbank():
            b = state["hbank"]
            state["hbank"] = (b + 1) % 3
            return 3 + b

        def bankap(b):
            return PSF[:, b * 512:(b + 1) * 512]

        def bbank():
            b = state["bb"]
            state["bb"] = (b + 1) % 2
            return b

        def bbankap(b):
            return PSB[:, b * 1024:(b + 1) * 1024]

        def tmp32():
            i = state["f32t"]
            state["f32t"] = (i + 1) % 4
            return i

        def nsq():
            i = state["sq"]
            state["sq"] = (i + 1) % 2
            return i

        def nstg():
            i = state["stg"]
            state["stg"] = (i + 1) % 2
            return i

        V, A, PE, PO = nc.vector, nc.scalar, nc.tensor, nc.gpsimd

        P.dma("sp", lambda: nc.sync.dma_start(out=cst[:], in_=c_d), w=[r_cst], chan="misc")

        consts_fns = [
            lambda: PO.memset(identf[:], 1.0),
            lambda: PO.affine_select(out=identf[:], in_=identf[:], pattern=[[-1, 128]], compare_op=ALU.is_equal,
                                     fill=0.0, base=0, channel_multiplier=1),
            lambda: PO.tensor_copy(out=identb[:], in_=identf[:]),
            lambda: PO.memset(onesb[:], 1.0),
            lambda: PO.memset(onesblk[:], 0.0),
            lambda: PO.memset(onesblk[0:64, 0:64], 1.0),
            lambda: PO.memset(onesblk[64:128, 64:128], 1.0),
            lambda: PO.memset(mask0[:], 0.0),
            lambda: PO.memset(mask0[0:64, 64:128], NEG),
            lambda: PO.memset(mask4[:], 0.0),
            lambda: PO.memset(mask4[64:128, 0:64], NEG),
            lambda: PO.memset(cmask[:], 1.0),
            lambda: PO.affine_select(out=cmask[:], in_=cmask[:], pattern=[[0, 8], [1, 64]], compare_op=ALU.is_ge,
                                     fill=0.0, base=0, channel_multiplier=-1),
            lambda: PO.memset(scanm[:], 1.0),
            lambda: PO.memset(scanm[:].rearrange("p (c l) -> p c l", l=64)[:, :, 0:1], 0.0),
            lambda: PO.memset(qnz[:], 0.0),
            lambda: PO.memset(Vb[:], 1.0),
            lambda: PO.memset(Kb[:], 0.0),
        ]
        P.chain("pool", consts_fns, w=[r_const, r_Kprev, r_Vprev] + r_qnz + r_Kcur + r_Vcur)

        def mk_lb():
            lg = cst[:, C_LB:C_LB + 16].rearrange("p (l h) -> p l h", h=4)
            return A.activation(out=lbtmp[:], in_=lg, func=AF.Exp)
        P.add("act", mk_lb, r=[r_cst], w=[r_lbtmp])

        lb_fns = [
            lambda: V.tensor_tensor(out=lbs[:], in0=lbtmp[:, 0, :], in1=lbtmp[:, 1, :], op=ALU.add),
            lambda: V.tensor_tensor(out=lbs[:], in0=lbs[:], in1=lbtmp[:, 2, :], op=ALU.add),
            lambda: V.tensor_tensor(out=lbs[:], in0=lbs[:], in1=lbtmp[:, 3, :], op=ALU.add),
            lambda: V.reciprocal(out=lbs[:], in_=lbs[:]),
            lambda: V.memset(lbt[:, 0, :], 0.0),
            lambda: V.tensor_tensor(out=lbt[:, 1, :], in0=lbtmp[:, 1, :], in1=lbs[:], op=ALU.mult),
            lambda: V.tensor_tensor(out=lbt[:, 2, :], in0=lbtmp[:, 2, :], in1=lbs[:], op=ALU.mult),
            lambda: V.tensor_tensor(out=lbt[:, 2, :], in0=lbt[:, 2, :], in1=lbt[:, 1, :], op=ALU.add),
            lambda: V.tensor_tensor(out=lbt[:, 3, :], in0=lbtmp[:, 3, :], in1=lbs[:], op=ALU.mult),
            lambda: V.tensor_tensor(out=lbt[:, 3, :], in0=lbt[:, 3, :], in1=lbt[:, 2, :], op=ALU.add),
            lambda: V.tensor_scalar(out=omlt[:], in0=lbt[:], scalar1=-1.0, scalar2=1.0, op0=ALU.mult, op1=ALU.add),
        ]
        P.chain("dve", lb_fns, r=[r_lbtmp], w=[r_const])

        def precast(l):
            for gi, (gn, gc) in enumerate(GROUPS):
                o = GOFF[gn][0]
                P.dma("pool", lambda l=l, o=o, gc=gc: PO.dma_start(out=wb_d[l, :, o:o + gc], in_=w_d[l, :, o:o + gc],
                                                                  max_dma_last_dim=8192),
                      w=[r_wbf[l][gi]], chan="pc", nslots=8)
        precast(0)

        def load_group(l, gn):
            gi = GIDX[gn]
            o, gc = GOFF[gn]
            s = state["ring"]
            state["ring"] = (s + 1) % NRING
            P.dma("sp", lambda: nc.sync.dma_start(out=ring[:, s, 0:gc], in_=wb_d[l, :, o:o + gc]),
                  r=[r_wbf[l][gi]], w=[r_ring[s]], chan="ring%d" % s, nslots=1)
            return s

        def mm_group(b, col0, ncol, pairs, r, part=128, qs=None):
            def fn():
                n = len(pairs)
                ins = None
                for i, (lt, rh) in enumerate(pairs):
                    ins = PE.matmul(PSF[0:part, b * 512 + col0:b * 512 + col0 + ncol], lhsT=lt, rhs=rh,
                                    start=(i == 0), stop=(i == n - 1))
                return ins
            return P.add("pe", fn, r=r, w=r_bank[b])

        def rmsnorm(l, cbase):
            bn = bank()
            for c in range(NCH):
                si = nsq()
                P.add("act", lambda c=c, si=si: A.activation(out=sq[si][:], in_=xT[:, c, :], func=AF.Square),
                      r=[r_xT[c]], w=[r_sq[si]])
                P.add("pe", lambda c=c, si=si: PE.matmul(bankap(bn), lhsT=onesb[:], rhs=sq[si][:],
                                                         start=(c == 0), stop=(c == NCH - 1)),
                      r=[r_sq[si], r_const], w=r_bank[bn])
            t = tmp32()

            P.chain("act", [lambda: A.activation(out=f32t[t][:], in_=bankap(bn), func=AF.Ln, scale=1.0 / D, bias=epsc[:, 0:1]),
                            lambda: A.activation(out=f32t[t][:], in_=f32t[t][:], func=AF.Exp, scale=-0.5)],
                    r=r_bank[bn] + [r_const], w=[r_f32t[t]])
            for c in range(NCH):
                P.add("dve", lambda c=c: V.scalar_tensor_tensor(out=hT[:, c, :], in0=xT[:, c, :],
                                                                scalar=cst[:, cbase + l * 8 + c:cbase + l * 8 + c + 1],
                                                                in1=f32t[t][:], op0=ALU.mult, op1=ALU.mult),
                      r=[r_xT[c], r_f32t[t], r_cst], w=[r_hT[c]])

        epsc = sb("epsc", [128, 4], F32)

        def mk_eps():
            V.memset(epsc[:, 0:1], EPS)
            V.memset(epsc[:, 1:2], 64.0 * EPS)
            return V.memset(epsc[:, 2:3], 0.0)
        P.add("dve", mk_eps, w=[r_const])

        def recip_lp(out, in_):
            with nc.allow_low_precision(reason="fp32 reciprocal, bf16 store of a matmul operand"):
                return V.reciprocal(out=out, in_=in_)

        def dump(name, ap, shape, dt, r):
            if not debug:
                return
            d = nc.dram_tensor("dbg_" + name, shape, dt, kind="ExternalOutput").ap()
            dbg_d[name] = d
            op = P.dma("pool", lambda: PO.dma_start(out=d, in_=ap), r=r, w=[], chan="dbg", nslots=2)
            op.is_out = True

        for ti in range(n_tiles):
            t0 = ti * TB
            for tg in range(4):
                si = nstg()
                P.dma("pool", lambda tg=tg, si=si, t0=t0: PO.dma_start(out=stg[si][:], in_=x_d[t0 + tg * 128:t0 + (tg + 1) * 128, :]),
                      w=[r_stg[si]], chan="xin", nslots=2)
                for half in range(2):
                    b = bank()

                    def ftr(tg=tg, si=si, half=half, b=b):
                        ins = None
                        for cc in range(4):
                            c = half * 4 + cc
                            ins = PE.transpose(PSF[:, b * 512 + cc * 128:b * 512 + (cc + 1) * 128],
                                               stg[si][:, c * 128:(c + 1) * 128], identf[:])
                        return ins
                    P.add("pe", ftr, r=[r_stg[si], r_const], w=r_bank[b])
                    P.add("act", lambda tg=tg, half=half, b=b: A.activation(
                        out=xT[:, half * 4:half * 4 + 4, tg * 128:(tg + 1) * 128],
                        in_=bankap(b).rearrange("p (c t) -> p c t", t=128), func=AF.Copy),
                        r=r_bank[b], w=r_xT[half * 4:half * 4 + 4])

            def layer(ti, l, t0):
                import os
                STG = int(os.environ.get("KSTAGE", "99"))
                if ti == 0 and l + 1 < n_layers:
                    precast(l + 1)
                if STG < 1:
                    return
                if ti > 0:
                    P.dma("pool", lambda l=l: PO.dma_start(out=Kb[:, :, 0:TB], in_=kst_d[l].rearrange("p (a t) -> p a t", t=TB)),
                          r=[r_kst[l]], w=[r_Kprev], chan="kvld", nslots=2)
                    P.dma("pool", lambda l=l: PO.dma_start(out=Vb[:, 0:4, :, :], in_=vst_d[l].rearrange("p (a h e) -> p a h e", h=8, e=VE)),
                          r=[r_vst[l]], w=[r_Vprev], chan="kvld", nslots=2)
                p_si = nstg()
                P.dma("pool", lambda: PO.dma_start(out=stg[p_si][:].rearrange("p (g f) -> p g f", g=4),
                                                   in_=p_d[l, t0:t0 + TB, :].rearrange("(g p) f -> p g f", p=128)),
                      w=[r_stg[p_si]], chan="xin", nslots=2)
                rmsnorm(l, C_GMIX)
                if STG < 2:
                    return
                zslots = {}

                def inproj_fm(gn, evac):
                    s = load_group(l, gn)
                    wv = ring[:, s, :].rearrange("p (j k c) -> p j k c", j=4, k=8)
                    for j in range(4):
                        b = bank()
                        mm_group(b, 0, 512, [(wv[:, j, k, :], hT[:, k, :]) for k in range(8)], r=r_hT + [r_ring[s]])
                        evac(j, b)

                fillers = []

                def add_fm_fillers(gn, evac):
                    st = {}
                    for j in range(4):
                        def fl(j=j):
                            if "s" not in st:
                                st["s"] = load_group(l, gn)
                            s = st["s"]
                            wv = ring[:, s, :].rearrange("p (j k c) -> p j k c", j=4, k=8)
                            b = bank()
                            mm_group(b, 0, 512, [(wv[:, j, k, :], hT[:, k, :]) for k in range(8)], r=r_hT + [r_ring[s]])
                            evac(j, b)
                        fillers.append(fl)

                inproj_fm("f", lambda j, b: P.add("act", lambda: A.activation(out=sigA[j][:], in_=bankap(b), func=AF.Sigmoid),
                                                  r=r_bank[b], w=[r_sigA[j]]))
                inproj_fm("q", lambda j, b: P.add("act", lambda: A.activation(out=qbf[j][:], in_=bankap(b), func=AF.Copy, scale=128.0 ** -0.5),
                                                  r=r_bank[b], w=[r_qbf[j]]))

                def qk_evac(kind):
                    def ev(j, b):
                        t = tmp32()
                        si = nsq()
                        P.add("dve", lambda: V.tensor_copy(out=f32t[t][:], in_=bankap(b)), r=r_bank[b], w=[r_f32t[t]])
                        P.add("act", lambda: A.activation(out=sq[si][:], in_=bankap(b), func=AF.Square), r=r_bank[b], w=[r_sq[si]])
                        b2 = bank()
                        P.add("pe", lambda: PE.matmul(bankap(b2), lhsT=onesblk[:], rhs=sq[si][:], start=True, stop=True),
                              r=[r_sq[si], r_const], w=r_bank[b2])
                        t2 = tmp32()

                        def f1():
                            if kind == "q":
                                return A.activation(out=f32t[t2][:], in_=bankap(b2), func=AF.Ln, scale=1.0, bias=epsc[:, 1:2])
                            return A.activation(out=f32t[t2][:], in_=bankap(b2), func=AF.Ln, scale=1.0 / 64, bias=epsc[:, 0:1])
                        P.chain("act", [f1, lambda: A.activation(out=f32t[t2][:], in_=f32t[t2][:], func=AF.Exp, scale=-0.5)],
                                r=r_bank[b2] + [r_const], w=[r_f32t[t2]])
                        if kind == "q":
                            def g():
                                V.scalar_tensor_tensor(out=qnz[0:64, 2 * j, :], in0=f32t[t][0:64, :], scalar=cst[0:64, C_QN + l:C_QN + l + 1],
                                                       in1=f32t[t2][0:64, :], op0=ALU.mult, op1=ALU.mult)
                                return V.scalar_tensor_tensor(out=qnz[64:128, 2 * j + 1, :], in0=f32t[t][64:128, :],
                                                              scalar=cst[64:128, C_QN + l:C_QN + l + 1],
                                                              in1=f32t[t2][64:128, :], op0=ALU.mult, op1=ALU.mult)
                            P.add("dve", g, r=[r_f32t[t], r_f32t[t2], r_cst], w=[r_qnz[2 * j], r_qnz[2 * j + 1]])
                        else:
                            P.add("dve", lambda: V.scalar_tensor_tensor(out=Kb[:, j, TB:2 * TB], in0=f32t[t][:], scalar=cst[:, C_KN + l:C_KN + l + 1],
                                                                        in1=f32t[t2][:], op0=ALU.mult, op1=ALU.mult),
                                  r=[r_f32t[t], r_f32t[t2], r_cst], w=[r_Kcur[j]])
                    return ev
                add_fm_fillers("aq", qk_evac("q"))
                add_fm_fillers("ak", qk_evac("k"))
                stv = {}
                for tg in range(4):
                    def flv(tg=tg):
                        if "s" not in stv:
                            stv["s"] = load_group(l, "av")
                        s = stv["s"]
                        wv = ring[:, s, :].rearrange("p (k c) -> p k c", k=8)
                        b = bank()
                        mm_group(b, 0, 512, [(hT[:, k, tg * 128:(tg + 1) * 128], wv[:, k, :]) for k in range(8)], r=r_hT + [r_ring[s]])
                        P.add("act", lambda: A.activation(out=Vb[:, 4 + tg, :, 0:64], in_=bankap(b).rearrange("p (h e) -> p h e", e=64), func=AF.Copy),
                              r=r_bank[b], w=[r_Vcur[tg]])
                    fillers.append(flv)

                def gate_evac(idx0):
                    def ev(j, b):
                        t = tmp32()
                        P.add("act", lambda: A.activation(out=f32t[t][:], in_=bankap(b), func=AF.Exp, scale=-1.0), r=r_bank[b], w=[r_f32t[t]])
                        P.chain("dve", [lambda: V.tensor_scalar(out=f32t[t][:], in0=f32t[t][:], scalar1=1.0, scalar2=None, op0=ALU.add),
                                        lambda: recip_lp(big[:, idx0 + j, :], f32t[t][:])],
                                r=[r_f32t[t]], w=[r_f32t[t], r_big[idx0 + j]])
                    return ev
                for gi_, gn in enumerate(("ga0", "ga1", "gb0", "gb1")):
                    add_fm_fillers(gn, gate_evac(gi_ * 4))

                if STG < 3:
                    return
                def hgrn_head(h):
                    hp = h % 2
                    P.add("dve", lambda h=h: V.tensor_scalar(out=sigA[h][:], in0=sigA[h][:], scalar1=omlt[:, l, h:h + 1], scalar2=lbt[:, l, h:h + 1],
                                                             op0=ALU.mult, op1=ALU.add), r=[r_sigA[h], r_const], w=[r_sigA[h]])
                    yield "s0"
                    P.add("act", lambda h=h: A.activation(out=hL[:], in_=sigA[h][:], func=AF.Ln), r=[r_sigA[h]], w=[r_hL])
                    yield "s0"
                    P.add("dve", lambda h=h: V.tensor_tensor_scan(out=hB[:], data0=scanm[:], data1=hL[:], initial=0.0, op0=ALU.mult, op1=ALU.add),
                          r=[r_hL, r_const], w=[r_hB])
                    bv = hB[:].rearrange("p (c t) -> p c t", t=64)
                    P.add("dve", lambda h=h, bv=bv: V.tensor_tensor(out=hF[:].rearrange("p (c t) -> p c t", t=64), in0=bv,
                                                                    in1=bv[:, :, 63:64].to_broadcast([128, 8, 64]), op=ALU.subtract),
                          r=[r_hB], w=[r_hF])
                    P.add("dve", lambda h=h: V.tensor_scalar(out=sigA[h][:], in0=sigA[h][:], scalar1=-1.0, scalar2=1.0, op0=ALU.mult, op1=ALU.add),
                          r=[r_sigA[h]], w=[r_sigA[h]])
                    yield "s0"

                    def fexp(h=h):
                        A.activation(out=hL[:], in_=hB[:], func=AF.Exp)
                        return A.activation(out=hE[:], in_=hB[:], func=AF.Exp, scale=-1.0)
                    P.add("act", fexp, r=[r_hB], w=[r_hL, r_hE])
                    P.add("act", lambda h=h: A.activation(out=hF[:], in_=hF[:], func=AF.Exp, scale=-1.0), r=[r_hF], w=[r_hF])
                    yield "s0"

                    def fqk(h=h, hp=hp):
                        V.tensor_tensor(out=qeT[hp][:], in0=qbf[h][:], in1=hL[:], op=ALU.mult)
                        V.tensor_tensor(out=keT[hp][:], in0=sigA[h][:], in1=hE[:], op=ALU.mult)
                        return V.tensor_tensor(out=kdT[hp][:], in0=sigA[h][:], in1=hF[:], op=ALU.mult)
                    P.add("dve", fqk, r=[r_qbf[h], r_hL, r_hE, r_hF, r_sigA[h]], w=[r_qeT[hp], r_keT[hp], r_kdT[hp]])
                    P.add("dve", lambda h=h, hp=hp: V.tensor_copy(out=acol[hp][:], in_=hL[:, 63::64]), r=[r_hL], w=[r_acol[hp]])
                    if ti == 0:
                        P.add("pool", lambda hp=hp: PO.memset(Sall[hp][:, 0, :], 0.0), w=[r_Sall[hp]])
                    else:
                        P.dma("pool", lambda l=l, h=h, hp=hp: PO.dma_start(out=Sall[hp][:, 0, :], in_=sst_d[l, :, h * 128:(h + 1) * 128]),
                              r=[r_sst[l]], w=[r_Sall[hp]], chan="sld", nslots=2)
                    yield "x"
                    for (src, rsrc, dst, rdst) in ((kdT[hp], r_kdT[hp], kdtok[hp], r_kdtok[hp]), (iTb[h], r_iT[h], itok[hp], r_itok[hp])):
                        bb = bbank()

                        def ftr(src=src, bb=bb):
                            ins = None
                            for c in range(8):
                                ins = PE.transpose(PSB[0:64, bb * 1024 + c * 128:bb * 1024 + (c + 1) * 128], src[:, c * 64:(c + 1) * 64], identb[:])
                            return ins
                        P.add("pe", ftr, r=[rsrc, r_const], w=[r_bbank[bb]])
                        P.add("act", lambda dst=dst, bb=bb: A.activation(out=dst[:].rearrange("p c k -> p (c k)"), in_=PSB[0:64, bb * 1024:(bb + 1) * 1024], func=AF.Copy),
                              r=[r_bbank[bb]], w=[rdst])
                    yield "x"
                    B0, B1 = 2 + 2 * hp, 3 + 2 * hp
                    ba = B0

                    def fat(hp=hp, ba=ba):
                        ins = None
                        for c in range(8):
                            ins = PE.matmul(PSF[0:64, ba * 512 + c * 64:ba * 512 + (c + 1) * 64], lhsT=keT[hp][:, c * 64:(c + 1) * 64],
                                            rhs=qeT[hp][:, c * 64:(c + 1) * 64], start=True, stop=True)
                        return ins
                    P.add("pe", fat, r=[r_keT[hp], r_qeT[hp]], w=r_bank[ba])
                    P.add("dve", lambda hp=hp, ba=ba: V.tensor_tensor(out=attm[hp][:], in0=PSF[0:64, ba * 512:(ba + 1) * 512], in1=cmask[:], op=ALU.mult),
                          r=r_bank[ba] + [r_const], w=[r_attm[hp]])
                    for rnd in range(2):
                        b = B1 if rnd == 0 else B0

                        def fdl(hp=hp, b=b, rnd=rnd):
                            ins = None
                            for cc in range(4):
                                c = rnd * 4 + cc
                                ins = PE.matmul(PSF[:, b * 512 + cc * 128:b * 512 + (cc + 1) * 128], lhsT=kdtok[hp][:, c, :], rhs=itok[hp][:, c, :],
                                                start=True, stop=True)
                            return ins
                        P.add("pe", fdl, r=[r_kdtok[hp], r_itok[hp]], w=r_bank[b])
                        yield "x"
                        for cc in range(4):
                            c = rnd * 4 + cc
                            P.add("dve", lambda hp=hp, c=c, cc=cc, b=b: V.scalar_tensor_tensor(
                                out=Sall[hp][:, c + 1, :], in0=Sall[hp][:, c, :], scalar=acol[hp][:, c:c + 1],
                                in1=PSF[:, b * 512 + cc * 128:b * 512 + (cc + 1) * 128], op0=ALU.mult, op1=ALU.add),
                                r=[r_Sall[hp], r_acol[hp]] + r_bank[b], w=[r_Sall[hp]])
                    P.add("act", lambda hp=hp: A.activation(out=Sbf[hp][:], in_=Sall[hp][:, 0:8, :], func=AF.Copy), r=[r_Sall[hp]], w=[r_Sbf[hp]])
                    if ti + 1 < n_tiles:
                        P.dma("pool", lambda l=l, h=h, hp=hp: PO.dma_start(out=sst_d[l, :, h * 128:(h + 1) * 128], in_=Sall[hp][:, 8, :]),
                              r=[r_Sall[hp]], w=[r_sst[l]], chan="sst", nslots=2)
                    yield "x"
                    bo = B1

                    def fo(hp=hp, bo=bo):
                        ins = None
                        for c in range(8):
                            PE.matmul(PSF[:, bo * 512 + c * 64:bo * 512 + (c + 1) * 64], lhsT=Sbf[hp][:, c, :], rhs=qeT[hp][:, c * 64:(c + 1) * 64],
                                      start=True, stop=False)
                            ins = PE.matmul(PSF[:, bo * 512 + c * 64:bo * 512 + (c + 1) * 64], lhsT=itok[hp][:, c, :], rhs=attm[hp][:, c * 64:(c + 1) * 64],
                                            start=False, stop=True)
                        return ins
                    P.add("pe", fo, r=[r_Sbf[hp], r_qeT[hp], r_itok[hp], r_attm[hp]], w=r_bank[bo])
                    yield "x"
                    si = nsq()
                    P.add("act", lambda si=si, bo=bo: A.activation(out=sq[si][:], in_=bankap(bo), func=AF.Square), r=r_bank[bo], w=[r_sq[si]])
                    bn = B0
                    P.add("pe", lambda si=si, bn=bn: PE.matmul(bankap(bn), lhsT=onesb[:], rhs=sq[si][:], start=True, stop=True),
                          r=[r_sq[si], r_const], w=r_bank[bn])
                    t = 4
                    yield "x"

                    P.chain("act", [lambda t=t, bn=bn: A.activation(out=f32t[t][:], in_=bankap(bn), func=AF.Ln, scale=1.0 / 128, bias=epsc[:, 0:1]),
                                    lambda t=t: A.activation(out=f32t[t][:], in_=f32t[t][:], func=AF.Exp, scale=-0.5)],
                            r=r_bank[bn] + [r_const], w=[r_f32t[t]])
                    yield "x"

                    P.chain("dve", [lambda t=t, bo=bo: V.scalar_tensor_tensor(out=f32t[t][:], in0=bankap(bo), scalar=cst[:, C_ON + l:C_ON + l + 1], in1=f32t[t][:],
                                                                              op0=ALU.mult, op1=ALU.mult),
                                    lambda h=h, t=t: V.tensor_tensor(out=oaT[:, h, :], in0=f32t[t][:], in1=sgb16[h][:], op=ALU.mult)],
                            r=r_bank[bo] + [r_f32t[t], r_cst, r_sg[h]], w=[r_f32t[t], r_oaT[h]])

                def inproj_gi():
                    inproj_fm("g", lambda j, b: P.add("act", lambda: A.activation(out=sgb16[j][:], in_=bankap(b), func=AF.Silu),
                                                      r=r_bank[b], w=[r_sg[j]]))
                    inproj_fm("i", lambda j, b: P.add("dve", lambda: V.tensor_copy(out=iTb[j][:], in_=bankap(b)),
                                                      r=r_bank[b], w=[r_iT[j]]))
                gens = [hgrn_head(h) for h in range(4)]
                state["pool_n"] = 2
                state["bank"] = state["bank"] % 2

                def F(n=1):
                    for _ in range(n):
                        if fillers:
                            fillers.pop(0)()
                started, done, in_s0 = set(), set(), [None]
                first = [True]

                def can_start(h):
                    return h not in started and in_s0[0] is None and (h < 2 or (h - 2) in done)
                while len(done) < 4:
                    progressed = False
                    for h in range(4):
                        if h in done:
                            continue
                        if h not in started:
                            if not can_start(h):
                                continue
                            started.add(h)
                            in_s0[0] = h
                        tag = next(gens[h], None)
                        progressed = True
                        if tag is None:
                            done.add(h)
                            if in_s0[0] == h:
                                in_s0[0] = None
                        elif tag == "x" and in_s0[0] == h:
                            in_s0[0] = None
                        if first[0] and 0 in started and in_s0[0] is None:
                            inproj_gi()
                            first[0] = False
                        F(1)
                    assert progressed
                if first[0]:
                    inproj_gi()
                F(100)
                for g_ in gens:
                    for _ in g_:
                        pass
                state["pool_n"] = 6
                if ti + 1 < n_tiles:
                    P.dma("pool", lambda l=l: PO.dma_start(out=kst_d[l].rearrange("p (a t) -> p a t", t=TB), in_=Kb[:, :, TB:2 * TB]),
                          r=r_Kcur, w=[r_kst[l]], chan="kvst", nslots=2)
                    P.dma("pool", lambda l=l: PO.dma_start(out=vst_d[l].rearrange("p (a h e) -> p a h e", h=8, e=VE), in_=Vb[:, 4:8, :, :]),
                          r=r_Vcur, w=[r_vst[l]], chan="kvst", nslots=2)
                s_ab = [load_group(l, "ab0"), load_group(l, "ab1")]
                if STG < 4:
                    return
                if debug and ti == n_tiles - 1 and l == 0:
                    dump("ab0", ring[:, s_ab[0], 0:2560], [128, 2560], BF16, [r_ring[s_ab[0]]])
                abv = [ring[:, s_ab[0], 0:2560].rearrange("p (h j q) -> p h j q", h=4, j=5),
                       ring[:, s_ab[1], 0:2560].rearrange("p (h j q) -> p h j q", h=4, j=5)]
                rk = [r_Kprev] + r_Kcur

                def att_scores(pr, h, pi):
                    hg, hh = h // 4, h % 4
                    j0 = max(0, 4 - (ti * 4 + pr))
                    bs, b4 = h % 2, 2 + h % 2

                    def fsc():
                        ins = None
                        for j in range(j0, 5):
                            dst = PSF[:, bs * 512 + j * 128:bs * 512 + (j + 1) * 128] if j < 4 else PSF[:, b4 * 512:b4 * 512 + 128]
                            PE.matmul(dst, lhsT=Kb[:, h // 2, (pr + j) * 128:(pr + j + 1) * 128], rhs=qnz[:, h, pr * 128:(pr + 1) * 128],
                                      start=True, stop=False)
                            msk = (j == 0) or (j == 4)
                            ins = PE.matmul(dst, lhsT=identb[:], rhs=abv[hg][:, hh, j, :], start=False, stop=not msk)
                            if msk:
                                ins = PE.matmul(dst, lhsT=identb[:], rhs=(mask0 if j == 0 else mask4)[:], start=False, stop=True)
                        return ins
                    P.add("pe", fsc, r=rk + [r_qnz[h], r_ring[s_ab[hg]], r_const], w=r_bank[bs] + r_bank[b4])

                    def fex():
                        if j0 < 4:
                            A.activation(out=Pt[pi][:, j0 * 128:512], in_=PSF[:, bs * 512 + j0 * 128:(bs + 1) * 512], func=AF.Exp)
                        return A.activation(out=Pt[pi][:, 512:640], in_=PSF[:, b4 * 512:b4 * 512 + 128], func=AF.Exp)
                    P.add("act", fex, r=r_bank[bs] + r_bank[b4], w=[r_Pt[pi]])

                def att_pv(pr, h, pi):
                    hg, hh = h // 4, h % 4
                    j0 = max(0, 4 - (ti * 4 + pr))
                    bov = 4 + hg
                    oi = hg

                    def fpv():
                        ins = None
                        for j in range(j0, 5):
                            ins = PE.matmul(PSF[:, bov * 512 + hh * VE:bov * 512 + (hh + 1) * VE], lhsT=Pt[pi][:, j * 128:(j + 1) * 128],
                                            rhs=Vb[:, pr + j, h, :], start=(j == j0), stop=(j == 4))
                        return ins
                    P.add("pe", fpv, r=[r_Pt[pi], r_Vprev] + r_Vcur, w=r_bank[bov])
                    if hh == 3:
                        def fnm1():
                            pv = PSF[:, bov * 512:bov * 512 + 4 * VE].rearrange("p (h e) -> p h e", e=VE)
                            return V.reciprocal(out=rden[oi][:], in_=pv[:, :, 64:65])

                        def fnm2():
                            pv = PSF[:, bov * 512:bov * 512 + 4 * VE].rearrange("p (h e) -> p h e", e=VE)
                            return V.tensor_tensor(out=obt[pr % 2][:, hg * 256:(hg + 1) * 256].rearrange("p (h e) -> p h e", e=64), in0=pv[:, :, 0:64],
                                                   in1=rden[oi][:].to_broadcast([128, 4, 64]), op=ALU.mult)
                        P.chain("dve", [fnm1, fnm2], r=r_bank[bov], w=[r_rden[oi], r_obt[pr % 2]])
                    if h == 7:
                        bb = bbank()

                        def ftr():
                            ins = None
                            for fc in range(4):
                                ins = PE.transpose(PSB[:, bb * 1024 + fc * 128:bb * 1024 + (fc + 1) * 128], obt[pr % 2][:, fc * 128:(fc + 1) * 128], identb[:])
                            return ins
                        P.add("pe", ftr, r=[r_obt[pr % 2], r_const], w=[r_bbank[bb]])
                        P.add("act", lambda: A.activation(out=obT[:, :, pr * 128:(pr + 1) * 128],
                                                          in_=PSB[:, bb * 1024:bb * 1024 + 512].rearrange("p (c t) -> p c t", t=128), func=AF.Copy),
                              r=[r_bbank[bb]], w=r_obT)

                steps = [(pr, h) for pr in range(4) for h in range(8)]
                LAG = 1
                for i in range(len(steps) + LAG):
                    if i < len(steps):
                        att_scores(steps[i][0], steps[i][1], i % 3)
                    if i >= LAG:
                        att_pv(steps[i - LAG][0], steps[i - LAG][1], (i - LAG) % 3)

                if STG < 5:
                    return
                sa = load_group(l, "wa")
                sb_ = load_group(l, "wb")
                wav = ring[:, sa, :].rearrange("p (j k c) -> p j k c", j=8, k=4)
                wbv = ring[:, sb_, :].rearrange("p (j k c) -> p j k c", j=8, k=4)
                for oc in range(8):
                    ba_, bb_ = bank(), bank()
                    mm_group(ba_, 0, 512, [(wav[:, oc, k, :], oaT[:, k, :]) for k in range(4)], r=r_oaT + [r_ring[sa]])
                    mm_group(bb_, 0, 512, [(wbv[:, oc, k, :], obT[:, k, :]) for k in range(4)], r=r_obT + [r_ring[sb_]])
                    t1, t2 = tmp32(), tmp32()
                    P.add("dve", lambda oc=oc, ba_=ba_, t1=t1: V.tensor_tensor(out=f32t[t1][:], in0=bankap(ba_), in1=big[:, oc, :], op=ALU.mult),
                          r=r_bank[ba_] + [r_big[oc]], w=[r_f32t[t1]])
                    P.add("dve", lambda oc=oc, bb_=bb_, t2=t2: V.tensor_tensor(out=f32t[t2][:], in0=bankap(bb_), in1=big[:, 8 + oc, :], op=ALU.mult),
                          r=r_bank[bb_] + [r_big[8 + oc]], w=[r_f32t[t2]])
                    P.add("dve", lambda oc=oc, t1=t1, t2=t2: V.tensor_tensor(out=hT[:, oc, :], in0=f32t[t1][:], in1=f32t[t2][:], op=ALU.add),
                          r=[r_f32t[t1], r_f32t[t2]], w=[r_hT[oc]])
                for half in range(2):
                    s = load_group(l, "wo%d" % half)
                    wv = ring[:, s, :].rearrange("p (j k c) -> p j k c", j=4, k=8)
                    for j in range(4):
                        oc = half * 4 + j
                        b = bank()
                        mm_group(b, 0, 512, [(wv[:, j, k, :], hT[:, k, :]) for k in range(8)], r=r_hT + [r_ring[s]])
                        P.add("dve", lambda oc=oc, b=b: V.tensor_tensor(out=xT[:, oc, :], in0=xT[:, oc, :], in1=bankap(b), op=ALU.add),
                              r=r_bank[b] + [r_xT[oc]], w=[r_xT[oc]])
                if debug and ti == n_tiles - 1 and l == 0:
                    dump("oaT", oaT[:], [128, 4, TB], BF16, r_oaT)
                    dump("qnz", qnz[:], [128, 8, TB], BF16, r_qnz)
                    dump("Kb", Kb[:], [128, 4, 2 * TB], BF16, [r_Kprev] + r_Kcur)
                    dump("Vb", Vb[:], [128, 8, 8, VE], BF16, [r_Vprev] + r_Vcur)
                    dump("obt1", obt[1][:], [128, 512], BF16, [r_obt[1]])
                    dump("Pt1", Pt[1][:], [128, 640], BF16, [r_Pt[1]])
                    dump("obT", obT[:], [128, 4, TB], BF16, r_obT)
                    dump("x1", xT[:], [128, 8, TB], F32, r_xT)

                if STG < 6:
                    return
                rmsnorm(l, C_GFFN)
                for tg in range(4):
                    b = bank()

                    def ftp(tg=tg, b=b):
                        PE.transpose(PSF[:, b * 512:b * 512 + 128], stg[p_si][:, tg * 256:tg * 256 + 128], identf[:])
                        return PE.transpose(PSF[:, b * 512 + 128:b * 512 + 256], stg[p_si][:, tg * 256 + 128:tg * 256 + 256], identf[:])
                    P.add("pe", ftp, r=[r_stg[p_si], r_const], w=r_bank[b])
                    P.add("act", lambda tg=tg, b=b: A.activation(out=pTb[:, :, tg * 128:(tg + 1) * 128],
                                                                 in_=PSF[:, b * 512:b * 512 + 256].rearrange("p (c t) -> p c t", t=128), func=AF.Copy),
                          r=r_bank[b], w=[r_pT])
                for i in range(11):
                    s = load_group(l, "ff%d" % i)
                    wv = ring[:, s, :].rearrange("p (j k c) -> p j k c", j=4, k=8)
                    for jj in range(2):
                        j = 2 * i + jj
                        bg, bu = bank(), bank()
                        mm_group(bg, 0, 512, [(wv[:, 2 * jj, k, :], hT[:, k, :]) for k in range(8)], r=r_hT + [r_ring[s]])
                        mm_group(bu, 0, 512, [(wv[:, 2 * jj + 1, k, :], hT[:, k, :]) for k in range(8)], r=r_hT + [r_ring[s]])
                        t = tmp32()
                        P.add("act", lambda bg=bg, t=t: A.activation(out=f32t[t][:], in_=bankap(bg), func=AF.Silu), r=r_bank[bg], w=[r_f32t[t]])
                        P.add("dve", lambda j=j, bu=bu, t=t: V.tensor_tensor(out=big[:, j, :], in0=bankap(bu), in1=f32t[t][:], op=ALU.mult),
                              r=r_bank[bu] + [r_f32t[t]], w=[r_big[j]])
                for oc in range(8):
                    s = load_group(l, "dn%d" % oc)
                    wv = ring[:, s, 0:2816].rearrange("p (k c) -> p k c", k=22)
                    b = bank()
                    mm_group(b, 0, 512, [(wv[:, k, :], big[:, k, :]) for k in range(22)], r=r_big + [r_ring[s]])
                    P.add("dve", lambda oc=oc, b=b: V.tensor_tensor(out=xT[:, oc, :], in0=xT[:, oc, :], in1=bankap(b), op=ALU.add),
                          r=r_bank[b] + [r_xT[oc]], w=[r_xT[oc]])

                if STG < 7:
                    return
                rmsnorm(l, C_GPLE)
                spp = None
                for half in range(2):
                    s = load_group(l, "pg%d" % half)
                    if half == 0:
                        spp = load_group(l, "pp")
                    wv = ring[:, s, :].rearrange("p (j k c) -> p j k c", j=4, k=8)
                    ppv = ring[:, spp, 0:2048].rearrange("p (j k c) -> p j k c", j=8, k=2)
                    for j in range(4):
                        oc = half * 4 + j
                        bg, bp = bank(), bank()
                        mm_group(bg, 0, 512, [(wv[:, j, k, :], hT[:, k, :]) for k in range(8)], r=r_hT + [r_ring[s]])
                        mm_group(bp, 0, 512, [(ppv[:, oc, k, :], pTb[:, k, :]) for k in range(2)], r=[r_pT, r_ring[spp]])
                        t = tmp32()
                        P.add("act", lambda bg=bg, t=t: A.activation(out=f32t[t][:], in_=bankap(bg), func=AF.Sigmoid), r=r_bank[bg], w=[r_f32t[t]])
                        P.add("dve", lambda bp=bp, t=t: V.tensor_tensor(out=f32t[t][:], in0=bankap(bp), in1=f32t[t][:], op=ALU.mult),
                              r=r_bank[bp] + [r_f32t[t]], w=[r_f32t[t]])
                        P.add("dve", lambda oc=oc, t=t: V.tensor_tensor(out=xT[:, oc, :], in0=xT[:, oc, :], in1=f32t[t][:], op=ALU.add),
                              r=[r_f32t[t], r_xT[oc]], w=[r_xT[oc]])

            for l in range(n_layers):
                layer(ti, l, t0)
            for tg in range(4):
                si = nstg()
                for half in range(2):
                    b = bank()

                    def ftr(tg=tg, half=half, b=b):
                        ins = None
                        for cc in range(4):
                            c = half * 4 + cc
                            ins = PE.transpose(PSF[:, b * 512 + cc * 128:b * 512 + (cc + 1) * 128], xT[:, c, tg * 128:(tg + 1) * 128], identf[:])
                        return ins
                    P.add("pe", ftr, r=r_xT[half * 4:half * 4 + 4] + [r_const], w=r_bank[b])
                    P.add("act", lambda si=si, half=half, b=b: A.activation(out=stg[si][:, half * 512:(half + 1) * 512], in_=bankap(b), func=AF.Copy),
                          r=r_bank[b], w=[r_stg[si]])
                op = P.dma("pool", lambda tg=tg, si=si, t0=t0: PO.dma_start(out=o_d[t0 + tg * 128:t0 + (tg + 1) * 128, :], in_=stg[si][:]),
                           r=[r_stg[si]], w=[r_out], chan="out", nslots=2)
                op.is_out = True

        nwait = P.emit()
    return nc, dbg_d


_CACHE = {}


def _run(inp, n_tiles, n_layers, ncores, debug=False, trace=False):
    wl, cst = _host_layout(inp)
    key = (n_tiles, n_layers, debug)
    if key not in _CACHE:
        _CACHE[key] = build(n_tiles, n_layers, debug)
    nc, dbg = _CACHE[key]
    S = n_tiles * TB
    in_maps = []
    for b in range(ncores):
        in_maps.append({"x": np.ascontiguousarray(inp["x"][b, :S]),
                        "p": np.ascontiguousarray(inp["p"][:, b, :S]),
                        "w": wl, "cst": cst})
    res = run_bass_kernel_spmd(nc, in_maps, core_ids=list(range(ncores)), trace=trace)
    return res


def kernel(**inputs):
    inp = {k: np.asarray(v) for k, v in inputs.items()}
    res = _run(inp, SEQ // TB, DEPTH, 8)
    out = np.stack([np.asarray(r["out"]) for r in res.results], axis=0)
    return out.astype(np.float32)
```

```python
import contextlib
import numpy as np
import concourse.bass as bass
import concourse.mybir as mybir
from concourse.bass_utils import run_bass_kernel_spmd

F32 = mybir.dt.float32
BF16 = mybir.dt.bfloat16
AF = mybir.ActivationFunctionType
ALU = mybir.AluOpType

D = 1024
SEQ = 4096
DEPTH = 4
TB = 512
NCH = 8
DFF = 2816
NFF = 22
PLE = 256
EPS = 1e-6
G = 4096
NEG = -30000.0
VE = 66

GROUPS = [("q", G), ("f", G), ("ga0", G), ("ga1", G), ("gb0", G), ("gb1", G), ("g", G), ("i", G),
          ("aq", G), ("ak", G), ("av", G), ("ab0", 2560), ("ab1", 2560), ("wa", G), ("wb", G),
          ("wo0", G), ("wo1", G)]
GROUPS += [("ff%d" % i, G) for i in range(11)]
GROUPS += [("dn%d" % i, 2816) for i in range(8)]
GROUPS += [("pg0", G), ("pg1", G), ("pp", 2048)]
GOFF = {}
_o = 0
for _n, _c in GROUPS:
    GOFF[_n] = (_o, _c)
    _o += _c
WCOLS = _o

C_GMIX, C_GFFN, C_GPLE = 0, 32, 64
C_LB = 96
C_ON = 112
C_QN = 116
C_KN = 120
NCST = 124


def _lhsT_slab(W, col0, ncc, nk):
    sub = W[:, col0:col0 + ncc * 128].reshape(nk, 128, ncc, 128)
    return np.ascontiguousarray(sub.transpose(1, 2, 0, 3)).reshape(128, ncc * nk * 128)


def _host_layout(inp):
    w_in = inp["w_in"]
    wl = np.empty((DEPTH, 128, WCOLS), np.float32)
    for l in range(DEPTH):
        def put(name, arr):
            o, c = GOFF[name]
            assert arr.shape == (128, c), (name, arr.shape, c)
            wl[l, :, o:o + c] = arr
        W = w_in[l]
        put("q", _lhsT_slab(W, 0, 4, 8))
        put("f", _lhsT_slab(W, 512, 4, 8))
        put("i", _lhsT_slab(W, 1024, 4, 8))
        put("g", _lhsT_slab(W, 1536, 4, 8))
        put("aq", _lhsT_slab(W, 2048, 4, 8))
        put("ak", _lhsT_slab(W, 2560, 4, 8))
        put("av", np.ascontiguousarray(W[:, 3072:3584].reshape(8, 128, 512).transpose(1, 0, 2)).reshape(128, G))
        put("ga0", _lhsT_slab(W, 3584, 4, 8))
        put("ga1", _lhsT_slab(W, 4096, 4, 8))
        put("gb0", _lhsT_slab(W, 4608, 4, 8))
        put("gb1", _lhsT_slab(W, 5120, 4, 8))
        rb = inp["attn_rel_bias"][l]
        k = np.arange(128)[:, None, None]
        j = np.arange(5)[None, :, None]
        q = np.arange(128)[None, None, :]
        idx = np.clip(512 + q - (128 * j + k), -128, 128) + 128
        ab = rb[:, idx]
        ab = np.ascontiguousarray(ab.transpose(1, 0, 2, 3)).reshape(128, 8 * 640)
        put("ab0", ab[:, :2560])
        put("ab1", ab[:, 2560:])
        put("wa", _lhsT_slab(inp["w_branch_a"][l], 0, 8, 4))
        put("wb", _lhsT_slab(inp["w_branch_b"][l], 0, 8, 4))
        put("wo0", _lhsT_slab(inp["w_out"][l], 0, 4, 8))
        put("wo1", _lhsT_slab(inp["w_out"][l], 512, 4, 8))
        wg, wu = inp["w_ffn_gate"][l], inp["w_ffn_up"][l]
        for i in range(11):
            parts = []
            for jj in (2 * i, 2 * i + 1):
                parts.append(_lhsT_slab(wg, jj * 128, 1, 8))
                parts.append(_lhsT_slab(wu, jj * 128, 1, 8))
            put("ff%d" % i, np.concatenate(parts, axis=1))
        wd = inp["w_ffn_down"][l]
        for oc in range(8):
            put("dn%d" % oc, _lhsT_slab(wd, oc * 128, 1, 22))
        put("pg0", _lhsT_slab(inp["w_ple_gate"][l], 0, 4, 8))
        put("pg1", _lhsT_slab(inp["w_ple_gate"][l], 512, 4, 8))
        put("pp", _lhsT_slab(inp["w_ple_proj"][l], 0, 8, 2))
    cst = np.zeros((128, NCST), np.float32)
    for l in range(DEPTH):
        cst[:, C_GMIX + l * 8:C_GMIX + l * 8 + 8] = inp["norm_mix_g"][l].reshape(8, 128).T
        cst[:, C_GFFN + l * 8:C_GFFN + l * 8 + 8] = inp["norm_ffn_g"][l].reshape(8, 128).T
        cst[:, C_GPLE + l * 8:C_GPLE + l * 8 + 8] = inp["norm_ple_g"][l].reshape(8, 128).T
        cst[:, C_LB + l * 4:C_LB + l * 4 + 4] = inp["hgrn_lb_logits"][l].reshape(4, 128).T
        cst[:, C_ON + l] = inp["hgrn_onorm_g"][l]
        cst[:, C_QN + l] = np.tile(inp["attn_qnorm_g"][l], 2)
        cst[:, C_KN + l] = np.tile(inp["attn_knorm_g"][l], 2)
    return wl, cst


class Reg:
    __slots__ = ("name", "ws", "rs", "excl")

    def __init__(self, name, excl=False):
        self.name = name
        self.ws = {}
        self.rs = []
        self.excl = excl


class Op:
    __slots__ = ("eng", "fn", "deps", "sig", "signal", "dma", "chan", "is_out")

    def __init__(self, eng, fn, dma=False, chan=None):
        self.eng = eng
        self.fn = fn
        self.deps = []
        self.sig = False
        self.signal = None
        self.dma = dma
        self.chan = chan
        self.is_out = False


class Prog:
    def __init__(self, nc, es):
        self.nc = nc
        self.es = es
        self.ops = []
        self.eng_obj = {"pe": nc.tensor, "act": nc.scalar, "dve": nc.vector, "pool": nc.gpsimd, "sp": nc.sync}
        self.sems = {}
        self.chan_sems = {}

    def sem(self, name):
        if name not in self.sems:
            self.sems[name] = self.es.enter_context(self.nc.semaphore(name))
        return self.sems[name]

    def _dep(self, op, d, raw):
        if d is op:
            return
        if d.eng == op.eng and not d.dma:
            if op.dma:
                pass
            elif op.eng == "pe":
                return
            elif not raw:
                return
        d.sig = True
        op.deps.append(d)

    def add(self, eng, fn, r=(), w=(), dma=False, chan=None):
        op = Op(eng, fn, dma, chan)
        if dma:
            op.sig = True
        w = list(w) + [R for R in r if R.excl]
        r = [R for R in r if not R.excl]
        w = list(dict.fromkeys(w))
        for R in r:
            for d in R.ws.values():
                self._dep(op, d, True)
        for R in w:
            for d in R.ws.values():
                self._dep(op, d, False)
            for d in R.rs:
                self._dep(op, d, False)
        for R in r:
            R.rs.append(op)
        for R in w:
            R.ws = {eng if not dma else ("dma", id(op)): op}
            R.rs = []
        self.ops.append(op)
        return op

    def chain(self, eng, fns, r=(), w=()):
        link = Reg("chain")
        op = None
        for i, fn in enumerate(fns):
            rr = list(r) + ([link] if i > 0 else [])
            op = self.add(eng, fn, r=rr, w=list(w) + [link])
        return op

    def dma(self, queue, fn, r=(), w=(), chan="misc", nslots=4):
        if chan not in self.chan_sems:
            self.chan_sems[chan] = {"n": nslots, "i": 0, "last": [None] * nslots,
                                    "sems": [self.sem("d_%s_%d" % (chan, i)) for i in range(nslots)],
                                    "cnt": [0] * nslots}
        return self.add(queue, fn, r, w, dma=True, chan=chan)

    def emit(self):
        waited = {e: {} for e in self.eng_obj}
        cnt = {e: 0 for e in self.eng_obj}
        esem = {e: self.sem("e_" + e) for e in ("pe", "act", "dve", "pool")}
        nwait = 0

        def wait(eng, sem, val):
            nonlocal nwait
            key = id(sem)
            if waited[eng].get(key, 0) < val:
                self.eng_obj[eng].wait_ge(sem, val)
                waited[eng][key] = val
                nwait += 1

        outs = []
        for op in self.ops:
            for d in op.deps:
                s, v = d.signal
                wait(op.eng, s, v)
            if op.dma:
                ch = self.chan_sems[op.chan]
                i = ch["i"]
                ch["i"] = (i + 1) % ch["n"]
                s = ch["sems"][i]
                if ch["cnt"][i] > 0:
                    wait(op.eng, s, ch["cnt"][i])
                inst = op.fn()
                ch["cnt"][i] += 16
                inst.then_inc(s, 16)
                op.signal = (s, ch["cnt"][i])
                if op.is_out:
                    outs.append(op)
            else:
                inst = op.fn()
                if op.sig:
                    cnt[op.eng] += 1
                    inst.then_inc(esem[op.eng], 1)
                    op.signal = (esem[op.eng], cnt[op.eng])
        for op in outs:
            s, v = op.signal
            wait("pool", s, v)
        return nwait


def build(n_tiles, n_layers, debug=False):
    nc = bass.Bass("TRN2", target_bir_lowering=False)
    S = n_tiles * TB
    x_d = nc.dram_tensor("x", [S, D], F32, kind="ExternalInput").ap()
    p_d = nc.dram_tensor("p", [DEPTH, S, PLE], F32, kind="ExternalInput").ap()
    w_d = nc.dram_tensor("w", [DEPTH, 128, WCOLS], F32, kind="ExternalInput").ap()
    c_d = nc.dram_tensor("cst", [128, NCST], F32, kind="ExternalInput").ap()
    o_d = nc.dram_tensor("out", [S, D], F32, kind="ExternalOutput").ap()
    wb_d = nc.dram_tensor("wbf", [DEPTH, 128, WCOLS], BF16, kind="Internal").ap()
    kst_d = nc.dram_tensor("kst", [DEPTH, 128, 4 * TB], BF16, kind="Internal").ap()
    vst_d = nc.dram_tensor("vst", [DEPTH, 128, 4 * 8 * VE], BF16, kind="Internal").ap()
    sst_d = nc.dram_tensor("sst", [DEPTH, 128, 4 * 128], F32, kind="Internal").ap()
    dbg_d = {}

    es = contextlib.ExitStack()
    with es:
        P = Prog(nc, es)

        def sb(name, shape, dt):
            return es.enter_context(nc.sbuf_tensor("s_" + name, shape, dt))

        xT = sb("xT", [128, NCH, TB], F32)
        hT = sb("hT", [128, NCH, TB], BF16)
        NRING = 5
        ring = sb("ring", [128, NRING, G], BF16)
        Kb = sb("Kb", [128, 4, 2 * TB], BF16)
        Vb = sb("Vb", [128, 8, 8, VE], BF16)
        big = sb("big", [128, NFF, TB], BF16)
        cst = sb("cst", [128, NCST], F32)
        identb = sb("identb", [128, 128], BF16)
        identf = sb("identf", [128, 128], F32)
        onesb = sb("onesb", [128, 128], BF16)
        onesblk = sb("onesblk", [128, 128], BF16)
        mask0 = sb("mask0", [128, 128], BF16)
        mask4 = sb("mask4", [128, 128], BF16)
        cmask = sb("cmask", [64, TB], BF16)
        scanm = sb("scanm", [128, TB], BF16)
        lbt = sb("lbt", [128, 4, 4], F32)
        omlt = sb("omlt", [128, 4, 4], F32)
        lbtmp = sb("lbtmp", [128, 4, 4], F32)
        lbs = sb("lbs", [128, 4], F32)
        stg = [sb("stg%d" % i, [128, D], F32) for i in range(2)]
        pTb = sb("pTb", [128, 2, TB], BF16)
        sq = [sb("sq%d" % i, [128, TB], BF16) for i in range(2)]
        NT32 = 5
        f32t = [sb("f32t%d" % i, [128, TB], F32) for i in range(NT32)]
        sigA = [sb("sigA%d" % h, [128, TB], F32) for h in range(4)]
        qbf = [sb("qbf%d" % h, [128, TB], BF16) for h in range(4)]
        iTb = [sb("iT%d" % h, [128, TB], BF16) for h in range(4)]
        sgb16 = [sb("sg%d" % h, [128, TB], BF16) for h in range(4)]
        hL = sb("hL", [128, TB], F32)
        hB = sb("hB", [128, TB], F32)
        hE = sb("hE", [128, TB], F32)
        hF = sb("hF", [128, TB], F32)
        qeT = [sb("qeT%d" % i, [128, TB], BF16) for i in range(2)]
        keT = [sb("keT%d" % i, [128, TB], BF16) for i in range(2)]
        kdT = [sb("kdT%d" % i, [128, TB], BF16) for i in range(2)]
        Sall = [sb("Sall%d" % i, [128, 9, 128], F32) for i in range(2)]
        Sbf = [sb("Sbf%d" % i, [128, 8, 128], BF16) for i in range(2)]
        kdtok = [sb("kdtok%d" % i, [64, 8, 128], BF16) for i in range(2)]
        itok = [sb("itok%d" % i, [64, 8, 128], BF16) for i in range(2)]
        attm = [sb("attm%d" % i, [64, TB], BF16) for i in range(2)]
        oaT = sb("oaT", [128, 4, TB], BF16)
        acol = [sb("acol%d" % i, [128, 8], F32) for i in range(2)]
        qnz = sb("qnz", [128, 8, TB], BF16)
        Pt = [sb("Pt%d" % i, [128, 640], BF16) for i in range(3)]
        obt = [sb("obt%d" % i, [128, 512], BF16) for i in range(2)]
        rden = [sb("rden%d" % i, [128, 4, 1], F32) for i in range(2)]
        obT = sb("obT", [128, 4, TB], BF16)
        if debug:
            dbgbuf = sb("dbgbuf", [128, 512], F32)
            r_dbgbuf = Reg("dbgbuf")
        PSF = es.enter_context(nc.psum_tensor("PSF", [128, 6 * 512], F32))
        PSB = es.enter_context(nc.psum_tensor("PSB", [128, 2 * 1024], BF16))

        def regs(prefix, n):
            return [Reg("%s%d" % (prefix, i)) for i in range(n)]
        r_xT = regs("xT", 8)
        r_hT = regs("hT", 8)
        r_ring = regs("ring", NRING)
        r_Kprev = Reg("Kprev"); r_Kcur = regs("Kcur", 4)
        r_Vprev = Reg("Vprev"); r_Vcur = regs("Vcur", 4)
        r_big = regs("big", NFF)
        r_cst = Reg("cst"); r_const = Reg("const")
        r_stg = regs("stg", 2)
        r_pT = Reg("pT")
        r_sq = regs("sq", 2)
        r_f32t = regs("f32t", 5)
        r_sigA = regs("sigA", 4); r_qbf = regs("qbf", 4); r_iT = regs("iT", 4); r_sg = regs("sg", 4)
        r_hL, r_hB, r_hE, r_hF = Reg("hL"), Reg("hB"), Reg("hE"), Reg("hF")
        r_qeT = regs("qeT", 2); r_keT = regs("keT", 2); r_kdT = regs("kdT", 2)
        r_Sall = regs("Sall", 2); r_Sbf = regs("Sbf", 2)
        r_kdtok = regs("kdtok", 2); r_itok = regs("itok", 2); r_attm = regs("attm", 2)
        r_oaT = regs("oaT", 4)
        r_acol = regs("acol", 2)
        r_lbtmp = Reg("lbtmp")
        r_qnz = regs("qnz", 8)
        r_Pt = regs("Pt", 3); r_obt = regs("obt", 2); r_rden = regs("rden", 2)
        r_obT = regs("obT", 4)
        r_bank = [[Reg("bk%d" % b, excl=True)] for b in range(6)]
        r_bbank = [Reg("bb%d" % i, excl=True) for i in range(2)]
        r_wsrc = Reg("wsrc")
        r_wbf = [[Reg("wbf%d_%s" % (l, n)) for n, _ in GROUPS] for l in range(DEPTH)]
        r_kst = regs("kst", DEPTH); r_vst = regs("vst", DEPTH); r_sst = regs("sst", DEPTH)
        r_out = Reg("out")
        GIDX = {n: i for i, (n, _) in enumerate(GROUPS)}

        state = {"bank": 0, "bb": 0, "f32t": 0, "sq": 0, "ring": 0, "stg": 0}

        state["pool_lo"] = 0
        state["pool_n"] = 6
        state["hbank"] = 0

        def bank():
            b = state["bank"]
            state["bank"] = (b + 1) % state["pool_n"]
            return state["pool_lo"] + b

        def hbank():
            b = state["hbank"]
            state["hbank"] = (b + 1) % 3
            return 3 + b

        def bankap(b):
            return PSF[:, b * 512:(b + 1) * 512]

        def bbank():
            b = state["bb"]
            state["bb"] = (b + 1) % 2
            return b

        def bbankap(b):
            return PSB[:, b * 1024:(b + 1) * 1024]

        def tmp32():
            i = state["f32t"]
            state["f32t"] = (i + 1) % 4
            return i

        def nsq():
            i = state["sq"]
            state["sq"] = (i + 1) % 2
            return i

        def nstg():
            i = state["stg"]
            state["stg"] = (i + 1) % 2
            return i

        V, A, PE, PO = nc.vector, nc.scalar, nc.tensor, nc.gpsimd

        P.dma("sp", lambda: nc.sync.dma_start(out=cst[:], in_=c_d), w=[r_cst], chan="misc")

        consts_fns = [
            lambda: PO.memset(identf[:], 1.0),
            lambda: PO.affine_select(out=identf[:], in_=identf[:], pattern=[[-1, 128]], compare_op=ALU.is_equal,
                                     fill=0.0, base=0, channel_multiplier=1),
            lambda: PO.tensor_copy(out=identb[:], in_=identf[:]),
            lambda: PO.memset(onesb[:], 1.0),
            lambda: PO.memset(onesblk[:], 0.0),
            lambda: PO.memset(onesblk[0:64, 0:64], 1.0),
            lambda: PO.memset(onesblk[64:128, 64:128], 1.0),
            lambda: PO.memset(mask0[:], 0.0),
            lambda: PO.memset(mask0[0:64, 64:128], NEG),
            lambda: PO.memset(mask4[:], 0.0),
            lambda: PO.memset(mask4[64:128, 0:64], NEG),
            lambda: PO.memset(cmask[:], 1.0),
            lambda: PO.affine_select(out=cmask[:], in_=cmask[:], pattern=[[0, 8], [1, 64]], compare_op=ALU.is_ge,
                                     fill=0.0, base=0, channel_multiplier=-1),
            lambda: PO.memset(scanm[:], 1.0),
            lambda: PO.memset(scanm[:].rearrange("p (c l) -> p c l", l=64)[:, :, 0:1], 0.0),
            lambda: PO.memset(qnz[:], 0.0),
            lambda: PO.memset(Vb[:], 1.0),
            lambda: PO.memset(Kb[:], 0.0),
        ]
        P.chain("pool", consts_fns, w=[r_const, r_Kprev, r_Vprev] + r_qnz + r_Kcur + r_Vcur)

        def mk_lb():
            lg = cst[:, C_LB:C_LB + 16].rearrange("p (l h) -> p l h", h=4)
            return A.activation(out=lbtmp[:], in_=lg, func=AF.Exp)
        P.add("act", mk_lb, r=[r_cst], w=[r_lbtmp])

        lb_fns = [
            lambda: V.tensor_tensor(out=lbs[:], in0=lbtmp[:, 0, :], in1=lbtmp[:, 1, :], op=ALU.add),
            lambda: V.tensor_tensor(out=lbs[:], in0=lbs[:], in1=lbtmp[:, 2, :], op=ALU.add),
            lambda: V.tensor_tensor(out=lbs[:], in0=lbs[:], in1=lbtmp[:, 3, :], op=ALU.add),
            lambda: V.reciprocal(out=lbs[:], in_=lbs[:]),
            lambda: V.memset(lbt[:, 0, :], 0.0),
            lambda: V.tensor_tensor(out=lbt[:, 1, :], in0=lbtmp[:, 1, :], in1=lbs[:], op=ALU.mult),
            lambda: V.tensor_tensor(out=lbt[:, 2, :], in0=lbtmp[:, 2, :], in1=lbs[:], op=ALU.mult),
            lambda: V.tensor_tensor(out=lbt[:, 2, :], in0=lbt[:, 2, :], in1=lbt[:, 1, :], op=ALU.add),
            lambda: V.tensor_tensor(out=lbt[:, 3, :], in0=lbtmp[:, 3, :], in1=lbs[:], op=ALU.mult),
            lambda: V.tensor_tensor(out=lbt[:, 3, :], in0=lbt[:, 3, :], in1=lbt[:, 2, :], op=ALU.add),
            lambda: V.tensor_scalar(out=omlt[:], in0=lbt[:], scalar1=-1.0, scalar2=1.0, op0=ALU.mult, op1=ALU.add),
        ]
        P.chain("dve", lb_fns, r=[r_lbtmp], w=[r_const])

        def precast(l):
            for gi, (gn, gc) in enumerate(GROUPS):
                o = GOFF[gn][0]
                P.dma("pool", lambda l=l, o=o, gc=gc: PO.dma_start(out=wb_d[l, :, o:o + gc], in_=w_d[l, :, o:o + gc],
                                                                  max_dma_last_dim=8192),
                      w=[r_wbf[l][gi]], chan="pc", nslots=8)
        precast(0)

        def load_group(l, gn):
            gi = GIDX[gn]
            o, gc = GOFF[gn]
            s = state["ring"]
            state["ring"] = (s + 1) % NRING
            P.dma("sp", lambda: nc.sync.dma_start(out=ring[:, s, 0:gc], in_=wb_d[l, :, o:o + gc]),
                  r=[r_wbf[l][gi]], w=[r_ring[s]], chan="ring%d" % s, nslots=1)
            return s

        def mm_group(b, col0, ncol, pairs, r, part=128, qs=None):
            def fn():
                n = len(pairs)
                ins = None
                for i, (lt, rh) in enumerate(pairs):
                    ins = PE.matmul(PSF[0:part, b * 512 + col0:b * 512 + col0 + ncol], lhsT=lt, rhs=rh,
                                    start=(i == 0), stop=(i == n - 1))
                return ins
            return P.add("pe", fn, r=r, w=r_bank[b])

        NORM_BANK = 5

        def norm_acc(c, bn):
            si = nsq()
            P.add("act", lambda: A.activation(out=sq[si][:], in_=xT[:, c, :], func=AF.Square),
                  r=[r_xT[c]], w=[r_sq[si]])

            def mm():
                P.add("pe", lambda: PE.matmul(bankap(bn), lhsT=onesb[:], rhs=sq[si][:],
                                              start=(c == 0), stop=(c == NCH - 1)),
                      r=[r_sq[si], r_const], w=r_bank[bn])
            return mm

        def norm_finish(l, cbase, bn):
            t = tmp32()
            P.chain("act", [lambda: A.activation(out=f32t[t][:], in_=bankap(bn), func=AF.Ln, scale=1.0 / D, bias=epsc[:, 0:1]),
                            lambda: A.activation(out=f32t[t][:], in_=f32t[t][:], func=AF.Exp, scale=-0.5)],
                    r=r_bank[bn] + [r_const], w=[r_f32t[t]])
            for c in range(NCH):
                P.add("dve", lambda c=c: V.scalar_tensor_tensor(out=hT[:, c, :], in0=xT[:, c, :],
                                                                scalar=cst[:, cbase + l * 8 + c:cbase + l * 8 + c + 1],
                                                                in1=f32t[t][:], op0=ALU.mult, op1=ALU.mult),
                      r=[r_xT[c], r_f32t[t], r_cst], w=[r_hT[c]])

        def rmsnorm(l, cbase):
            if state.get("prenorm"):
                state["prenorm"] = False
                norm_finish(l, cbase, NORM_BANK)
                return
            bn = bank()
            for c in range(NCH):
                norm_acc(c, bn)()
            norm_finish(l, cbase, bn)

        class FusedNorm:
            def __init__(self, enable=True):
                self.enable = enable
                self.pending = None
                if enable:
                    state["pool_n"] = 5
                    state["bank"] = state["bank"] % 5

            def chunk_done(self, c):
                if not self.enable:
                    return
                if self.pending is not None:
                    self.pending()
                self.pending = norm_acc(c, NORM_BANK)

            def close(self):
                if not self.enable:
                    return
                self.pending()
                state["pool_n"] = 6
                state["prenorm"] = True

        epsc = sb("epsc", [128, 4], F32)

        def mk_eps():
            V.memset(epsc[:, 0:1], EPS)
            V.memset(epsc[:, 1:2], 64.0 * EPS)
            return V.memset(epsc[:, 2:3], 0.0)
        P.add("dve", mk_eps, w=[r_const])

        def recip_lp(out, in_):
            with nc.allow_low_precision(reason="fp32 reciprocal, bf16 store of a matmul operand"):
                return V.reciprocal(out=out, in_=in_)

        def dump(name, ap, shape, dt, r):
            if not debug:
                return
            d = nc.dram_tensor("dbg_" + name, shape, dt, kind="ExternalOutput").ap()
            dbg_d[name] = d
            op = P.dma("pool", lambda: PO.dma_start(out=d, in_=ap), r=r, w=[], chan="dbg", nslots=2)
            op.is_out = True

        for ti in range(n_tiles):
            t0 = ti * TB
            for tg in range(4):
                si = nstg()
                P.dma("pool", lambda tg=tg, si=si, t0=t0: PO.dma_start(out=stg[si][:], in_=x_d[t0 + tg * 128:t0 + (tg + 1) * 128, :]),
                      w=[r_stg[si]], chan="xin", nslots=2)
                for half in range(2):
                    b = bank()

                    def ftr(tg=tg, si=si, half=half, b=b):
                        ins = None
                        for cc in range(4):
                            c = half * 4 + cc
                            ins = PE.transpose(PSF[:, b * 512 + cc * 128:b * 512 + (cc + 1) * 128],
                                               stg[si][:, c * 128:(c + 1) * 128], identf[:])
                        return ins
                    P.add("pe", ftr, r=[r_stg[si], r_const], w=r_bank[b])
                    P.add("act", lambda tg=tg, half=half, b=b: A.activation(
                        out=xT[:, half * 4:half * 4 + 4, tg * 128:(tg + 1) * 128],
                        in_=bankap(b).rearrange("p (c t) -> p c t", t=128), func=AF.Copy),
                        r=r_bank[b], w=r_xT[half * 4:half * 4 + 4])

            def layer(ti, l, t0):
                import os
                STG = int(os.environ.get("KSTAGE", "99"))
                if ti == 0 and l + 1 < n_layers:
                    precast(l + 1)
                if STG < 1:
                    return
                if ti > 0:
                    P.dma("pool", lambda l=l: PO.dma_start(out=Kb[:, :, 0:TB], in_=kst_d[l].rearrange("p (a t) -> p a t", t=TB)),
                          r=[r_kst[l]], w=[r_Kprev], chan="kvld", nslots=2)
                    P.dma("pool", lambda l=l: PO.dma_start(out=Vb[:, 0:4, :, :], in_=vst_d[l].rearrange("p (a h e) -> p a h e", h=8, e=VE)),
                          r=[r_vst[l]], w=[r_Vprev], chan="kvld", nslots=2)
                p_si = nstg()
                P.dma("pool", lambda: PO.dma_start(out=stg[p_si][:].rearrange("p (g f) -> p g f", g=4),
                                                   in_=p_d[l, t0:t0 + TB, :].rearrange("(g p) f -> p g f", p=128)),
                      w=[r_stg[p_si]], chan="xin", nslots=2)
                rmsnorm(l, C_GMIX)
                if STG < 2:
                    return
                zslots = {}

                def inproj_fm(gn, evac):
                    s = load_group(l, gn)
                    wv = ring[:, s, :].rearrange("p (j k c) -> p j k c", j=4, k=8)
                    for j in range(4):
                        b = bank()
                        mm_group(b, 0, 512, [(wv[:, j, k, :], hT[:, k, :]) for k in range(8)], r=r_hT + [r_ring[s]])
                        evac(j, b)

                fillers = []

                def add_fm_fillers(gn, evac):
                    st = {}
                    for j in range(4):
                        def fl(j=j):
                            if "s" not in st:
                                st["s"] = load_group(l, gn)
                            s = st["s"]
                            wv = ring[:, s, :].rearrange("p (j k c) -> p j k c", j=4, k=8)
                            b = bank()
                            mm_group(b, 0, 512, [(wv[:, j, k, :], hT[:, k, :]) for k in range(8)], r=r_hT + [r_ring[s]])
                            evac(j, b)
                        fillers.append(fl)

                inproj_fm("f", lambda j, b: P.add("act", lambda: A.activation(out=sigA[j][:], in_=bankap(b), func=AF.Sigmoid),
                                                  r=r_bank[b], w=[r_sigA[j]]))
                inproj_fm("q", lambda j, b: P.add("act", lambda: A.activation(out=qbf[j][:], in_=bankap(b), func=AF.Copy, scale=128.0 ** -0.5),
                                                  r=r_bank[b], w=[r_qbf[j]]))

                def qk_evac(kind):
                    def ev(j, b):
                        t = tmp32()
                        si = nsq()
                        P.add("dve", lambda: V.tensor_copy(out=f32t[t][:], in_=bankap(b)), r=r_bank[b], w=[r_f32t[t]])
                        P.add("act", lambda: A.activation(out=sq[si][:], in_=bankap(b), func=AF.Square), r=r_bank[b], w=[r_sq[si]])
                        b2 = bank()
                        P.add("pe", lambda: PE.matmul(bankap(b2), lhsT=onesblk[:], rhs=sq[si][:], start=True, stop=True),
                              r=[r_sq[si], r_const], w=r_bank[b2])
                        t2 = tmp32()

                        def f1():
                            if kind == "q":
                                return A.activation(out=f32t[t2][:], in_=bankap(b2), func=AF.Ln, scale=1.0, bias=epsc[:, 1:2])
                            return A.activation(out=f32t[t2][:], in_=bankap(b2), func=AF.Ln, scale=1.0 / 64, bias=epsc[:, 0:1])
                        P.chain("act", [f1, lambda: A.activation(out=f32t[t2][:], in_=f32t[t2][:], func=AF.Exp, scale=-0.5)],
                                r=r_bank[b2] + [r_const], w=[r_f32t[t2]])
                        if kind == "q":
                            def g():
                                V.scalar_tensor_tensor(out=qnz[0:64, 2 * j, :], in0=f32t[t][0:64, :], scalar=cst[0:64, C_QN + l:C_QN + l + 1],
                                                       in1=f32t[t2][0:64, :], op0=ALU.mult, op1=ALU.mult)
                                return V.scalar_tensor_tensor(out=qnz[64:128, 2 * j + 1, :], in0=f32t[t][64:128, :],
                                                              scalar=cst[64:128, C_QN + l:C_QN + l + 1],
                                                              in1=f32t[t2][64:128, :], op0=ALU.mult, op1=ALU.mult)
                            P.add("dve", g, r=[r_f32t[t], r_f32t[t2], r_cst], w=[r_qnz[2 * j], r_qnz[2 * j + 1]])
                        else:
                            P.add("dve", lambda: V.scalar_tensor_tensor(out=Kb[:, j, TB:2 * TB], in0=f32t[t][:], scalar=cst[:, C_KN + l:C_KN + l + 1],
                                                                        in1=f32t[t2][:], op0=ALU.mult, op1=ALU.mult),
                                  r=[r_f32t[t], r_f32t[t2], r_cst], w=[r_Kcur[j]])
                    return ev
                add_fm_fillers("aq", qk_evac("q"))
                add_fm_fillers("ak", qk_evac("k"))
                stv = {}
                for tg in range(4):
                    def flv(tg=tg):
                        if "s" not in stv:
                            stv["s"] = load_group(l, "av")
                        s = stv["s"]
                        wv = ring[:, s, :].rearrange("p (k c) -> p k c", k=8)
                        b = bank()
                        mm_group(b, 0, 512, [(hT[:, k, tg * 128:(tg + 1) * 128], wv[:, k, :]) for k in range(8)], r=r_hT + [r_ring[s]])
                        P.add("act", lambda: A.activation(out=Vb[:, 4 + tg, :, 0:64], in_=bankap(b).rearrange("p (h e) -> p h e", e=64), func=AF.Copy),
                              r=r_bank[b], w=[r_Vcur[tg]])
                    fillers.append(flv)

                def gate_evac(idx0):
                    def ev(j, b):
                        t = tmp32()
                        P.add("act", lambda: A.activation(out=f32t[t][:], in_=bankap(b), func=AF.Exp, scale=-1.0), r=r_bank[b], w=[r_f32t[t]])
                        P.chain("dve", [lambda: V.tensor_scalar(out=f32t[t][:], in0=f32t[t][:], scalar1=1.0, scalar2=None, op0=ALU.add),
                                        lambda: recip_lp(big[:, idx0 + j, :], f32t[t][:])],
                                r=[r_f32t[t]], w=[r_f32t[t], r_big[idx0 + j]])
                    return ev
                for gi_, gn in enumerate(("ga0", "ga1", "gb0", "gb1")):
                    add_fm_fillers(gn, gate_evac(gi_ * 4))

                if STG < 3:
                    return
                def hgrn_head(h):
                    hp = h % 2
                    P.add("dve", lambda h=h: V.tensor_scalar(out=sigA[h][:], in0=sigA[h][:], scalar1=omlt[:, l, h:h + 1], scalar2=lbt[:, l, h:h + 1],
                                                             op0=ALU.mult, op1=ALU.add), r=[r_sigA[h], r_const], w=[r_sigA[h]])
                    yield "s0"
                    P.add("act", lambda h=h: A.activation(out=hL[:], in_=sigA[h][:], func=AF.Ln), r=[r_sigA[h]], w=[r_hL])
                    yield "s0"
                    P.add("dve", lambda h=h: V.tensor_tensor_scan(out=hB[:], data0=scanm[:], data1=hL[:], initial=0.0, op0=ALU.mult, op1=ALU.add),
                          r=[r_hL, r_const], w=[r_hB])
                    bv = hB[:].rearrange("p (c t) -> p c t", t=64)
                    P.add("dve", lambda h=h, bv=bv: V.tensor_tensor(out=hF[:].rearrange("p (c t) -> p c t", t=64), in0=bv,
                                                                    in1=bv[:, :, 63:64].to_broadcast([128, 8, 64]), op=ALU.subtract),
                          r=[r_hB], w=[r_hF])
                    P.add("dve", lambda h=h: V.tensor_scalar(out=sigA[h][:], in0=sigA[h][:], scalar1=-1.0, scalar2=1.0, op0=ALU.mult, op1=ALU.add),
                          r=[r_sigA[h]], w=[r_sigA[h]])
                    yield "s0"

                    def fexp(h=h):
                        A.activation(out=hL[:], in_=hB[:], func=AF.Exp)
                        return A.activation(out=hE[:], in_=hB[:], func=AF.Exp, scale=-1.0)
                    P.add("act", fexp, r=[r_hB], w=[r_hL, r_hE])
                    P.add("act", lambda h=h: A.activation(out=hF[:], in_=hF[:], func=AF.Exp, scale=-1.0), r=[r_hF], w=[r_hF])
                    yield "s0"

                    def fqk(h=h, hp=hp):
                        V.tensor_tensor(out=qeT[hp][:], in0=qbf[h][:], in1=hL[:], op=ALU.mult)
                        V.tensor_tensor(out=keT[hp][:], in0=sigA[h][:], in1=hE[:], op=ALU.mult)
                        return V.tensor_tensor(out=kdT[hp][:], in0=sigA[h][:], in1=hF[:], op=ALU.mult)
                    P.add("dve", fqk, r=[r_qbf[h], r_hL, r_hE, r_hF, r_sigA[h]], w=[r_qeT[hp], r_keT[hp], r_kdT[hp]])
                    P.add("dve", lambda h=h, hp=hp: V.tensor_copy(out=acol[hp][:], in_=hL[:, 63::64]), r=[r_hL], w=[r_acol[hp]])
                    if ti == 0:
                        P.add("pool", lambda hp=hp: PO.memset(Sall[hp][:, 0, :], 0.0), w=[r_Sall[hp]])
                    else:
                        P.dma("pool", lambda l=l, h=h, hp=hp: PO.dma_start(out=Sall[hp][:, 0, :], in_=sst_d[l, :, h * 128:(h + 1) * 128]),
                              r=[r_sst[l]], w=[r_Sall[hp]], chan="sld", nslots=2)
                    yield "x"
                    for (src, rsrc, dst, rdst) in ((kdT[hp], r_kdT[hp], kdtok[hp], r_kdtok[hp]), (iTb[h], r_iT[h], itok[hp], r_itok[hp])):
                        bb = bbank()

                        def ftr(src=src, bb=bb):
                            ins = None
                            for c in range(8):
                                ins = PE.transpose(PSB[0:64, bb * 1024 + c * 128:bb * 1024 + (c + 1) * 128], src[:, c * 64:(c + 1) * 64], identb[:])
                            return ins
                        P.add("pe", ftr, r=[rsrc, r_const], w=[r_bbank[bb]])
                        P.add("act", lambda dst=dst, bb=bb: A.activation(out=dst[:].rearrange("p c k -> p (c k)"), in_=PSB[0:64, bb * 1024:(bb + 1) * 1024], func=AF.Copy),
                              r=[r_bbank[bb]], w=[rdst])
                    yield "x"
                    B0, B1 = 2 + 2 * hp, 3 + 2 * hp
                    ba = B0

                    def fat(hp=hp, ba=ba):
                        ins = None
                        for c in range(8):
                            ins = PE.matmul(PSF[0:64, ba * 512 + c * 64:ba * 512 + (c + 1) * 64], lhsT=keT[hp][:, c * 64:(c + 1) * 64],
                                            rhs=qeT[hp][:, c * 64:(c + 1) * 64], start=True, stop=True)
                        return ins
                    P.add("pe", fat, r=[r_keT[hp], r_qeT[hp]], w=r_bank[ba])
                    P.add("dve", lambda hp=hp, ba=ba: V.tensor_tensor(out=attm[hp][:], in0=PSF[0:64, ba * 512:(ba + 1) * 512], in1=cmask[:], op=ALU.mult),
                          r=r_bank[ba] + [r_const], w=[r_attm[hp]])
                    for rnd in range(2):
                        b = B1 if rnd == 0 else B0

                        def fdl(hp=hp, b=b, rnd=rnd):
                            ins = None
                            for cc in range(4):
                                c = rnd * 4 + cc
                                ins = PE.matmul(PSF[:, b * 512 + cc * 128:b * 512 + (cc + 1) * 128], lhsT=kdtok[hp][:, c, :], rhs=itok[hp][:, c, :],
                                                start=True, stop=True)
                            return ins
                        P.add("pe", fdl, r=[r_kdtok[hp], r_itok[hp]], w=r_bank[b])
                        yield "x"
                        for cc in range(4):
                            c = rnd * 4 + cc
                            P.add("dve", lambda hp=hp, c=c, cc=cc, b=b: V.scalar_tensor_tensor(
                                out=Sall[hp][:, c + 1, :], in0=Sall[hp][:, c, :], scalar=acol[hp][:, c:c + 1],
                                in1=PSF[:, b * 512 + cc * 128:b * 512 + (cc + 1) * 128], op0=ALU.mult, op1=ALU.add),
                                r=[r_Sall[hp], r_acol[hp]] + r_bank[b], w=[r_Sall[hp]])
                    P.add("act", lambda hp=hp: A.activation(out=Sbf[hp][:], in_=Sall[hp][:, 0:8, :], func=AF.Copy), r=[r_Sall[hp]], w=[r_Sbf[hp]])
                    if ti + 1 < n_tiles:
                        P.dma("pool", lambda l=l, h=h, hp=hp: PO.dma_start(out=sst_d[l, :, h * 128:(h + 1) * 128], in_=Sall[hp][:, 8, :]),
                              r=[r_Sall[hp]], w=[r_sst[l]], chan="sst", nslots=2)
                    yield "x"
                    bo = B1

                    def fo(hp=hp, bo=bo):
                        ins = None
                        for c in range(8):
                            PE.matmul(PSF[:, bo * 512 + c * 64:bo * 512 + (c + 1) * 64], lhsT=Sbf[hp][:, c, :], rhs=qeT[hp][:, c * 64:(c + 1) * 64],
                                      start=True, stop=False)
                            ins = PE.matmul(PSF[:, bo * 512 + c * 64:bo * 512 + (c + 1) * 64], lhsT=itok[hp][:, c, :], rhs=attm[hp][:, c * 64:(c + 1) * 64],
                                            start=False, stop=True)
                        return ins
                    P.add("pe", fo, r=[r_Sbf[hp], r_qeT[hp], r_itok[hp], r_attm[hp]], w=r_bank[bo])
                    yield "x"
                    si = nsq()
                    P.add("act", lambda si=si, bo=bo: A.activation(out=sq[si][:], in_=bankap(bo), func=AF.Square), r=r_bank[bo], w=[r_sq[si]])
                    bn = B0
                    P.add("pe", lambda si=si, bn=bn: PE.matmul(bankap(bn), lhsT=onesb[:], rhs=sq[si][:], start=True, stop=True),
                          r=[r_sq[si], r_const], w=r_bank[bn])
                    t = 4
                    yield "x"

                    P.chain("act", [lambda t=t, bn=bn: A.activation(out=f32t[t][:], in_=bankap(bn), func=AF.Ln, scale=1.0 / 128, bias=epsc[:, 0:1]),
                                    lambda t=t: A.activation(out=f32t[t][:], in_=f32t[t][:], func=AF.Exp, scale=-0.5)],
                            r=r_bank[bn] + [r_const], w=[r_f32t[t]])
                    yield "x"

                    P.chain("dve", [lambda t=t, bo=bo: V.scalar_tensor_tensor(out=f32t[t][:], in0=bankap(bo), scalar=cst[:, C_ON + l:C_ON + l + 1], in1=f32t[t][:],
                                                                              op0=ALU.mult, op1=ALU.mult),
                                    lambda h=h, t=t: V.tensor_tensor(out=oaT[:, h, :], in0=f32t[t][:], in1=sgb16[h][:], op=ALU.mult)],
                            r=r_bank[bo] + [r_f32t[t], r_cst, r_sg[h]], w=[r_f32t[t], r_oaT[h]])

                def inproj_gi():
                    inproj_fm("g", lambda j, b: P.add("act", lambda: A.activation(out=sgb16[j][:], in_=bankap(b), func=AF.Silu),
                                                      r=r_bank[b], w=[r_sg[j]]))
                    inproj_fm("i", lambda j, b: P.add("dve", lambda: V.tensor_copy(out=iTb[j][:], in_=bankap(b)),
                                                      r=r_bank[b], w=[r_iT[j]]))
                gens = [hgrn_head(h) for h in range(4)]
                state["pool_n"] = 2
                state["bank"] = state["bank"] % 2

                def F(n=1):
                    for _ in range(n):
                        if fillers:
                            fillers.pop(0)()
                started, done, in_s0 = set(), set(), [None]
                first = [True]

                def can_start(h):
                    return h not in started and in_s0[0] is None and (h < 2 or (h - 2) in done)
                while len(done) < 4:
                    progressed = False
                    for h in range(4):
                        if h in done:
                            continue
                        if h not in started:
                            if not can_start(h):
                                continue
                            started.add(h)
                            in_s0[0] = h
                        tag = next(gens[h], None)
                        progressed = True
                        if tag is None:
                            done.add(h)
                            if in_s0[0] == h:
                                in_s0[0] = None
                        elif tag == "x" and in_s0[0] == h:
                            in_s0[0] = None
                        if first[0] and 0 in started and in_s0[0] is None:
                            inproj_gi()
                            first[0] = False
                        F(1)
                    assert progressed
                if first[0]:
                    inproj_gi()
                F(100)
                for g_ in gens:
                    for _ in g_:
                        pass
                state["pool_n"] = 6
                if ti + 1 < n_tiles:
                    P.dma("pool", lambda l=l: PO.dma_start(out=kst_d[l].rearrange("p (a t) -> p a t", t=TB), in_=Kb[:, :, TB:2 * TB]),
                          r=r_Kcur, w=[r_kst[l]], chan="kvst", nslots=2)
                    P.dma("pool", lambda l=l: PO.dma_start(out=vst_d[l].rearrange("p (a h e) -> p a h e", h=8, e=VE), in_=Vb[:, 4:8, :, :]),
                          r=r_Vcur, w=[r_vst[l]], chan="kvst", nslots=2)
                s_ab = [load_group(l, "ab0"), load_group(l, "ab1")]
                if STG < 4:
                    return
                if debug and ti == n_tiles - 1 and l == 0:
                    dump("ab0", ring[:, s_ab[0], 0:2560], [128, 2560], BF16, [r_ring[s_ab[0]]])
                abv = [ring[:, s_ab[0], 0:2560].rearrange("p (h j q) -> p h j q", h=4, j=5),
                       ring[:, s_ab[1], 0:2560].rearrange("p (h j q) -> p h j q", h=4, j=5)]
                rk = [r_Kprev] + r_Kcur

                def att_scores(pr, h, pi):
                    hg, hh = h // 4, h % 4
                    j0 = max(0, 4 - (ti * 4 + pr))
                    bs, b4 = h % 2, 2 + h % 2

                    def fsc():
                        ins = None
                        for j in range(j0, 5):
                            dst = PSF[:, bs * 512 + j * 128:bs * 512 + (j + 1) * 128] if j < 4 else PSF[:, b4 * 512:b4 * 512 + 128]
                            PE.matmul(dst, lhsT=Kb[:, h // 2, (pr + j) * 128:(pr + j + 1) * 128], rhs=qnz[:, h, pr * 128:(pr + 1) * 128],
                                      start=True, stop=False)
                            msk = (j == 0) or (j == 4)
                            ins = PE.matmul(dst, lhsT=identb[:], rhs=abv[hg][:, hh, j, :], start=False, stop=not msk)
                            if msk:
                                ins = PE.matmul(dst, lhsT=identb[:], rhs=(mask0 if j == 0 else mask4)[:], start=False, stop=True)
                        return ins
                    P.add("pe", fsc, r=rk + [r_qnz[h], r_ring[s_ab[hg]], r_const], w=r_bank[bs] + r_bank[b4])

                    def fex():
                        if j0 < 4:
                            A.activation(out=Pt[pi][:, j0 * 128:512], in_=PSF[:, bs * 512 + j0 * 128:(bs + 1) * 512], func=AF.Exp)
                        return A.activation(out=Pt[pi][:, 512:640], in_=PSF[:, b4 * 512:b4 * 512 + 128], func=AF.Exp)
                    P.add("act", fex, r=r_bank[bs] + r_bank[b4], w=[r_Pt[pi]])

                def att_pv(pr, h, pi):
                    hg, hh = h // 4, h % 4
                    j0 = max(0, 4 - (ti * 4 + pr))
                    bov = 4 + hg
                    oi = hg

                    def fpv():
                        ins = None
                        for j in range(j0, 5):
                            ins = PE.matmul(PSF[:, bov * 512 + hh * VE:bov * 512 + (hh + 1) * VE], lhsT=Pt[pi][:, j * 128:(j + 1) * 128],
                                            rhs=Vb[:, pr + j, h, :], start=(j == j0), stop=(j == 4))
                        return ins
                    P.add("pe", fpv, r=[r_Pt[pi], r_Vprev] + r_Vcur, w=r_bank[bov])
                    if hh == 3:
                        def fnm1():
                            pv = PSF[:, bov * 512:bov * 512 + 4 * VE].rearrange("p (h e) -> p h e", e=VE)
                            return V.reciprocal(out=rden[oi][:], in_=pv[:, :, 64:65])

                        def fnm2():
                            pv = PSF[:, bov * 512:bov * 512 + 4 * VE].rearrange("p (h e) -> p h e", e=VE)
                            return V.tensor_tensor(out=obt[pr % 2][:, hg * 256:(hg + 1) * 256].rearrange("p (h e) -> p h e", e=64), in0=pv[:, :, 0:64],
                                                   in1=rden[oi][:].to_broadcast([128, 4, 64]), op=ALU.mult)
                        P.chain("dve", [fnm1, fnm2], r=r_bank[bov], w=[r_rden[oi], r_obt[pr % 2]])
                    if h == 7:
                        bb = bbank()

                        def ftr():
                            ins = None
                            for fc in range(4):
                                ins = PE.transpose(PSB[:, bb * 1024 + fc * 128:bb * 1024 + (fc + 1) * 128], obt[pr % 2][:, fc * 128:(fc + 1) * 128], identb[:])
                            return ins
                        P.add("pe", ftr, r=[r_obt[pr % 2], r_const], w=[r_bbank[bb]])
                        P.add("act", lambda: A.activation(out=obT[:, :, pr * 128:(pr + 1) * 128],
                                                          in_=PSB[:, bb * 1024:bb * 1024 + 512].rearrange("p (c t) -> p c t", t=128), func=AF.Copy),
                              r=[r_bbank[bb]], w=r_obT)

                steps = [(pr, h) for pr in range(4) for h in range(8)]
                LAG = 1
                for i in range(len(steps) + LAG):
                    if i < len(steps):
                        att_scores(steps[i][0], steps[i][1], i % 3)
                    if i >= LAG:
                        att_pv(steps[i - LAG][0], steps[i - LAG][1], (i - LAG) % 3)

                if STG < 5:
                    return
                sa = load_group(l, "wa")
                sb_ = load_group(l, "wb")
                wav = ring[:, sa, :].rearrange("p (j k c) -> p j k c", j=8, k=4)
                wbv = ring[:, sb_, :].rearrange("p (j k c) -> p j k c", j=8, k=4)
                for oc in range(8):
                    ba_, bb_ = bank(), bank()
                    mm_group(ba_, 0, 512, [(wav[:, oc, k, :], oaT[:, k, :]) for k in range(4)], r=r_oaT + [r_ring[sa]])
                    mm_group(bb_, 0, 512, [(wbv[:, oc, k, :], obT[:, k, :]) for k in range(4)], r=r_obT + [r_ring[sb_]])
                    t1, t2 = tmp32(), tmp32()
                    P.add("dve", lambda oc=oc, ba_=ba_, t1=t1: V.tensor_tensor(out=f32t[t1][:], in0=bankap(ba_), in1=big[:, oc, :], op=ALU.mult),
                          r=r_bank[ba_] + [r_big[oc]], w=[r_f32t[t1]])
                    P.add("dve", lambda oc=oc, bb_=bb_, t2=t2: V.tensor_tensor(out=f32t[t2][:], in0=bankap(bb_), in1=big[:, 8 + oc, :], op=ALU.mult),
                          r=r_bank[bb_] + [r_big[8 + oc]], w=[r_f32t[t2]])
                    P.add("dve", lambda oc=oc, t1=t1, t2=t2: V.tensor_tensor(out=hT[:, oc, :], in0=f32t[t1][:], in1=f32t[t2][:], op=ALU.add),
                          r=[r_f32t[t1], r_f32t[t2]], w=[r_hT[oc]])
                fn_ = FusedNorm()
                for half in range(2):
                    s = load_group(l, "wo%d" % half)
                    wv = ring[:, s, :].rearrange("p (j k c) -> p j k c", j=4, k=8)
                    for j in range(4):
                        oc = half * 4 + j
                        b = bank()
                        mm_group(b, 0, 512, [(wv[:, j, k, :], hT[:, k, :]) for k in range(8)], r=r_hT + [r_ring[s]])
                        P.add("dve", lambda oc=oc, b=b: V.tensor_tensor(out=xT[:, oc, :], in0=xT[:, oc, :], in1=bankap(b), op=ALU.add),
                              r=r_bank[b] + [r_xT[oc]], w=[r_xT[oc]])
                        fn_.chunk_done(oc)
                fn_.close()
                if debug and ti == n_tiles - 1 and l == 0:
                    dump("oaT", oaT[:], [128, 4, TB], BF16, r_oaT)
                    dump("qnz", qnz[:], [128, 8, TB], BF16, r_qnz)
                    dump("Kb", Kb[:], [128, 4, 2 * TB], BF16, [r_Kprev] + r_Kcur)
                    dump("Vb", Vb[:], [128, 8, 8, VE], BF16, [r_Vprev] + r_Vcur)
                    dump("obt1", obt[1][:], [128, 512], BF16, [r_obt[1]])
                    dump("Pt1", Pt[1][:], [128, 640], BF16, [r_Pt[1]])
                    dump("obT", obT[:], [128, 4, TB], BF16, r_obT)
                    dump("x1", xT[:], [128, 8, TB], F32, r_xT)

                if STG < 6:
                    return
                rmsnorm(l, C_GFFN)
                for tg in range(4):
                    b = bank()

                    def ftp(tg=tg, b=b):
                        PE.transpose(PSF[:, b * 512:b * 512 + 128], stg[p_si][:, tg * 256:tg * 256 + 128], identf[:])
                        return PE.transpose(PSF[:, b * 512 + 128:b * 512 + 256], stg[p_si][:, tg * 256 + 128:tg * 256 + 256], identf[:])
                    P.add("pe", ftp, r=[r_stg[p_si], r_const], w=r_bank[b])
                    P.add("act", lambda tg=tg, b=b: A.activation(out=pTb[:, :, tg * 128:(tg + 1) * 128],
                                                                 in_=PSF[:, b * 512:b * 512 + 256].rearrange("p (c t) -> p c t", t=128), func=AF.Copy),
                          r=r_bank[b], w=[r_pT])
                for i in range(11):
                    s = load_group(l, "ff%d" % i)
                    wv = ring[:, s, :].rearrange("p (j k c) -> p j k c", j=4, k=8)
                    for jj in range(2):
                        j = 2 * i + jj
                        bg, bu = bank(), bank()
                        mm_group(bg, 0, 512, [(wv[:, 2 * jj, k, :], hT[:, k, :]) for k in range(8)], r=r_hT + [r_ring[s]])
                        mm_group(bu, 0, 512, [(wv[:, 2 * jj + 1, k, :], hT[:, k, :]) for k in range(8)], r=r_hT + [r_ring[s]])
                        t = tmp32()
                        P.add("act", lambda bg=bg, t=t: A.activation(out=f32t[t][:], in_=bankap(bg), func=AF.Silu), r=r_bank[bg], w=[r_f32t[t]])
                        P.add("dve", lambda j=j, bu=bu, t=t: V.tensor_tensor(out=big[:, j, :], in0=bankap(bu), in1=f32t[t][:], op=ALU.mult),
                              r=r_bank[bu] + [r_f32t[t]], w=[r_big[j]])
                fn_ = FusedNorm()
                for oc in range(8):
                    s = load_group(l, "dn%d" % oc)
                    wv = ring[:, s, 0:2816].rearrange("p (k c) -> p k c", k=22)
                    b = bank()
                    mm_group(b, 0, 512, [(wv[:, k, :], big[:, k, :]) for k in range(22)], r=r_big + [r_ring[s]])
                    P.add("dve", lambda oc=oc, b=b: V.tensor_tensor(out=xT[:, oc, :], in0=xT[:, oc, :], in1=bankap(b), op=ALU.add),
                          r=r_bank[b] + [r_xT[oc]], w=[r_xT[oc]])
                    fn_.chunk_done(oc)
                fn_.close()

                if STG < 7:
                    return
                rmsnorm(l, C_GPLE)
                spp = None
                fn_ = FusedNorm(enable=(l + 1 < n_layers))
                for half in range(2):
                    s = load_group(l, "pg%d" % half)
                    if half == 0:
                        spp = load_group(l, "pp")
                    wv = ring[:, s, :].rearrange("p (j k c) -> p j k c", j=4, k=8)
                    ppv = ring[:, spp, 0:2048].rearrange("p (j k c) -> p j k c", j=8, k=2)
                    for j in range(4):
                        oc = half * 4 + j
                        bg, bp = bank(), bank()
                        mm_group(bg, 0, 512, [(wv[:, j, k, :], hT[:, k, :]) for k in range(8)], r=r_hT + [r_ring[s]])
                        mm_group(bp, 0, 512, [(ppv[:, oc, k, :], pTb[:, k, :]) for k in range(2)], r=[r_pT, r_ring[spp]])
                        t = tmp32()
                        P.add("act", lambda bg=bg, t=t: A.activation(out=f32t[t][:], in_=bankap(bg), func=AF.Sigmoid), r=r_bank[bg], w=[r_f32t[t]])
                        P.add("dve", lambda bp=bp, t=t: V.tensor_tensor(out=f32t[t][:], in0=bankap(bp), in1=f32t[t][:], op=ALU.mult),
                              r=r_bank[bp] + [r_f32t[t]], w=[r_f32t[t]])
                        P.add("dve", lambda oc=oc, t=t: V.tensor_tensor(out=xT[:, oc, :], in0=xT[:, oc, :], in1=f32t[t][:], op=ALU.add),
                              r=[r_f32t[t], r_xT[oc]], w=[r_xT[oc]])
                        fn_.chunk_done(oc)
                fn_.close()

            for l in range(n_layers):
                layer(ti, l, t0)
            for tg in range(4):
                si = nstg()
                for half in range(2):
                    b = bank()

                    def ftr(tg=tg, half=half, b=b):
                        ins = None
                        for cc in range(4):
                            c = half * 4 + cc
                            ins = PE.transpose(PSF[:, b * 512 + cc * 128:b * 512 + (cc + 1) * 128], xT[:, c, tg * 128:(tg + 1) * 128], identf[:])
                        return ins
                    P.add("pe", ftr, r=r_xT[half * 4:half * 4 + 4] + [r_const], w=r_bank[b])
                    P.add("act", lambda si=si, half=half, b=b: A.activation(out=stg[si][:, half * 512:(half + 1) * 512], in_=bankap(b), func=AF.Copy),
                          r=r_bank[b], w=[r_stg[si]])
                op = P.dma("pool", lambda tg=tg, si=si, t0=t0: PO.dma_start(out=o_d[t0 + tg * 128:t0 + (tg + 1) * 128, :], in_=stg[si][:]),
                           r=[r_stg[si]], w=[r_out], chan="out", nslots=2)
                op.is_out = True

        nwait = P.emit()
    return nc, dbg_d


_CACHE = {}


def _run(inp, n_tiles, n_layers, ncores, debug=False, trace=False):
    wl, cst = _host_layout(inp)
    key = (n_tiles, n_layers, debug)
    if key not in _CACHE:
        _CACHE[key] = build(n_tiles, n_layers, debug)
    nc, dbg = _CACHE[key]
    S = n_tiles * TB
    in_maps = []
    for b in range(ncores):
        in_maps.append({"x": np.ascontiguousarray(inp["x"][b, :S]),
                        "p": np.ascontiguousarray(inp["p"][:, b, :S]),
                        "w": wl, "cst": cst})
    res = run_bass_kernel_spmd(nc, in_maps, core_ids=list(range(ncores)), trace=trace)
    return res


def kernel(**inputs):
    inp = {k: np.asarray(v) for k, v in inputs.items()}
    res = _run(inp, SEQ // TB, DEPTH, 8)
    out = np.stack([np.asarray(r["out"]) for r in res.results], axis=0)
    return out.astype(np.float32)
```

```python
import contextlib
import numpy as np
import concourse.bass as bass
import concourse.mybir as mybir
from concourse.bass_utils import run_bass_kernel_spmd

F32 = mybir.dt.float32
BF16 = mybir.dt.bfloat16
AF = mybir.ActivationFunctionType
ALU = mybir.AluOpType

D = 1024
SEQ = 4096
DEPTH = 4
TB = 512
NCH = 8
DFF = 2816
NFF = 22
PLE = 256
EPS = 1e-6
G = 4096
NEG = -30000.0
VE = 66

GROUPS = [("q", G), ("f", G), ("ga0", G), ("ga1", G), ("gb0", G), ("gb1", G), ("g", G), ("i", G),
          ("aq", G), ("ak", G), ("av", G), ("ab0", 2560), ("ab1", 2560), ("wa", G), ("wb", G),
          ("wo0", G), ("wo1", G)]
GROUPS += [("ff%d" % i, G) for i in range(11)]
GROUPS += [("dn%d" % i, 2816) for i in range(8)]
GROUPS += [("pg0", G), ("pg1", G), ("pp", 2048)]
GOFF = {}
_o = 0
for _n, _c in GROUPS:
    GOFF[_n] = (_o, _c)
    _o += _c
WCOLS = _o

C_GMIX, C_GFFN, C_GPLE = 0, 32, 64
C_LB = 96
C_ON = 112
C_QN = 116
C_KN = 120
NCST = 124


def _lhsT_slab(W, col0, ncc, nk):
    sub = W[:, col0:col0 + ncc * 128].reshape(nk, 128, ncc, 128)
    return np.ascontiguousarray(sub.transpose(1, 2, 0, 3)).reshape(128, ncc * nk * 128)


def _host_layout(inp):
    w_in = inp["w_in"]
    wl = np.empty((DEPTH, 128, WCOLS), np.float32)
    for l in range(DEPTH):
        def put(name, arr):
            o, c = GOFF[name]
            assert arr.shape == (128, c), (name, arr.shape, c)
            wl[l, :, o:o + c] = arr
        W = w_in[l]
        put("q", _lhsT_slab(W, 0, 4, 8))
        put("f", _lhsT_slab(W, 512, 4, 8))
        put("i", _lhsT_slab(W, 1024, 4, 8))
        put("g", _lhsT_slab(W, 1536, 4, 8))
        put("aq", _lhsT_slab(W, 2048, 4, 8))
        put("ak", _lhsT_slab(W, 2560, 4, 8))
        put("av", np.ascontiguousarray(W[:, 3072:3584].reshape(8, 128, 512).transpose(1, 0, 2)).reshape(128, G))
        put("ga0", _lhsT_slab(W, 3584, 4, 8))
        put("ga1", _lhsT_slab(W, 4096, 4, 8))
        put("gb0", _lhsT_slab(W, 4608, 4, 8))
        put("gb1", _lhsT_slab(W, 5120, 4, 8))
        rb = inp["attn_rel_bias"][l]
        k = np.arange(128)[:, None, None]
        j = np.arange(5)[None, :, None]
        q = np.arange(128)[None, None, :]
        idx = np.clip(512 + q - (128 * j + k), -128, 128) + 128
        ab = rb[:, idx]
        ab = np.ascontiguousarray(ab.transpose(1, 0, 2, 3)).reshape(128, 8 * 640)
        put("ab0", ab[:, :2560])
        put("ab1", ab[:, 2560:])
        put("wa", _lhsT_slab(inp["w_branch_a"][l], 0, 8, 4))
        put("wb", _lhsT_slab(inp["w_branch_b"][l], 0, 8, 4))
        put("wo0", _lhsT_slab(inp["w_out"][l], 0, 4, 8))
        put("wo1", _lhsT_slab(inp["w_out"][l], 512, 4, 8))
        wg, wu = inp["w_ffn_gate"][l], inp["w_ffn_up"][l]
        for i in range(11):
            parts = []
            for jj in (2 * i, 2 * i + 1):
                parts.append(_lhsT_slab(wg, jj * 128, 1, 8))
                parts.append(_lhsT_slab(wu, jj * 128, 1, 8))
            put("ff%d" % i, np.concatenate(parts, axis=1))
        wd = inp["w_ffn_down"][l]
        for oc in range(8):
            put("dn%d" % oc, _lhsT_slab(wd, oc * 128, 1, 22))
        put("pg0", _lhsT_slab(inp["w_ple_gate"][l], 0, 4, 8))
        put("pg1", _lhsT_slab(inp["w_ple_gate"][l], 512, 4, 8))
        put("pp", _lhsT_slab(inp["w_ple_proj"][l], 0, 8, 2))
    cst = np.zeros((128, NCST), np.float32)
    for l in range(DEPTH):
        cst[:, C_GMIX + l * 8:C_GMIX + l * 8 + 8] = inp["norm_mix_g"][l].reshape(8, 128).T
        cst[:, C_GFFN + l * 8:C_GFFN + l * 8 + 8] = inp["norm_ffn_g"][l].reshape(8, 128).T
        cst[:, C_GPLE + l * 8:C_GPLE + l * 8 + 8] = inp["norm_ple_g"][l].reshape(8, 128).T
        cst[:, C_LB + l * 4:C_LB + l * 4 + 4] = inp["hgrn_lb_logits"][l].reshape(4, 128).T
        cst[:, C_ON + l] = inp["hgrn_onorm_g"][l]
        cst[:, C_QN + l] = np.tile(inp["attn_qnorm_g"][l], 2)
        cst[:, C_KN + l] = np.tile(inp["attn_knorm_g"][l], 2)
    return wl, cst


class Reg:
    __slots__ = ("name", "ws", "rs", "excl")

    def __init__(self, name, excl=False):
        self.name = name
        self.ws = {}
        self.rs = []
        self.excl = excl


class Op:
    __slots__ = ("eng", "fn", "deps", "sig", "signal", "dma", "chan", "is_out")

    def __init__(self, eng, fn, dma=False, chan=None):
        self.eng = eng
        self.fn = fn
        self.deps = []
        self.sig = False
        self.signal = None
        self.dma = dma
        self.chan = chan
        self.is_out = False


class Prog:
    def __init__(self, nc, es):
        self.nc = nc
        self.es = es
        self.ops = []
        self.eng_obj = {"pe": nc.tensor, "act": nc.scalar, "dve": nc.vector, "pool": nc.gpsimd, "sp": nc.sync}
        self.sems = {}
        self.chan_sems = {}

    def sem(self, name):
        if name not in self.sems:
            self.sems[name] = self.es.enter_context(self.nc.semaphore(name))
        return self.sems[name]

    def _dep(self, op, d, raw):
        if d is op:
            return
        if d.eng == op.eng and not d.dma:
            if op.dma:
                pass
            elif op.eng == "pe":
                return
            elif not raw:
                return
        d.sig = True
        op.deps.append(d)

    def add(self, eng, fn, r=(), w=(), dma=False, chan=None):
        op = Op(eng, fn, dma, chan)
        if dma:
            op.sig = True
        w = list(w) + [R for R in r if R.excl]
        r = [R for R in r if not R.excl]
        w = list(dict.fromkeys(w))
        for R in r:
            for d in R.ws.values():
                self._dep(op, d, True)
        for R in w:
            for d in R.ws.values():
                self._dep(op, d, False)
            for d in R.rs:
                self._dep(op, d, False)
        for R in r:
            R.rs.append(op)
        for R in w:
            R.ws = {eng if not dma else ("dma", id(op)): op}
            R.rs = []
        self.ops.append(op)
        return op

    def chain(self, eng, fns, r=(), w=()):
        link = Reg("chain")
        op = None
        for i, fn in enumerate(fns):
            rr = list(r) + ([link] if i > 0 else [])
            op = self.add(eng, fn, r=rr, w=list(w) + [link])
        return op

    def dma(self, queue, fn, r=(), w=(), chan="misc", nslots=4):
        if chan not in self.chan_sems:
            self.chan_sems[chan] = {"n": nslots, "i": 0, "last": [None] * nslots,
                                    "sems": [self.sem("d_%s_%d" % (chan, i)) for i in range(nslots)],
                                    "cnt": [0] * nslots}
        return self.add(queue, fn, r, w, dma=True, chan=chan)

    def emit(self):
        waited = {e: {} for e in self.eng_obj}
        cnt = {e: 0 for e in self.eng_obj}
        esem = {e: self.sem("e_" + e) for e in ("pe", "act", "dve", "pool")}
        nwait = 0

        def wait(eng, sem, val):
            nonlocal nwait
            key = id(sem)
            if waited[eng].get(key, 0) < val:
                self.eng_obj[eng].wait_ge(sem, val)
                waited[eng][key] = val
                nwait += 1

        outs = []
        for op in self.ops:
            for d in op.deps:
                s, v = d.signal
                wait(op.eng, s, v)
            if op.dma:
                ch = self.chan_sems[op.chan]
                i = ch["i"]
                ch["i"] = (i + 1) % ch["n"]
                s = ch["sems"][i]
                if ch["cnt"][i] > 0:
                    wait(op.eng, s, ch["cnt"][i])
                inst = op.fn()
                ch["cnt"][i] += 16
                inst.then_inc(s, 16)
                op.signal = (s, ch["cnt"][i])
                if op.is_out:
                    outs.append(op)
            else:
                inst = op.fn()
                if op.sig:
                    cnt[op.eng] += 1
                    inst.then_inc(esem[op.eng], 1)
                    op.signal = (esem[op.eng], cnt[op.eng])
        for op in outs:
            s, v = op.signal
            wait("pool", s, v)
        return nwait


def build(n_tiles, n_layers, debug=False):
    nc = bass.Bass("TRN2", target_bir_lowering=False)
    S = n_tiles * TB
    x_d = nc.dram_tensor("x", [S, D], F32, kind="ExternalInput").ap()
    p_d = nc.dram_tensor("p", [DEPTH, S, PLE], F32, kind="ExternalInput").ap()
    w_d = nc.dram_tensor("w", [DEPTH, 128, WCOLS], F32, kind="ExternalInput").ap()
    c_d = nc.dram_tensor("cst", [128, NCST], F32, kind="ExternalInput").ap()
    o_d = nc.dram_tensor("out", [S, D], F32, kind="ExternalOutput").ap()
    wb_d = nc.dram_tensor("wbf", [DEPTH, 128, WCOLS], BF16, kind="Internal").ap()
    kst_d = nc.dram_tensor("kst", [DEPTH, 128, 4 * TB], BF16, kind="Internal").ap()
    vst_d = nc.dram_tensor("vst", [DEPTH, 128, 4 * 8 * VE], BF16, kind="Internal").ap()
    sst_d = nc.dram_tensor("sst", [DEPTH, 128, 4 * 128], F32, kind="Internal").ap()
    dbg_d = {}

    es = contextlib.ExitStack()
    with es:
        P = Prog(nc, es)

        def sb(name, shape, dt):
            return es.enter_context(nc.sbuf_tensor("s_" + name, shape, dt))

        xT = sb("xT", [128, NCH, TB], F32)
        hT = sb("hT", [128, NCH, TB], BF16)
        NRING = 5
        ring = sb("ring", [128, NRING, G], BF16)
        Kb = sb("Kb", [128, 4, 2 * TB], BF16)
        Vb = sb("Vb", [128, 8, 8, VE], BF16)
        big = sb("big", [128, NFF, TB], BF16)
        cst = sb("cst", [128, NCST], F32)
        identb = sb("identb", [128, 128], BF16)
        identf = sb("identf", [128, 128], F32)
        onesb = sb("onesb", [128, 128], BF16)
        onesblk = sb("onesblk", [128, 128], BF16)
        mask0 = sb("mask0", [128, 128], BF16)
        mask4 = sb("mask4", [128, 128], BF16)
        cmask = sb("cmask", [64, TB], BF16)
        scanm = sb("scanm", [128, TB], BF16)
        lbt = sb("lbt", [128, 4, 4], F32)
        omlt = sb("omlt", [128, 4, 4], F32)
        lbtmp = sb("lbtmp", [128, 4, 4], F32)
        lbs = sb("lbs", [128, 4], F32)
        stg = [sb("stg%d" % i, [128, D], F32) for i in range(2)]
        pTb = sb("pTb", [128, 2, TB], BF16)
        sq = [sb("sq%d" % i, [128, TB], BF16) for i in range(2)]
        NT32 = 5
        f32t = [sb("f32t%d" % i, [128, TB], F32) for i in range(NT32)]
        sigA = [sb("sigA%d" % h, [128, TB], F32) for h in range(4)]
        qbf = [sb("qbf%d" % h, [128, TB], BF16) for h in range(4)]
        iTb = [sb("iT%d" % h, [128, TB], BF16) for h in range(4)]
        sgb16 = [sb("sg%d" % h, [128, TB], BF16) for h in range(4)]
        hL = sb("hL", [128, TB], F32)
        hB = sb("hB", [128, TB], F32)
        hE = sb("hE", [128, TB], F32)
        hF = sb("hF", [128, TB], F32)
        qeT = [sb("qeT%d" % i, [128, TB], BF16) for i in range(2)]
        keT = [sb("keT%d" % i, [128, TB], BF16) for i in range(2)]
        kdT = [sb("kdT%d" % i, [128, TB], BF16) for i in range(2)]
        Sall = [sb("Sall%d" % i, [128, 9, 128], F32) for i in range(2)]
        Sbf = [sb("Sbf%d" % i, [128, 8, 128], BF16) for i in range(2)]
        kdtok = [sb("kdtok%d" % i, [64, 8, 128], BF16) for i in range(2)]
        itok = [sb("itok%d" % i, [64, 8, 128], BF16) for i in range(2)]
        attm = [sb("attm%d" % i, [64, TB], BF16) for i in range(2)]
        oaT = sb("oaT", [128, 4, TB], BF16)
        acol = [sb("acol%d" % i, [128, 8], F32) for i in range(2)]
        qnz = sb("qnz", [128, 8, TB], BF16)
        Pt = [sb("Pt%d" % i, [128, 640], BF16) for i in range(3)]
        obt = [sb("obt%d" % i, [128, 512], BF16) for i in range(2)]
        rden = [sb("rden%d" % i, [128, 4, 1], F32) for i in range(2)]
        obT = sb("obT", [128, 4, TB], BF16)
        if debug:
            dbgbuf = sb("dbgbuf", [128, 512], F32)
            r_dbgbuf = Reg("dbgbuf")
        PSF = es.enter_context(nc.psum_tensor("PSF", [128, 6 * 512], F32))
        PSB = es.enter_context(nc.psum_tensor("PSB", [128, 2 * 1024], BF16))

        def regs(prefix, n):
            return [Reg("%s%d" % (prefix, i)) for i in range(n)]
        r_xT = regs("xT", 8)
        r_hT = regs("hT", 8)
        r_ring = regs("ring", NRING)
        r_Kprev = Reg("Kprev"); r_Kcur = regs("Kcur", 4)
        r_Vprev = Reg("Vprev"); r_Vcur = regs("Vcur", 4)
        r_big = regs("big", NFF)
        r_cst = Reg("cst"); r_const = Reg("const")
        r_stg = regs("stg", 2)
        r_pT = Reg("pT")
        r_sq = regs("sq", 2)
        r_f32t = regs("f32t", 5)
        r_sigA = regs("sigA", 4); r_qbf = regs("qbf", 4); r_iT = regs("iT", 4); r_sg = regs("sg", 4)
        r_hL, r_hB, r_hE, r_hF = Reg("hL"), Reg("hB"), Reg("hE"), Reg("hF")
        r_qeT = regs("qeT", 2); r_keT = regs("keT", 2); r_kdT = regs("kdT", 2)
        r_Sall = regs("Sall", 2); r_Sbf = regs("Sbf", 2)
        r_kdtok = regs("kdtok", 2); r_itok = regs("itok", 2); r_attm = regs("attm", 2)
        r_oaT = regs("oaT", 4)
        r_acol = regs("acol", 2)
        r_lbtmp = Reg("lbtmp")
        r_qnz = regs("qnz", 8)
        r_Pt = regs("Pt", 3); r_obt = regs("obt", 2); r_rden = regs("rden", 2)
        r_obT = regs("obT", 4)
        r_bank = [[Reg("bk%d" % b, excl=True)] for b in range(6)]
        r_bbank = [Reg("bb%d" % i, excl=True) for i in range(2)]
        r_wsrc = Reg("wsrc")
        r_wbf = [[Reg("wbf%d_%s" % (l, n)) for n, _ in GROUPS] for l in range(DEPTH)]
        r_kst = regs("kst", DEPTH); r_vst = regs("vst", DEPTH); r_sst = regs("sst", DEPTH)
        r_out = Reg("out")
        GIDX = {n: i for i, (n, _) in enumerate(GROUPS)}

        state = {"bank": 0, "bb": 0, "f32t": 0, "sq": 0, "ring": 0, "stg": 0}

        state["pool_lo"] = 0
        state["pool_n"] = 6
        state["hbank"] = 0

        def bank():
            b = state["bank"]
            state["bank"] = (b + 1) % state["pool_n"]
            return state["pool_lo"] + b

        def hbank():
            b = state["hbank"]
            state["hbank"] = (b + 1) % 3
            return 3 + b

        def bankap(b):
            return PSF[:, b * 512:(b + 1) * 512]

        def bbank():
            b = state["bb"]
            state["bb"] = (b + 1) % 2
            return b

        def bbankap(b):
            return PSB[:, b * 1024:(b + 1) * 1024]

        def tmp32():
            i = state["f32t"]
            state["f32t"] = (i + 1) % 4
            return i

        def nsq():
            i = state["sq"]
            state["sq"] = (i + 1) % 2
            return i

        def nstg():
            i = state["stg"]
            state["stg"] = (i + 1) % 2
            return i

        V, A, PE, PO = nc.vector, nc.scalar, nc.tensor, nc.gpsimd

        P.dma("sp", lambda: nc.sync.dma_start(out=cst[:], in_=c_d), w=[r_cst], chan="misc")

        consts_fns = [
            lambda: PO.memset(identf[:], 1.0),
            lambda: PO.affine_select(out=identf[:], in_=identf[:], pattern=[[-1, 128]], compare_op=ALU.is_equal,
                                     fill=0.0, base=0, channel_multiplier=1),
            lambda: PO.tensor_copy(out=identb[:], in_=identf[:]),
            lambda: PO.memset(onesb[:], 1.0),
            lambda: PO.memset(onesblk[:], 0.0),
            lambda: PO.memset(onesblk[0:64, 0:64], 1.0),
            lambda: PO.memset(onesblk[64:128, 64:128], 1.0),
            lambda: PO.memset(mask0[:], 0.0),
            lambda: PO.memset(mask0[0:64, 64:128], NEG),
            lambda: PO.memset(mask4[:], 0.0),
            lambda: PO.memset(mask4[64:128, 0:64], NEG),
            lambda: PO.memset(cmask[:], 1.0),
            lambda: PO.affine_select(out=cmask[:], in_=cmask[:], pattern=[[0, 8], [1, 64]], compare_op=ALU.is_ge,
                                     fill=0.0, base=0, channel_multiplier=-1),
            lambda: PO.memset(scanm[:], 1.0),
            lambda: PO.memset(scanm[:].rearrange("p (c l) -> p c l", l=64)[:, :, 0:1], 0.0),
            lambda: PO.memset(qnz[:], 0.0),
            lambda: PO.memset(Vb[:], 1.0),
            lambda: PO.memset(Kb[:], 0.0),
        ]
        P.chain("pool", consts_fns, w=[r_const, r_Kprev, r_Vprev] + r_qnz + r_Kcur + r_Vcur)

        def mk_lb():
            lg = cst[:, C_LB:C_LB + 16].rearrange("p (l h) -> p l h", h=4)
            return A.activation(out=lbtmp[:], in_=lg, func=AF.Exp)
        P.add("act", mk_lb, r=[r_cst], w=[r_lbtmp])

        lb_fns = [
            lambda: V.tensor_tensor(out=lbs[:], in0=lbtmp[:, 0, :], in1=lbtmp[:, 1, :], op=ALU.add),
            lambda: V.tensor_tensor(out=lbs[:], in0=lbs[:], in1=lbtmp[:, 2, :], op=ALU.add),
            lambda: V.tensor_tensor(out=lbs[:], in0=lbs[:], in1=lbtmp[:, 3, :], op=ALU.add),
            lambda: V.reciprocal(out=lbs[:], in_=lbs[:]),
            lambda: V.memset(lbt[:, 0, :], 0.0),
            lambda: V.tensor_tensor(out=lbt[:, 1, :], in0=lbtmp[:, 1, :], in1=lbs[:], op=ALU.mult),
            lambda: V.tensor_tensor(out=lbt[:, 2, :], in0=lbtmp[:, 2, :], in1=lbs[:], op=ALU.mult),
            lambda: V.tensor_tensor(out=lbt[:, 2, :], in0=lbt[:, 2, :], in1=lbt[:, 1, :], op=ALU.add),
            lambda: V.tensor_tensor(out=lbt[:, 3, :], in0=lbtmp[:, 3, :], in1=lbs[:], op=ALU.mult),
            lambda: V.tensor_tensor(out=lbt[:, 3, :], in0=lbt[:, 3, :], in1=lbt[:, 2, :], op=ALU.add),
            lambda: V.tensor_scalar(out=omlt[:], in0=lbt[:], scalar1=-1.0, scalar2=1.0, op0=ALU.mult, op1=ALU.add),
        ]
        P.chain("dve", lb_fns, r=[r_lbtmp], w=[r_const])

        def precast(l):
            for gi, (gn, gc) in enumerate(GROUPS):
                o = GOFF[gn][0]
                P.dma("pool", lambda l=l, o=o, gc=gc: PO.dma_start(out=wb_d[l, :, o:o + gc], in_=w_d[l, :, o:o + gc],
                                                                  max_dma_last_dim=8192),
                      w=[r_wbf[l][gi]], chan="pc", nslots=8)
        precast(0)

        def load_group(l, gn):
            gi = GIDX[gn]
            o, gc = GOFF[gn]
            s = state["ring"]
            state["ring"] = (s + 1) % NRING
            P.dma("sp", lambda: nc.sync.dma_start(out=ring[:, s, 0:gc], in_=wb_d[l, :, o:o + gc]),
                  r=[r_wbf[l][gi]], w=[r_ring[s]], chan="ring%d" % s, nslots=1)
            return s

        def mm_group_split(b, pairs, rk, r_other):
            n = len(pairs)
            for i, (lt, rh) in enumerate(pairs):
                P.add("pe", lambda i=i, lt=lt, rh=rh: PE.matmul(PSF[:, b * 512:(b + 1) * 512], lhsT=lt, rhs=rh, start=(i == 0), stop=(i == n - 1)),
                      r=[rk[i]] + list(r_other), w=r_bank[b])

        def mm_group(b, col0, ncol, pairs, r, part=128, qs=None):
            def fn():
                n = len(pairs)
                ins = None
                for i, (lt, rh) in enumerate(pairs):
                    ins = PE.matmul(PSF[0:part, b * 512 + col0:b * 512 + col0 + ncol], lhsT=lt, rhs=rh,
                                    start=(i == 0), stop=(i == n - 1))
                return ins
            return P.add("pe", fn, r=r, w=r_bank[b])

        NORM_BANK = 5

        def norm_acc(c, bn):
            si = nsq()
            P.add("act", lambda: A.activation(out=sq[si][:], in_=xT[:, c, :], func=AF.Square),
                  r=[r_xT[c]], w=[r_sq[si]])

            def mm():
                P.add("pe", lambda: PE.matmul(bankap(bn), lhsT=onesb[:], rhs=sq[si][:],
                                              start=(c == 0), stop=(c == NCH - 1)),
                      r=[r_sq[si], r_const], w=r_bank[bn])
            return mm

        def norm_finish(l, cbase, bn):
            t = tmp32()
            P.chain("act", [lambda: A.activation(out=f32t[t][:], in_=bankap(bn), func=AF.Ln, scale=1.0 / D, bias=epsc[:, 0:1]),
                            lambda: A.activation(out=f32t[t][:], in_=f32t[t][:], func=AF.Exp, scale=-0.5)],
                    r=r_bank[bn] + [r_const], w=[r_f32t[t]])
            for c in range(NCH):
                P.add("dve", lambda c=c: V.scalar_tensor_tensor(out=hT[:, c, :], in0=xT[:, c, :],
                                                                scalar=cst[:, cbase + l * 8 + c:cbase + l * 8 + c + 1],
                                                                in1=f32t[t][:], op0=ALU.mult, op1=ALU.mult),
                      r=[r_xT[c], r_f32t[t], r_cst], w=[r_hT[c]])

        def rmsnorm(l, cbase):
            if state.get("prenorm"):
                state["prenorm"] = False
                norm_finish(l, cbase, NORM_BANK)
                return
            bn = bank()
            for c in range(NCH):
                norm_acc(c, bn)()
            norm_finish(l, cbase, bn)

        class FusedNorm:
            def __init__(self, enable=True):
                self.enable = enable
                self.pending = None
                if enable:
                    state["pool_n"] = 5
                    state["bank"] = state["bank"] % 5

            def chunk_done(self, c):
                if not self.enable:
                    return
                if self.pending is not None:
                    self.pending()
                self.pending = norm_acc(c, NORM_BANK)

            def close(self):
                if not self.enable:
                    return
                self.pending()
                state["pool_n"] = 6
                state["prenorm"] = True

        epsc = sb("epsc", [128, 4], F32)

        def mk_eps():
            V.memset(epsc[:, 0:1], EPS)
            V.memset(epsc[:, 1:2], 64.0 * EPS)
            return V.memset(epsc[:, 2:3], 0.0)
        P.add("dve", mk_eps, w=[r_const])

        def recip_lp(out, in_):
            with nc.allow_low_precision(reason="fp32 reciprocal, bf16 store of a matmul operand"):
                return V.reciprocal(out=out, in_=in_)

        def dump(name, ap, shape, dt, r):
            if not debug:
                return
            d = nc.dram_tensor("dbg_" + name, shape, dt, kind="ExternalOutput").ap()
            dbg_d[name] = d
            op = P.dma("pool", lambda: PO.dma_start(out=d, in_=ap), r=r, w=[], chan="dbg", nslots=2)
            op.is_out = True

        for ti in range(n_tiles):
            t0 = ti * TB
            for tg in range(4):
                si = nstg()
                P.dma("pool", lambda tg=tg, si=si, t0=t0: PO.dma_start(out=stg[si][:], in_=x_d[t0 + tg * 128:t0 + (tg + 1) * 128, :]),
                      w=[r_stg[si]], chan="xin", nslots=2)
                for half in range(2):
                    b = bank()

                    def ftr(tg=tg, si=si, half=half, b=b):
                        ins = None
                        for cc in range(4):
                            c = half * 4 + cc
                            ins = PE.transpose(PSF[:, b * 512 + cc * 128:b * 512 + (cc + 1) * 128],
                                               stg[si][:, c * 128:(c + 1) * 128], identf[:])
                        return ins
                    P.add("pe", ftr, r=[r_stg[si], r_const], w=r_bank[b])
                    P.add("act", lambda tg=tg, half=half, b=b: A.activation(
                        out=xT[:, half * 4:half * 4 + 4, tg * 128:(tg + 1) * 128],
                        in_=bankap(b).rearrange("p (c t) -> p c t", t=128), func=AF.Copy),
                        r=r_bank[b], w=r_xT[half * 4:half * 4 + 4])

            def layer(ti, l, t0):
                import os
                STG = int(os.environ.get("KSTAGE", "99"))
                if ti == 0 and l + 1 < n_layers:
                    precast(l + 1)
                if STG < 1:
                    return
                if ti > 0:
                    P.dma("pool", lambda l=l: PO.dma_start(out=Kb[:, :, 0:TB], in_=kst_d[l].rearrange("p (a t) -> p a t", t=TB)),
                          r=[r_kst[l]], w=[r_Kprev], chan="kvld", nslots=2)
                    P.dma("pool", lambda l=l: PO.dma_start(out=Vb[:, 0:4, :, :], in_=vst_d[l].rearrange("p (a h e) -> p a h e", h=8, e=VE)),
                          r=[r_vst[l]], w=[r_Vprev], chan="kvld", nslots=2)
                p_si = nstg()
                P.dma("pool", lambda: PO.dma_start(out=stg[p_si][:].rearrange("p (g f) -> p g f", g=4),
                                                   in_=p_d[l, t0:t0 + TB, :].rearrange("(g p) f -> p g f", p=128)),
                      w=[r_stg[p_si]], chan="xin", nslots=2)
                rmsnorm(l, C_GMIX)
                if STG < 2:
                    return
                zslots = {}

                def inproj_fm(gn, evac, first=False):
                    s = load_group(l, gn)
                    wv = ring[:, s, :].rearrange("p (j k c) -> p j k c", j=4, k=8)
                    for j in range(4):
                        b = bank()
                        if first and j == 0:
                            mm_group_split(b, [(wv[:, j, k, :], hT[:, k, :]) for k in range(8)], r_hT, [r_ring[s]])
                        else:
                            mm_group(b, 0, 512, [(wv[:, j, k, :], hT[:, k, :]) for k in range(8)], r=r_hT + [r_ring[s]])
                        evac(j, b)

                fillers = []

                def add_fm_fillers(gn, evac):
                    st = {}
                    for j in range(4):
                        def fl(j=j):
                            if "s" not in st:
                                st["s"] = load_group(l, gn)
                            s = st["s"]
                            wv = ring[:, s, :].rearrange("p (j k c) -> p j k c", j=4, k=8)
                            b = bank()
                            mm_group(b, 0, 512, [(wv[:, j, k, :], hT[:, k, :]) for k in range(8)], r=r_hT + [r_ring[s]])
                            evac(j, b)
                        fillers.append(fl)

                inproj_fm("f", lambda j, b: P.add("act", lambda: A.activation(out=sigA[j][:], in_=bankap(b), func=AF.Sigmoid),
                                                  r=r_bank[b], w=[r_sigA[j]]), first=True)
                inproj_fm("q", lambda j, b: P.add("act", lambda: A.activation(out=qbf[j][:], in_=bankap(b), func=AF.Copy, scale=128.0 ** -0.5),
                                                  r=r_bank[b], w=[r_qbf[j]]))

                def qk_evac(kind):
                    def ev(j, b):
                        t = tmp32()
                        si = nsq()
                        P.add("act", lambda: A.activation(out=sq[si][:], in_=bankap(b), func=AF.Square), r=r_bank[b], w=[r_sq[si]])
                        P.add("act", lambda: A.activation(out=f32t[t][:], in_=bankap(b), func=AF.Copy), r=r_bank[b], w=[r_f32t[t]])
                        b2 = bank()
                        P.add("pe", lambda: PE.matmul(bankap(b2), lhsT=onesblk[:], rhs=sq[si][:], start=True, stop=True),
                              r=[r_sq[si], r_const], w=r_bank[b2])
                        t2 = tmp32()

                        def f1():
                            if kind == "q":
                                return A.activation(out=f32t[t2][:], in_=bankap(b2), func=AF.Ln, scale=1.0, bias=epsc[:, 1:2])
                            return A.activation(out=f32t[t2][:], in_=bankap(b2), func=AF.Ln, scale=1.0 / 64, bias=epsc[:, 0:1])
                        P.chain("act", [f1, lambda: A.activation(out=f32t[t2][:], in_=f32t[t2][:], func=AF.Exp, scale=-0.5)],
                                r=r_bank[b2] + [r_const], w=[r_f32t[t2]])
                        if kind == "q":
                            def g():
                                V.scalar_tensor_tensor(out=qnz[0:64, 2 * j, :], in0=f32t[t][0:64, :], scalar=cst[0:64, C_QN + l:C_QN + l + 1],
                                                       in1=f32t[t2][0:64, :], op0=ALU.mult, op1=ALU.mult)
                                return V.scalar_tensor_tensor(out=qnz[64:128, 2 * j + 1, :], in0=f32t[t][64:128, :],
                                                              scalar=cst[64:128, C_QN + l:C_QN + l + 1],
                                                              in1=f32t[t2][64:128, :], op0=ALU.mult, op1=ALU.mult)
                            P.add("dve", g, r=[r_f32t[t], r_f32t[t2], r_cst], w=[r_qnz[2 * j], r_qnz[2 * j + 1]])
                        else:
                            P.add("dve", lambda: V.scalar_tensor_tensor(out=Kb[:, j, TB:2 * TB], in0=f32t[t][:], scalar=cst[:, C_KN + l:C_KN + l + 1],
                                                                        in1=f32t[t2][:], op0=ALU.mult, op1=ALU.mult),
                                  r=[r_f32t[t], r_f32t[t2], r_cst], w=[r_Kcur[j]])
                    return ev
                add_fm_fillers("aq", qk_evac("q"))
                add_fm_fillers("ak", qk_evac("k"))
                stv = {}
                for tg in range(4):
                    def flv(tg=tg):
                        if "s" not in stv:
                            stv["s"] = load_group(l, "av")
                        s = stv["s"]
                        wv = ring[:, s, :].rearrange("p (k c) -> p k c", k=8)
                        b = bank()
                        mm_group(b, 0, 512, [(hT[:, k, tg * 128:(tg + 1) * 128], wv[:, k, :]) for k in range(8)], r=r_hT + [r_ring[s]])
                        P.add("act", lambda: A.activation(out=Vb[:, 4 + tg, :, 0:64], in_=bankap(b).rearrange("p (h e) -> p h e", e=64), func=AF.Copy),
                              r=r_bank[b], w=[r_Vcur[tg]])
                    fillers.append(flv)

                def gate_evac(idx0):
                    def ev(j, b):
                        t = tmp32()
                        P.add("act", lambda: A.activation(out=f32t[t][:], in_=bankap(b), func=AF.Exp, scale=-1.0), r=r_bank[b], w=[r_f32t[t]])
                        P.chain("act", [lambda: A.activation(out=f32t[t][:], in_=f32t[t][:], func=AF.Ln, bias=1.0),
                                        lambda: A.activation(out=big[:, idx0 + j, :], in_=f32t[t][:], func=AF.Exp, scale=-1.0)],
                                r=[r_f32t[t]], w=[r_f32t[t], r_big[idx0 + j]])
                    return ev
                for gi_, gn in enumerate(("ga0", "ga1", "gb0", "gb1")):
                    add_fm_fillers(gn, gate_evac(gi_ * 4))

                if STG < 3:
                    return
                def hgrn_head(h):
                    hp = h % 2
                    P.add("dve", lambda h=h: V.tensor_scalar(out=sigA[h][:], in0=sigA[h][:], scalar1=omlt[:, l, h:h + 1], scalar2=lbt[:, l, h:h + 1],
                                                             op0=ALU.mult, op1=ALU.add), r=[r_sigA[h], r_const], w=[r_sigA[h]])
                    yield "s0"
                    P.add("act", lambda h=h: A.activation(out=hL[:], in_=sigA[h][:], func=AF.Ln), r=[r_sigA[h]], w=[r_hL])
                    yield "s0"
                    P.add("dve", lambda h=h: V.tensor_tensor_scan(out=hB[:], data0=scanm[:], data1=hL[:], initial=0.0, op0=ALU.mult, op1=ALU.add),
                          r=[r_hL, r_const], w=[r_hB])
                    bv = hB[:].rearrange("p (c t) -> p c t", t=64)
                    P.add("dve", lambda h=h, bv=bv: V.tensor_tensor(out=hF[:].rearrange("p (c t) -> p c t", t=64), in0=bv,
                                                                    in1=bv[:, :, 63:64].to_broadcast([128, 8, 64]), op=ALU.subtract),
                          r=[r_hB], w=[r_hF])
                    yield "s0"
                    P.add("act", lambda h=h: A.activation(out=sigA[h][:], in_=sigA[h][:], func=AF.Identity, scale=-1.0, bias=1.0),
                          r=[r_sigA[h]], w=[r_sigA[h]])

                    def fexp(h=h):
                        A.activation(out=hL[:], in_=hB[:], func=AF.Exp)
                        return A.activation(out=hE[:], in_=hB[:], func=AF.Exp, scale=-1.0)
                    P.add("act", fexp, r=[r_hB], w=[r_hL, r_hE])
                    P.add("act", lambda h=h: A.activation(out=hF[:], in_=hF[:], func=AF.Exp, scale=-1.0), r=[r_hF], w=[r_hF])
                    yield "s0"

                    P.add("dve", lambda h=h, hp=hp: V.tensor_tensor(out=qeT[hp][:], in0=qbf[h][:], in1=hL[:], op=ALU.mult),
                          r=[r_qbf[h], r_hL], w=[r_qeT[hp]])

                    def fkk(h=h, hp=hp):
                        PO.tensor_tensor(out=kdT[hp][:], in0=sigA[h][:], in1=hF[:], op=ALU.mult)
                        return PO.tensor_tensor(out=keT[hp][:], in0=sigA[h][:], in1=hE[:], op=ALU.mult)
                    P.add("pool", fkk, r=[r_hE, r_hF, r_sigA[h]], w=[r_keT[hp], r_kdT[hp]])
                    P.add("dve", lambda h=h, hp=hp: V.tensor_copy(out=acol[hp][:], in_=hL[:, 63::64]), r=[r_hL], w=[r_acol[hp]])
                    if ti == 0:
                        P.add("pool", lambda hp=hp: PO.memset(Sall[hp][:, 0, :], 0.0), w=[r_Sall[hp]])
                    else:
                        P.dma("pool", lambda l=l, h=h, hp=hp: PO.dma_start(out=Sall[hp][:, 0, :], in_=sst_d[l, :, h * 128:(h + 1) * 128]),
                              r=[r_sst[l]], w=[r_Sall[hp]], chan="sld", nslots=2)
                    yield "x"
                    for (src, rsrc, dst, rdst) in ((kdT[hp], r_kdT[hp], kdtok[hp], r_kdtok[hp]), (iTb[h], r_iT[h], itok[hp], r_itok[hp])):
                        bb = bbank()

                        def ftr(src=src, bb=bb):
                            ins = None
                            for c in range(8):
                                ins = PE.transpose(PSB[0:64, bb * 1024 + c * 128:bb * 1024 + (c + 1) * 128], src[:, c * 64:(c + 1) * 64], identb[:])
                            return ins
                        P.add("pe", ftr, r=[rsrc, r_const], w=[r_bbank[bb]])
                        P.add("act", lambda dst=dst, bb=bb: A.activation(out=dst[:].rearrange("p c k -> p (c k)"), in_=PSB[0:64, bb * 1024:(bb + 1) * 1024], func=AF.Copy),
                              r=[r_bbank[bb]], w=[rdst])
                    yield "x"
                    B0, B1 = 2 + 2 * hp, 3 + 2 * hp
                    ba = B0

                    def fat(hp=hp, ba=ba):
                        ins = None
                        for c in range(8):
                            ins = PE.matmul(PSF[0:64, ba * 512 + c * 64:ba * 512 + (c + 1) * 64], lhsT=keT[hp][:, c * 64:(c + 1) * 64],
                                            rhs=qeT[hp][:, c * 64:(c + 1) * 64], start=True, stop=True)
                        return ins
                    P.add("pe", fat, r=[r_keT[hp], r_qeT[hp]], w=r_bank[ba])
                    P.add("dve", lambda hp=hp, ba=ba: V.tensor_tensor(out=attm[hp][:], in0=PSF[0:64, ba * 512:(ba + 1) * 512], in1=cmask[:], op=ALU.mult),
                          r=r_bank[ba] + [r_const], w=[r_attm[hp]])
                    for rnd in range(2):
                        b = B1 if rnd == 0 else B0

                        def fdl(hp=hp, b=b, rnd=rnd):
                            ins = None
                            for cc in range(4):
                                c = rnd * 4 + cc
                                ins = PE.matmul(PSF[:, b * 512 + cc * 128:b * 512 + (cc + 1) * 128], lhsT=kdtok[hp][:, c, :], rhs=itok[hp][:, c, :],
                                                start=True, stop=True)
                            return ins
                        P.add("pe", fdl, r=[r_kdtok[hp], r_itok[hp]], w=r_bank[b])
                        yield "x"
                        for cc in range(4):
                            c = rnd * 4 + cc
                            P.add("dve", lambda hp=hp, c=c, cc=cc, b=b: V.scalar_tensor_tensor(
                                out=Sall[hp][:, c + 1, :], in0=Sall[hp][:, c, :], scalar=acol[hp][:, c:c + 1],
                                in1=PSF[:, b * 512 + cc * 128:b * 512 + (cc + 1) * 128], op0=ALU.mult, op1=ALU.add),
                                r=[r_Sall[hp], r_acol[hp]] + r_bank[b], w=[r_Sall[hp]])
                    P.add("act", lambda hp=hp: A.activation(out=Sbf[hp][:], in_=Sall[hp][:, 0:8, :], func=AF.Copy), r=[r_Sall[hp]], w=[r_Sbf[hp]])
                    if ti + 1 < n_tiles:
                        P.dma("pool", lambda l=l, h=h, hp=hp: PO.dma_start(out=sst_d[l, :, h * 128:(h + 1) * 128], in_=Sall[hp][:, 8, :]),
                              r=[r_Sall[hp]], w=[r_sst[l]], chan="sst", nslots=2)
                    yield "x"
                    bo = B1

                    def fo(hp=hp, bo=bo):
                        ins = None
                        for c in range(8):
                            PE.matmul(PSF[:, bo * 512 + c * 64:bo * 512 + (c + 1) * 64], lhsT=Sbf[hp][:, c, :], rhs=qeT[hp][:, c * 64:(c + 1) * 64],
                                      start=True, stop=False)
                            ins = PE.matmul(PSF[:, bo * 512 + c * 64:bo * 512 + (c + 1) * 64], lhsT=itok[hp][:, c, :], rhs=attm[hp][:, c * 64:(c + 1) * 64],
                                            start=False, stop=True)
                        return ins
                    P.add("pe", fo, r=[r_Sbf[hp], r_qeT[hp], r_itok[hp], r_attm[hp]], w=r_bank[bo])
                    yield "x"
                    si = nsq()
                    P.add("act", lambda si=si, bo=bo: A.activation(out=sq[si][:], in_=bankap(bo), func=AF.Square), r=r_bank[bo], w=[r_sq[si]])
                    bn = B0
                    P.add("pe", lambda si=si, bn=bn: PE.matmul(bankap(bn), lhsT=onesb[:], rhs=sq[si][:], start=True, stop=True),
                          r=[r_sq[si], r_const], w=r_bank[bn])
                    t = 4
                    yield "x"

                    P.chain("act", [lambda t=t, bn=bn: A.activation(out=f32t[t][:], in_=bankap(bn), func=AF.Ln, scale=1.0 / 128, bias=epsc[:, 0:1]),
                                    lambda t=t: A.activation(out=f32t[t][:], in_=f32t[t][:], func=AF.Exp, scale=-0.5)],
                            r=r_bank[bn] + [r_const], w=[r_f32t[t]])
                    yield "x"

                    P.chain("dve", [lambda t=t, bo=bo: V.scalar_tensor_tensor(out=f32t[t][:], in0=bankap(bo), scalar=cst[:, C_ON + l:C_ON + l + 1], in1=f32t[t][:],
                                                                              op0=ALU.mult, op1=ALU.mult),
                                    lambda h=h, t=t: V.tensor_tensor(out=oaT[:, h, :], in0=f32t[t][:], in1=sgb16[h][:], op=ALU.mult)],
                            r=r_bank[bo] + [r_f32t[t], r_cst, r_sg[h]], w=[r_f32t[t], r_oaT[h]])

                def inproj_gi():
                    inproj_fm("g", lambda j, b: P.add("act", lambda: A.activation(out=sgb16[j][:], in_=bankap(b), func=AF.Silu),
                                                      r=r_bank[b], w=[r_sg[j]]))
                    inproj_fm("i", lambda j, b: P.add("act", lambda: A.activation(out=iTb[j][:], in_=bankap(b), func=AF.Copy),
                                                      r=r_bank[b], w=[r_iT[j]]))
                gens = [hgrn_head(h) for h in range(4)]
                state["pool_n"] = 2
                state["bank"] = state["bank"] % 2

                def F(n=1):
                    for _ in range(n):
                        if fillers:
                            fillers.pop(0)()
                started, done, in_s0 = set(), set(), [None]
                first = [True]

                def can_start(h):
                    return h not in started and in_s0[0] is None and (h < 2 or (h - 2) in done)
                while len(done) < 4:
                    progressed = False
                    for h in range(4):
                        if h in done:
                            continue
                        if h not in started:
                            if not can_start(h):
                                continue
                            started.add(h)
                            in_s0[0] = h
                        tag = next(gens[h], None)
                        progressed = True
                        if tag is None:
                            done.add(h)
                            if in_s0[0] == h:
                                in_s0[0] = None
                        elif tag == "x" and in_s0[0] == h:
                            in_s0[0] = None
                        if first[0] and 0 in started and in_s0[0] is None:
                            inproj_gi()
                            first[0] = False
                        F(1)
                    assert progressed
                if first[0]:
                    inproj_gi()
                F(100)
                for g_ in gens:
                    for _ in g_:
                        pass
                state["pool_n"] = 6
                if ti + 1 < n_tiles:
                    P.dma("pool", lambda l=l: PO.dma_start(out=kst_d[l].rearrange("p (a t) -> p a t", t=TB), in_=Kb[:, :, TB:2 * TB]),
                          r=r_Kcur, w=[r_kst[l]], chan="kvst", nslots=2)
                    P.dma("pool", lambda l=l: PO.dma_start(out=vst_d[l].rearrange("p (a h e) -> p a h e", h=8, e=VE), in_=Vb[:, 4:8, :, :]),
                          r=r_Vcur, w=[r_vst[l]], chan="kvst", nslots=2)
                s_ab = [load_group(l, "ab0"), load_group(l, "ab1")]
                if STG < 4:
                    return
                if debug and ti == n_tiles - 1 and l == 0:
                    dump("ab0", ring[:, s_ab[0], 0:2560], [128, 2560], BF16, [r_ring[s_ab[0]]])
                abv = [ring[:, s_ab[0], 0:2560].rearrange("p (h j q) -> p h j q", h=4, j=5),
                       ring[:, s_ab[1], 0:2560].rearrange("p (h j q) -> p h j q", h=4, j=5)]
                rk = [r_Kprev] + r_Kcur

                def att_scores(pr, h, pi):
                    hg, hh = h // 4, h % 4
                    j0 = max(0, 4 - (ti * 4 + pr))
                    bs, b4 = h % 2, 2 + h % 2

                    def fsc():
                        ins = None
                        for j in range(j0, 5):
                            dst = PSF[:, bs * 512 + j * 128:bs * 512 + (j + 1) * 128] if j < 4 else PSF[:, b4 * 512:b4 * 512 + 128]
                            PE.matmul(dst, lhsT=Kb[:, h // 2, (pr + j) * 128:(pr + j + 1) * 128], rhs=qnz[:, h, pr * 128:(pr + 1) * 128],
                                      start=True, stop=False)
                            msk = (j == 0) or (j == 4)
                            ins = PE.matmul(dst, lhsT=identb[:], rhs=abv[hg][:, hh, j, :], start=False, stop=not msk)
                            if msk:
                                ins = PE.matmul(dst, lhsT=identb[:], rhs=(mask0 if j == 0 else mask4)[:], start=False, stop=True)
                        return ins
                    P.add("pe", fsc, r=rk + [r_qnz[h], r_ring[s_ab[hg]], r_const], w=r_bank[bs] + r_bank[b4])

                    def fex():
                        if j0 < 4:
                            A.activation(out=Pt[pi][:, j0 * 128:512], in_=PSF[:, bs * 512 + j0 * 128:(bs + 1) * 512], func=AF.Exp)
                        return A.activation(out=Pt[pi][:, 512:640], in_=PSF[:, b4 * 512:b4 * 512 + 128], func=AF.Exp)
                    P.add("act", fex, r=r_bank[bs] + r_bank[b4], w=[r_Pt[pi]])

                def att_pv(pr, h, pi):
                    hg, hh = h // 4, h % 4
                    j0 = max(0, 4 - (ti * 4 + pr))
                    bov = 4 + hg
                    oi = hg

                    def fpv():
                        ins = None
                        for j in range(j0, 5):
                            ins = PE.matmul(PSF[:, bov * 512 + hh * VE:bov * 512 + (hh + 1) * VE], lhsT=Pt[pi][:, j * 128:(j + 1) * 128],
                                            rhs=Vb[:, pr + j, h, :], start=(j == j0), stop=(j == 4))
                        return ins
                    P.add("pe", fpv, r=[r_Pt[pi], r_Vprev] + r_Vcur, w=r_bank[bov])
                    if hh == 3:
                        def fnm1():
                            pv = PSF[:, bov * 512:bov * 512 + 4 * VE].rearrange("p (h e) -> p h e", e=VE)
                            return V.reciprocal(out=rden[oi][:], in_=pv[:, :, 64:65])

                        def fnm2():
                            pv = PSF[:, bov * 512:bov * 512 + 4 * VE].rearrange("p (h e) -> p h e", e=VE)
                            return V.tensor_tensor(out=obt[pr % 2][:, hg * 256:(hg + 1) * 256].rearrange("p (h e) -> p h e", e=64), in0=pv[:, :, 0:64],
                                                   in1=rden[oi][:].to_broadcast([128, 4, 64]), op=ALU.mult)
                        P.chain("dve", [fnm1, fnm2], r=r_bank[bov], w=[r_rden[oi], r_obt[pr % 2]])
                    if h == 7:
                        bb = bbank()

                        def ftr():
                            ins = None
                            for fc in range(4):
                                ins = PE.transpose(PSB[:, bb * 1024 + fc * 128:bb * 1024 + (fc + 1) * 128], obt[pr % 2][:, fc * 128:(fc + 1) * 128], identb[:])
                            return ins
                        P.add("pe", ftr, r=[r_obt[pr % 2], r_const], w=[r_bbank[bb]])
                        P.add("act", lambda: A.activation(out=obT[:, :, pr * 128:(pr + 1) * 128],
                                                          in_=PSB[:, bb * 1024:bb * 1024 + 512].rearrange("p (c t) -> p c t", t=128), func=AF.Copy),
                              r=[r_bbank[bb]], w=r_obT)

                steps = [(pr, h) for pr in range(4) for h in range(8)]
                LAG = 1
                for i in range(len(steps) + LAG):
                    if i < len(steps):
                        att_scores(steps[i][0], steps[i][1], i % 3)
                    if i >= LAG:
                        att_pv(steps[i - LAG][0], steps[i - LAG][1], (i - LAG) % 3)

                if STG < 5:
                    return
                sa = load_group(l, "wa")
                sb_ = load_group(l, "wb")
                wav = ring[:, sa, :].rearrange("p (j k c) -> p j k c", j=8, k=4)
                wbv = ring[:, sb_, :].rearrange("p (j k c) -> p j k c", j=8, k=4)
                for oc in range(8):
                    ba_, bb_ = bank(), bank()
                    mm_group(ba_, 0, 512, [(wav[:, oc, k, :], oaT[:, k, :]) for k in range(4)], r=r_oaT + [r_ring[sa]])
                    mm_group(bb_, 0, 512, [(wbv[:, oc, k, :], obT[:, k, :]) for k in range(4)], r=r_obT + [r_ring[sb_]])
                    t1, t2 = tmp32(), tmp32()
                    P.add("dve", lambda oc=oc, ba_=ba_, t1=t1: V.tensor_tensor(out=f32t[t1][:], in0=bankap(ba_), in1=big[:, oc, :], op=ALU.mult),
                          r=r_bank[ba_] + [r_big[oc]], w=[r_f32t[t1]])
                    P.add("dve", lambda oc=oc, bb_=bb_, t2=t2: V.tensor_tensor(out=f32t[t2][:], in0=bankap(bb_), in1=big[:, 8 + oc, :], op=ALU.mult),
                          r=r_bank[bb_] + [r_big[8 + oc]], w=[r_f32t[t2]])
                    P.add("pool", lambda oc=oc, t1=t1, t2=t2: PO.tensor_tensor(out=hT[:, oc, :], in0=f32t[t1][:], in1=f32t[t2][:], op=ALU.add),
                          r=[r_f32t[t1], r_f32t[t2]], w=[r_hT[oc]])
                fn_ = FusedNorm()
                for half in range(2):
                    s = load_group(l, "wo%d" % half)
                    wv = ring[:, s, :].rearrange("p (j k c) -> p j k c", j=4, k=8)
                    for j in range(4):
                        oc = half * 4 + j
                        b = bank()
                        mm_group(b, 0, 512, [(wv[:, j, k, :], hT[:, k, :]) for k in range(8)], r=r_hT + [r_ring[s]])
                        P.add("dve", lambda oc=oc, b=b: V.tensor_tensor(out=xT[:, oc, :], in0=xT[:, oc, :], in1=bankap(b), op=ALU.add),
                              r=r_bank[b] + [r_xT[oc]], w=[r_xT[oc]])
                        fn_.chunk_done(oc)
                fn_.close()
                if debug and ti == n_tiles - 1 and l == 0:
                    dump("oaT", oaT[:], [128, 4, TB], BF16, r_oaT)
                    dump("qnz", qnz[:], [128, 8, TB], BF16, r_qnz)
                    dump("Kb", Kb[:], [128, 4, 2 * TB], BF16, [r_Kprev] + r_Kcur)
                    dump("Vb", Vb[:], [128, 8, 8, VE], BF16, [r_Vprev] + r_Vcur)
                    dump("obt1", obt[1][:], [128, 512], BF16, [r_obt[1]])
                    dump("Pt1", Pt[1][:], [128, 640], BF16, [r_Pt[1]])
                    dump("obT", obT[:], [128, 4, TB], BF16, r_obT)
                    dump("x1", xT[:], [128, 8, TB], F32, r_xT)

                if STG < 6:
                    return
                rmsnorm(l, C_GFFN)
                for tg in range(4):
                    b = bank()

                    def ftp(tg=tg, b=b):
                        PE.transpose(PSF[:, b * 512:b * 512 + 128], stg[p_si][:, tg * 256:tg * 256 + 128], identf[:])
                        return PE.transpose(PSF[:, b * 512 + 128:b * 512 + 256], stg[p_si][:, tg * 256 + 128:tg * 256 + 256], identf[:])
                    P.add("pe", ftp, r=[r_stg[p_si], r_const], w=r_bank[b])
                    P.add("act", lambda tg=tg, b=b: A.activation(out=pTb[:, :, tg * 128:(tg + 1) * 128],
                                                                 in_=PSF[:, b * 512:b * 512 + 256].rearrange("p (c t) -> p c t", t=128), func=AF.Copy),
                          r=r_bank[b], w=[r_pT])
                for i in range(11):
                    s = load_group(l, "ff%d" % i)
                    wv = ring[:, s, :].rearrange("p (j k c) -> p j k c", j=4, k=8)
                    for jj in range(2):
                        j = 2 * i + jj
                        bg, bu = bank(), bank()
                        if j == 0:
                            mm_group_split(bg, [(wv[:, 2 * jj, k, :], hT[:, k, :]) for k in range(8)], r_hT, [r_ring[s]])
                        else:
                            mm_group(bg, 0, 512, [(wv[:, 2 * jj, k, :], hT[:, k, :]) for k in range(8)], r=r_hT + [r_ring[s]])
                        mm_group(bu, 0, 512, [(wv[:, 2 * jj + 1, k, :], hT[:, k, :]) for k in range(8)], r=r_hT + [r_ring[s]])
                        t = tmp32()
                        P.add("act", lambda bg=bg, t=t: A.activation(out=f32t[t][:], in_=bankap(bg), func=AF.Silu), r=r_bank[bg], w=[r_f32t[t]])
                        P.add("dve", lambda j=j, bu=bu, t=t: V.tensor_tensor(out=big[:, j, :], in0=bankap(bu), in1=f32t[t][:], op=ALU.mult),
                              r=r_bank[bu] + [r_f32t[t]], w=[r_big[j]])
                fn_ = FusedNorm()
                for oc in range(8):
                    s = load_group(l, "dn%d" % oc)
                    wv = ring[:, s, 0:2816].rearrange("p (k c) -> p k c", k=22)
                    b = bank()
                    mm_group(b, 0, 512, [(wv[:, k, :], big[:, k, :]) for k in range(22)], r=r_big + [r_ring[s]])
                    P.add("dve", lambda oc=oc, b=b: V.tensor_tensor(out=xT[:, oc, :], in0=xT[:, oc, :], in1=bankap(b), op=ALU.add),
                          r=r_bank[b] + [r_xT[oc]], w=[r_xT[oc]])
                    fn_.chunk_done(oc)
                fn_.close()

                if STG < 7:
                    return
                rmsnorm(l, C_GPLE)
                spp = None
                fn_ = FusedNorm(enable=(l + 1 < n_layers))
                for half in range(2):
                    s = load_group(l, "pg%d" % half)
                    if half == 0:
                        spp = load_group(l, "pp")
                    wv = ring[:, s, :].rearrange("p (j k c) -> p j k c", j=4, k=8)
                    ppv = ring[:, spp, 0:2048].rearrange("p (j k c) -> p j k c", j=8, k=2)
                    for j in range(4):
                        oc = half * 4 + j
                        bg, bp = bank(), bank()
                        mm_group(bg, 0, 512, [(wv[:, j, k, :], hT[:, k, :]) for k in range(8)], r=r_hT + [r_ring[s]])
                        mm_group(bp, 0, 512, [(ppv[:, oc, k, :], pTb[:, k, :]) for k in range(2)], r=[r_pT, r_ring[spp]])
                        t = tmp32()
                        P.add("act", lambda bg=bg, t=t: A.activation(out=f32t[t][:], in_=bankap(bg), func=AF.Sigmoid), r=r_bank[bg], w=[r_f32t[t]])
                        P.add("dve", lambda bp=bp, t=t: V.tensor_tensor(out=f32t[t][:], in0=bankap(bp), in1=f32t[t][:], op=ALU.mult),
                              r=r_bank[bp] + [r_f32t[t]], w=[r_f32t[t]])
                        P.add("pool", lambda oc=oc, t=t: PO.tensor_tensor(out=xT[:, oc, :], in0=xT[:, oc, :], in1=f32t[t][:], op=ALU.add),
                              r=[r_f32t[t], r_xT[oc]], w=[r_xT[oc]])
                        fn_.chunk_done(oc)
                fn_.close()

            for l in range(n_layers):
                layer(ti, l, t0)
            for tg in range(4):
                si = nstg()
                for half in range(2):
                    b = bank()

                    def ftr(tg=tg, half=half, b=b):
                        ins = None
                        for cc in range(4):
                            c = half * 4 + cc
                            ins = PE.transpose(PSF[:, b * 512 + cc * 128:b * 512 + (cc + 1) * 128], xT[:, c, tg * 128:(tg + 1) * 128], identf[:])
                        return ins
                    P.add("pe", ftr, r=r_xT[half * 4:half * 4 + 4] + [r_const], w=r_bank[b])
                    P.add("act", lambda si=si, half=half, b=b: A.activation(out=stg[si][:, half * 512:(half + 1) * 512], in_=bankap(b), func=AF.Copy),
                          r=r_bank[b], w=[r_stg[si]])
                op = P.dma("pool", lambda tg=tg, si=si, t0=t0: PO.dma_start(out=o_d[t0 + tg * 128:t0 + (tg + 1) * 128, :], in_=stg[si][:]),
                           r=[r_stg[si]], w=[r_out], chan="out", nslots=2)
                op.is_out = True

        nwait = P.emit()
    return nc, dbg_d


_CACHE = {}


def _run(inp, n_tiles, n_layers, ncores, debug=False, trace=False):
    wl, cst = _host_layout(inp)
    key = (n_tiles, n_layers, debug)
    if key not in _CACHE:
        _CACHE[key] = build(n_tiles, n_layers, debug)
    nc, dbg = _CACHE[key]
    S = n_tiles * TB
    in_maps = []
    for b in range(ncores):
        in_maps.append({"x": np.ascontiguousarray(inp["x"][b, :S]),
                        "p": np.ascontiguousarray(inp["p"][:, b, :S]),
                        "w": wl, "cst": cst})
    res = run_bass_kernel_spmd(nc, in_maps, core_ids=list(range(ncores)), trace=trace)
    return res


def kernel(**inputs):
    inp = {k: np.asarray(v) for k, v in inputs.items()}
    res = _run(inp, SEQ // TB, DEPTH, 8)
    out = np.stack([np.asarray(r["out"]) for r in res.results], axis=0)
    return out.astype(np.float32)
```

```python
import contextlib
import numpy as np
import concourse.bass as bass
import concourse.mybir as mybir
from concourse.bass_utils import run_bass_kernel_spmd

F32 = mybir.dt.float32
BF16 = mybir.dt.bfloat16
AF = mybir.ActivationFunctionType
ALU = mybir.AluOpType

D = 1024
SEQ = 4096
DEPTH = 4
TB = 512
NCH = 8
DFF = 2816
NFF = 22
PLE = 256
EPS = 1e-6
G = 4096
NEG = -30000.0
VE = 66

GROUPS = [("q", G), ("f", G), ("ga0", G), ("ga1", G), ("gb0", G), ("gb1", G), ("g", G), ("i", G),
          ("aq", G), ("ak", G), ("av", G), ("ab0", 2560), ("ab1", 2560), ("wa", G), ("wb", G),
          ("wo0", G), ("wo1", G)]
GROUPS += [("ff%d" % i, G) for i in range(11)]
GROUPS += [("dn%d" % i, 2816) for i in range(8)]
GROUPS += [("pg0", G), ("pg1", G), ("pp", 2048)]
GOFF = {}
_o = 0
for _n, _c in GROUPS:
    GOFF[_n] = (_o, _c)
    _o += _c
WCOLS = _o

C_GMIX, C_GFFN, C_GPLE = 0, 32, 64
C_LB = 96
C_ON = 112
C_QN = 116
C_KN = 120
NCST = 124


def _lhsT_slab(W, col0, ncc, nk):
    sub = W[:, col0:col0 + ncc * 128].reshape(nk, 128, ncc, 128)
    return np.ascontiguousarray(sub.transpose(1, 2, 0, 3)).reshape(128, ncc * nk * 128)


def _host_layout(inp):
    w_in = inp["w_in"]
    wl = np.empty((DEPTH, 128, WCOLS), np.float32)
    for l in range(DEPTH):
        def put(name, arr):
            o, c = GOFF[name]
            assert arr.shape == (128, c), (name, arr.shape, c)
            wl[l, :, o:o + c] = arr
        W = w_in[l]
        put("q", _lhsT_slab(W, 0, 4, 8))
        put("f", _lhsT_slab(W, 512, 4, 8))
        put("i", _lhsT_slab(W, 1024, 4, 8))
        put("g", _lhsT_slab(W, 1536, 4, 8))
        put("aq", _lhsT_slab(W, 2048, 4, 8))
        put("ak", _lhsT_slab(W, 2560, 4, 8))
        put("av", np.ascontiguousarray(W[:, 3072:3584].reshape(8, 128, 512).transpose(1, 0, 2)).reshape(128, G))
        put("ga0", _lhsT_slab(W, 3584, 4, 8))
        put("ga1", _lhsT_slab(W, 4096, 4, 8))
        put("gb0", _lhsT_slab(W, 4608, 4, 8))
        put("gb1", _lhsT_slab(W, 5120, 4, 8))
        rb = inp["attn_rel_bias"][l]
        k = np.arange(128)[:, None, None]
        j = np.arange(5)[None, :, None]
        q = np.arange(128)[None, None, :]
        idx = np.clip(512 + q - (128 * j + k), -128, 128) + 128
        ab = rb[:, idx]
        ab = np.ascontiguousarray(ab.transpose(1, 0, 2, 3)).reshape(128, 8 * 640)
        put("ab0", ab[:, :2560])
        put("ab1", ab[:, 2560:])
        put("wa", _lhsT_slab(inp["w_branch_a"][l], 0, 8, 4))
        put("wb", _lhsT_slab(inp["w_branch_b"][l], 0, 8, 4))
        put("wo0", _lhsT_slab(inp["w_out"][l], 0, 4, 8))
        put("wo1", _lhsT_slab(inp["w_out"][l], 512, 4, 8))
        wg, wu = inp["w_ffn_gate"][l], inp["w_ffn_up"][l]
        for i in range(11):
            parts = []
            for jj in (2 * i, 2 * i + 1):
                parts.append(_lhsT_slab(wg, jj * 128, 1, 8))
                parts.append(_lhsT_slab(wu, jj * 128, 1, 8))
            put("ff%d" % i, np.concatenate(parts, axis=1))
        wd = inp["w_ffn_down"][l]
        for oc in range(8):
            put("dn%d" % oc, _lhsT_slab(wd, oc * 128, 1, 22))
        put("pg0", _lhsT_slab(inp["w_ple_gate"][l], 0, 4, 8))
        put("pg1", _lhsT_slab(inp["w_ple_gate"][l], 512, 4, 8))
        put("pp", _lhsT_slab(inp["w_ple_proj"][l], 0, 8, 2))
    cst = np.zeros((128, NCST), np.float32)
    for l in range(DEPTH):
        cst[:, C_GMIX + l * 8:C_GMIX + l * 8 + 8] = inp["norm_mix_g"][l].reshape(8, 128).T
        cst[:, C_GFFN + l * 8:C_GFFN + l * 8 + 8] = inp["norm_ffn_g"][l].reshape(8, 128).T
        cst[:, C_GPLE + l * 8:C_GPLE + l * 8 + 8] = inp["norm_ple_g"][l].reshape(8, 128).T
        cst[:, C_LB + l * 4:C_LB + l * 4 + 4] = inp["hgrn_lb_logits"][l].reshape(4, 128).T
        cst[:, C_ON + l] = inp["hgrn_onorm_g"][l]
        cst[:, C_QN + l] = np.tile(inp["attn_qnorm_g"][l], 2)
        cst[:, C_KN + l] = np.tile(inp["attn_knorm_g"][l], 2)
    return wl, cst


class Reg:
    __slots__ = ("name", "ws", "rs", "excl")

    def __init__(self, name, excl=False):
        self.name = name
        self.ws = {}
        self.rs = []
        self.excl = excl


class Op:
    __slots__ = ("eng", "fn", "deps", "sig", "signal", "dma", "chan", "is_out")

    def __init__(self, eng, fn, dma=False, chan=None):
        self.eng = eng
        self.fn = fn
        self.deps = []
        self.sig = False
        self.signal = None
        self.dma = dma
        self.chan = chan
        self.is_out = False


class Prog:
    def __init__(self, nc, es):
        self.nc = nc
        self.es = es
        self.ops = []
        self.eng_obj = {"pe": nc.tensor, "act": nc.scalar, "dve": nc.vector, "pool": nc.gpsimd, "sp": nc.sync}
        self.sems = {}
        self.chan_sems = {}

    def sem(self, name):
        if name not in self.sems:
            self.sems[name] = self.es.enter_context(self.nc.semaphore(name))
        return self.sems[name]

    def _dep(self, op, d, raw):
        if d is op:
            return
        if d.eng == op.eng and not d.dma:
            if op.dma:
                pass
            elif op.eng == "pe":
                return
            elif not raw:
                return
        d.sig = True
        op.deps.append(d)

    def add(self, eng, fn, r=(), w=(), dma=False, chan=None):
        op = Op(eng, fn, dma, chan)
        if dma:
            op.sig = True
        w = list(w) + [R for R in r if R.excl]
        r = [R for R in r if not R.excl]
        w = list(dict.fromkeys(w))
        for R in r:
            for d in R.ws.values():
                self._dep(op, d, True)
        for R in w:
            for d in R.ws.values():
                self._dep(op, d, False)
            for d in R.rs:
                self._dep(op, d, False)
        for R in r:
            R.rs.append(op)
        for R in w:
            R.ws = {eng if not dma else ("dma", id(op)): op}
            R.rs = []
        self.ops.append(op)
        return op

    def chain(self, eng, fns, r=(), w=()):
        link = Reg("chain")
        op = None
        for i, fn in enumerate(fns):
            rr = list(r) + ([link] if i > 0 else [])
            op = self.add(eng, fn, r=rr, w=list(w) + [link])
        return op

    def dma(self, queue, fn, r=(), w=(), chan="misc", nslots=4):
        if chan not in self.chan_sems:
            self.chan_sems[chan] = {"n": nslots, "i": 0, "last": [None] * nslots,
                                    "sems": [self.sem("d_%s_%d" % (chan, i)) for i in range(nslots)],
                                    "cnt": [0] * nslots}
        return self.add(queue, fn, r, w, dma=True, chan=chan)

    def emit(self):
        waited = {e: {} for e in self.eng_obj}
        cnt = {e: 0 for e in self.eng_obj}
        esem = {e: self.sem("e_" + e) for e in ("pe", "act", "dve", "pool")}
        nwait = 0

        def wait(eng, sem, val):
            nonlocal nwait
            key = id(sem)
            if waited[eng].get(key, 0) < val:
                self.eng_obj[eng].wait_ge(sem, val)
                waited[eng][key] = val
                nwait += 1

        outs = []
        for op in self.ops:
            for d in op.deps:
                s, v = d.signal
                wait(op.eng, s, v)
            if op.dma:
                ch = self.chan_sems[op.chan]
                i = ch["i"]
                ch["i"] = (i + 1) % ch["n"]
                s = ch["sems"][i]
                if ch["cnt"][i] > 0:
                    wait(op.eng, s, ch["cnt"][i])
                inst = op.fn()
                ch["cnt"][i] += 16
                inst.then_inc(s, 16)
                op.signal = (s, ch["cnt"][i])
                if op.is_out:
                    outs.append(op)
            else:
                inst = op.fn()
                if op.sig:
                    cnt[op.eng] += 1
                    inst.then_inc(esem[op.eng], 1)
                    op.signal = (esem[op.eng], cnt[op.eng])
        for op in outs:
            s, v = op.signal
            wait("pool", s, v)
        return nwait


def build(n_tiles, n_layers, debug=False):
    nc = bass.Bass("TRN2", target_bir_lowering=False)
    S = n_tiles * TB
    x_d = nc.dram_tensor("x", [S, D], F32, kind="ExternalInput").ap()
    p_d = nc.dram_tensor("p", [DEPTH, S, PLE], F32, kind="ExternalInput").ap()
    w_d = nc.dram_tensor("w", [DEPTH, 128, WCOLS], F32, kind="ExternalInput").ap()
    c_d = nc.dram_tensor("cst", [128, NCST], F32, kind="ExternalInput").ap()
    o_d = nc.dram_tensor("out", [S, D], F32, kind="ExternalOutput").ap()
    wb_d = nc.dram_tensor("wbf", [DEPTH, 128, WCOLS], BF16, kind="Internal").ap()
    kst_d = nc.dram_tensor("kst", [DEPTH, 128, 4 * TB], BF16, kind="Internal").ap()
    vst_d = nc.dram_tensor("vst", [DEPTH, 128, 4 * 8 * VE], BF16, kind="Internal").ap()
    sst_d = nc.dram_tensor("sst", [DEPTH, 128, 4 * 128], F32, kind="Internal").ap()
    dbg_d = {}

    es = contextlib.ExitStack()
    with es:
        P = Prog(nc, es)

        def sb(name, shape, dt):
            return es.enter_context(nc.sbuf_tensor("s_" + name, shape, dt))

        xT = sb("xT", [128, NCH, TB], F32)
        hT = sb("hT", [128, NCH, TB], BF16)
        NRING = 5
        ring = sb("ring", [128, NRING, G], BF16)
        Kb = sb("Kb", [128, 4, 2 * TB], BF16)
        Vb = sb("Vb", [128, 8, 8, VE], BF16)
        big = sb("big", [128, NFF, TB], BF16)
        cst = sb("cst", [128, NCST], F32)
        identb = sb("identb", [128, 128], BF16)
        identf = sb("identf", [128, 128], F32)
        onesb = sb("onesb", [128, 128], BF16)
        onesblk = sb("onesblk", [128, 128], BF16)
        mask0 = sb("mask0", [128, 128], BF16)
        mask4 = sb("mask4", [128, 128], BF16)
        cmask = sb("cmask", [64, TB], BF16)
        scanm = sb("scanm", [128, TB], BF16)
        lbt = sb("lbt", [128, 4, 4], F32)
        omlt = sb("omlt", [128, 4, 4], F32)
        lbtmp = sb("lbtmp", [128, 4, 4], F32)
        lbs = sb("lbs", [128, 4], F32)
        stg = [sb("stg%d" % i, [128, D], F32) for i in range(2)]
        pTb = sb("pTb", [128, 2, TB], BF16)
        sq = [sb("sq%d" % i, [128, TB], BF16) for i in range(2)]
        NT32 = 5
        f32t = [sb("f32t%d" % i, [128, TB], F32) for i in range(NT32)]
        sigA = [sb("sigA%d" % h, [128, TB], F32) for h in range(4)]
        qbf = [sb("qbf%d" % h, [128, TB], BF16) for h in range(4)]
        iTb = [sb("iT%d" % h, [128, TB], BF16) for h in range(4)]
        sgb16 = [sb("sg%d" % h, [128, TB], BF16) for h in range(4)]
        hL = sb("hL", [128, TB], F32)
        hB = sb("hB", [128, TB], F32)
        hE = sb("hE", [128, TB], F32)
        hF = sb("hF", [128, TB], F32)
        qeT = [sb("qeT%d" % i, [128, TB], BF16) for i in range(2)]
        keT = [sb("keT%d" % i, [128, TB], BF16) for i in range(2)]
        kdT = [sb("kdT%d" % i, [128, TB], BF16) for i in range(2)]
        Sall = [sb("Sall%d" % i, [128, 9, 128], F32) for i in range(2)]
        Sbf = [sb("Sbf%d" % i, [128, 8, 128], BF16) for i in range(2)]
        kdtok = [sb("kdtok%d" % i, [64, 8, 128], BF16) for i in range(2)]
        itok = [sb("itok%d" % i, [64, 8, 128], BF16) for i in range(2)]
        attm = [sb("attm%d" % i, [64, TB], BF16) for i in range(2)]
        oaT = sb("oaT", [128, 4, TB], BF16)
        acol = [sb("acol%d" % i, [128, 8], F32) for i in range(2)]
        qnz = sb("qnz", [128, 8, TB], BF16)
        Pt = [sb("Pt%d" % i, [128, 640], BF16) for i in range(3)]
        obt = [sb("obt%d" % i, [128, 512], BF16) for i in range(2)]
        rden = [sb("rden%d" % i, [128, 4, 1], F32) for i in range(2)]
        obT = sb("obT", [128, 4, TB], BF16)
        if debug:
            dbgbuf = sb("dbgbuf", [128, 512], F32)
            r_dbgbuf = Reg("dbgbuf")
        PSF = es.enter_context(nc.psum_tensor("PSF", [128, 6 * 512], F32))
        PSB = es.enter_context(nc.psum_tensor("PSB", [128, 2 * 1024], BF16))

        def regs(prefix, n):
            return [Reg("%s%d" % (prefix, i)) for i in range(n)]
        r_xT = regs("xT", 8)
        r_hT = regs("hT", 8)
        r_ring = regs("ring", NRING)
        r_Kprev = Reg("Kprev"); r_Kcur = regs("Kcur", 4)
        r_Vprev = Reg("Vprev"); r_Vcur = regs("Vcur", 4)
        r_big = regs("big", NFF)
        r_cst = Reg("cst"); r_const = Reg("const")
        r_stg = regs("stg", 2)
        r_pT = Reg("pT")
        r_sq = regs("sq", 2)
        r_f32t = regs("f32t", 5)
        r_sigA = regs("sigA", 4); r_qbf = regs("qbf", 4); r_iT = regs("iT", 4); r_sg = regs("sg", 4)
        r_hL, r_hB, r_hE, r_hF = Reg("hL"), Reg("hB"), Reg("hE"), Reg("hF")
        r_qeT = regs("qeT", 2); r_keT = regs("keT", 2); r_kdT = regs("kdT", 2)
        r_Sall = regs("Sall", 2); r_Sbf = regs("Sbf", 2)
        r_kdtok = regs("kdtok", 2); r_itok = regs("itok", 2); r_attm = regs("attm", 2)
        r_oaT = regs("oaT", 4)
        r_acol = regs("acol", 2)
        r_lbtmp = Reg("lbtmp")
        r_qnz = regs("qnz", 8)
        r_Pt = regs("Pt", 3); r_obt = regs("obt", 2); r_rden = regs("rden", 2)
        r_obT = regs("obT", 4)
        r_bank = [[Reg("bk%d" % b, excl=True)] for b in range(6)]
        r_bbank = [Reg("bb%d" % i, excl=True) for i in range(2)]
        r_wsrc = Reg("wsrc")
        r_wbf = [[Reg("wbf%d_%s" % (l, n)) for n, _ in GROUPS] for l in range(DEPTH)]
        r_kst = regs("kst", DEPTH); r_vst = regs("vst", DEPTH); r_sst = regs("sst", DEPTH)
        r_out = Reg("out")
        GIDX = {n: i for i, (n, _) in enumerate(GROUPS)}

        state = {"bank": 0, "bb": 0, "f32t": 0, "sq": 0, "ring": 0, "stg": 0}

        state["pool_lo"] = 0
        state["pool_n"] = 6
        state["hbank"] = 0

        def bank():
            b = state["bank"]
            state["bank"] = (b + 1) % state["pool_n"]
            return state["pool_lo"] + b

        def hbank():
            b = state["hbank"]
            state["hbank"] = (b + 1) % 3
            return 3 + b

        def bankap(b):
            return PSF[:, b * 512:(b + 1) * 512]

        def bbank():
            b = state["bb"]
            state["bb"] = (b + 1) % 2
            return b

        def bbankap(b):
            return PSB[:, b * 1024:(b + 1) * 1024]

        def tmp32():
            i = state["f32t"]
            state["f32t"] = (i + 1) % 4
            return i

        def nsq():
            i = state["sq"]
            state["sq"] = (i + 1) % 2
            return i

        def nstg():
            i = state["stg"]
            state["stg"] = (i + 1) % 2
            return i

        V, A, PE, PO = nc.vector, nc.scalar, nc.tensor, nc.gpsimd

        P.dma("sp", lambda: nc.sync.dma_start(out=cst[:], in_=c_d), w=[r_cst], chan="misc")

        consts_fns = [
            lambda: PO.memset(identf[:], 1.0),
            lambda: PO.affine_select(out=identf[:], in_=identf[:], pattern=[[-1, 128]], compare_op=ALU.is_equal,
                                     fill=0.0, base=0, channel_multiplier=1),
            lambda: PO.tensor_copy(out=identb[:], in_=identf[:]),
            lambda: PO.memset(onesb[:], 1.0),
            lambda: PO.memset(onesblk[:], 0.0),
            lambda: PO.memset(onesblk[0:64, 0:64], 1.0),
            lambda: PO.memset(onesblk[64:128, 64:128], 1.0),
            lambda: PO.memset(mask0[:], 0.0),
            lambda: PO.memset(mask0[0:64, 64:128], NEG),
            lambda: PO.memset(mask4[:], 0.0),
            lambda: PO.memset(mask4[64:128, 0:64], NEG),
            lambda: PO.memset(cmask[:], 1.0),
            lambda: PO.affine_select(out=cmask[:], in_=cmask[:], pattern=[[0, 8], [1, 64]], compare_op=ALU.is_ge,
                                     fill=0.0, base=0, channel_multiplier=-1),
            lambda: PO.memset(scanm[:], 1.0),
            lambda: PO.memset(scanm[:].rearrange("p (c l) -> p c l", l=64)[:, :, 0:1], 0.0),
            lambda: PO.memset(qnz[:], 0.0),
            lambda: PO.memset(Vb[:], 1.0),
            lambda: PO.memset(Kb[:], 0.0),
        ]
        P.chain("pool", consts_fns, w=[r_const, r_Kprev, r_Vprev] + r_qnz + r_Kcur + r_Vcur)

        def mk_lb():
            lg = cst[:, C_LB:C_LB + 16].rearrange("p (l h) -> p l h", h=4)
            return A.activation(out=lbtmp[:], in_=lg, func=AF.Exp)
        P.add("act", mk_lb, r=[r_cst], w=[r_lbtmp])

        lb_fns = [
            lambda: V.tensor_tensor(out=lbs[:], in0=lbtmp[:, 0, :], in1=lbtmp[:, 1, :], op=ALU.add),
            lambda: V.tensor_tensor(out=lbs[:], in0=lbs[:], in1=lbtmp[:, 2, :], op=ALU.add),
            lambda: V.tensor_tensor(out=lbs[:], in0=lbs[:], in1=lbtmp[:, 3, :], op=ALU.add),
            lambda: V.reciprocal(out=lbs[:], in_=lbs[:]),
            lambda: V.memset(lbt[:, 0, :], 0.0),
            lambda: V.tensor_tensor(out=lbt[:, 1, :], in0=lbtmp[:, 1, :], in1=lbs[:], op=ALU.mult),
            lambda: V.tensor_tensor(out=lbt[:, 2, :], in0=lbtmp[:, 2, :], in1=lbs[:], op=ALU.mult),
            lambda: V.tensor_tensor(out=lbt[:, 2, :], in0=lbt[:, 2, :], in1=lbt[:, 1, :], op=ALU.add),
            lambda: V.tensor_tensor(out=lbt[:, 3, :], in0=lbtmp[:, 3, :], in1=lbs[:], op=ALU.mult),
            lambda: V.tensor_tensor(out=lbt[:, 3, :], in0=lbt[:, 3, :], in1=lbt[:, 2, :], op=ALU.add),
            lambda: V.tensor_scalar(out=omlt[:], in0=lbt[:], scalar1=-1.0, scalar2=1.0, op0=ALU.mult, op1=ALU.add),
        ]
        P.chain("dve", lb_fns, r=[r_lbtmp], w=[r_const])

        def precast(l):
            for gi, (gn, gc) in enumerate(GROUPS):
                o = GOFF[gn][0]
                P.dma("pool", lambda l=l, o=o, gc=gc: PO.dma_start(out=wb_d[l, :, o:o + gc], in_=w_d[l, :, o:o + gc],
                                                                  max_dma_last_dim=8192),
                      w=[r_wbf[l][gi]], chan="pc", nslots=8)
        precast(0)

        def load_group(l, gn):
            gi = GIDX[gn]
            o, gc = GOFF[gn]
            s = state["ring"]
            state["ring"] = (s + 1) % NRING
            P.dma("sp", lambda: nc.sync.dma_start(out=ring[:, s, 0:gc], in_=wb_d[l, :, o:o + gc]),
                  r=[r_wbf[l][gi]], w=[r_ring[s]], chan="ring%d" % s, nslots=1)
            return s

        def mm_group_split(b, pairs, rk, r_other):
            n = len(pairs)
            for i, (lt, rh) in enumerate(pairs):
                P.add("pe", lambda i=i, lt=lt, rh=rh: PE.matmul(PSF[:, b * 512:(b + 1) * 512], lhsT=lt, rhs=rh, start=(i == 0), stop=(i == n - 1)),
                      r=[rk[i]] + list(r_other), w=r_bank[b])

        def mm_group(b, col0, ncol, pairs, r, part=128, qs=None):
            def fn():
                n = len(pairs)
                ins = None
                for i, (lt, rh) in enumerate(pairs):
                    ins = PE.matmul(PSF[0:part, b * 512 + col0:b * 512 + col0 + ncol], lhsT=lt, rhs=rh,
                                    start=(i == 0), stop=(i == n - 1))
                return ins
            return P.add("pe", fn, r=r, w=r_bank[b])

        NORM_BANK = 5

        def norm_acc(c, bn):
            si = nsq()
            P.add("act", lambda: A.activation(out=sq[si][:], in_=xT[:, c, :], func=AF.Square),
                  r=[r_xT[c]], w=[r_sq[si]])

            def mm():
                P.add("pe", lambda: PE.matmul(bankap(bn), lhsT=onesb[:], rhs=sq[si][:],
                                              start=(c == 0), stop=(c == NCH - 1)),
                      r=[r_sq[si], r_const], w=r_bank[bn])
            return mm

        def norm_finish(l, cbase, bn):
            t = tmp32()
            P.chain("act", [lambda: A.activation(out=f32t[t][:], in_=bankap(bn), func=AF.Ln, scale=1.0 / D, bias=epsc[:, 0:1]),
                            lambda: A.activation(out=f32t[t][:], in_=f32t[t][:], func=AF.Exp, scale=-0.5)],
                    r=r_bank[bn] + [r_const], w=[r_f32t[t]])
            for c in range(NCH):
                P.add("dve", lambda c=c: V.scalar_tensor_tensor(out=hT[:, c, :], in0=xT[:, c, :],
                                                                scalar=cst[:, cbase + l * 8 + c:cbase + l * 8 + c + 1],
                                                                in1=f32t[t][:], op0=ALU.mult, op1=ALU.mult),
                      r=[r_xT[c], r_f32t[t], r_cst], w=[r_hT[c]])

        def rmsnorm(l, cbase):
            if state.get("prenorm"):
                state["prenorm"] = False
                norm_finish(l, cbase, NORM_BANK)
                return
            bn = bank()
            for c in range(NCH):
                norm_acc(c, bn)()
            norm_finish(l, cbase, bn)

        class FusedNorm:
            def __init__(self, enable=True):
                self.enable = enable
                self.pending = None
                if enable:
                    state["pool_n"] = 5
                    state["bank"] = state["bank"] % 5

            def chunk_done(self, c):
                if not self.enable:
                    return
                if self.pending is not None:
                    self.pending()
                self.pending = norm_acc(c, NORM_BANK)

            def close(self):
                if not self.enable:
                    return
                self.pending()
                state["pool_n"] = 6
                state["prenorm"] = True

        epsc = sb("epsc", [128, 4], F32)

        def mk_eps():
            V.memset(epsc[:, 0:1], EPS)
            V.memset(epsc[:, 1:2], 64.0 * EPS)
            return V.memset(epsc[:, 2:3], 0.0)
        P.add("dve", mk_eps, w=[r_const])

        def recip_lp(out, in_):
            with nc.allow_low_precision(reason="fp32 reciprocal, bf16 store of a matmul operand"):
                return V.reciprocal(out=out, in_=in_)

        def dump(name, ap, shape, dt, r):
            if not debug:
                return
            d = nc.dram_tensor("dbg_" + name, shape, dt, kind="ExternalOutput").ap()
            dbg_d[name] = d
            op = P.dma("pool", lambda: PO.dma_start(out=d, in_=ap), r=r, w=[], chan="dbg", nslots=2)
            op.is_out = True

        for ti in range(n_tiles):
            t0 = ti * TB
            for tg in range(4):
                si = nstg()
                P.dma("pool", lambda tg=tg, si=si, t0=t0: PO.dma_start(out=stg[si][:], in_=x_d[t0 + tg * 128:t0 + (tg + 1) * 128, :]),
                      w=[r_stg[si]], chan="xin", nslots=2)
                for half in range(2):
                    b = bank()

                    def ftr(tg=tg, si=si, half=half, b=b):
                        ins = None
                        for cc in range(4):
                            c = half * 4 + cc
                            ins = PE.transpose(PSF[:, b * 512 + cc * 128:b * 512 + (cc + 1) * 128],
                                               stg[si][:, c * 128:(c + 1) * 128], identf[:])
                        return ins
                    P.add("pe", ftr, r=[r_stg[si], r_const], w=r_bank[b])
                    P.add("act", lambda tg=tg, half=half, b=b: A.activation(
                        out=xT[:, half * 4:half * 4 + 4, tg * 128:(tg + 1) * 128],
                        in_=bankap(b).rearrange("p (c t) -> p c t", t=128), func=AF.Copy),
                        r=r_bank[b], w=r_xT[half * 4:half * 4 + 4])

            def layer(ti, l, t0):
                import os
                STG = int(os.environ.get("KSTAGE", "99"))
                if ti == 0 and l + 1 < n_layers:
                    precast(l + 1)
                if STG < 1:
                    return
                if ti > 0:
                    P.dma("pool", lambda l=l: PO.dma_start(out=Kb[:, :, 0:TB], in_=kst_d[l].rearrange("p (a t) -> p a t", t=TB)),
                          r=[r_kst[l]], w=[r_Kprev], chan="kvld", nslots=2)
                    P.dma("pool", lambda l=l: PO.dma_start(out=Vb[:, 0:4, :, :], in_=vst_d[l].rearrange("p (a h e) -> p a h e", h=8, e=VE)),
                          r=[r_vst[l]], w=[r_Vprev], chan="kvld", nslots=2)
                p_si = nstg()
                P.dma("pool", lambda: PO.dma_start(out=stg[p_si][:].rearrange("p (g f) -> p g f", g=4),
                                                   in_=p_d[l, t0:t0 + TB, :].rearrange("(g p) f -> p g f", p=128)),
                      w=[r_stg[p_si]], chan="xin", nslots=2)
                rmsnorm(l, C_GMIX)
                if STG < 2:
                    return
                zslots = {}

                def inproj_fm(gn, evac, first=False):
                    s = load_group(l, gn)
                    wv = ring[:, s, :].rearrange("p (j k c) -> p j k c", j=4, k=8)
                    for j in range(4):
                        b = bank()
                        if first and j == 0:
                            mm_group_split(b, [(wv[:, j, k, :], hT[:, k, :]) for k in range(8)], r_hT, [r_ring[s]])
                        else:
                            mm_group(b, 0, 512, [(wv[:, j, k, :], hT[:, k, :]) for k in range(8)], r=r_hT + [r_ring[s]])
                        evac(j, b)

                fillers = []

                def add_fm_fillers(gn, evac):
                    st = {}
                    for j in range(4):
                        def fl(j=j):
                            if "s" not in st:
                                st["s"] = load_group(l, gn)
                            s = st["s"]
                            wv = ring[:, s, :].rearrange("p (j k c) -> p j k c", j=4, k=8)
                            b = bank()
                            mm_group(b, 0, 512, [(wv[:, j, k, :], hT[:, k, :]) for k in range(8)], r=r_hT + [r_ring[s]])
                            evac(j, b)
                        fillers.append(fl)

                inproj_fm("f", lambda j, b: P.add("act", lambda: A.activation(out=sigA[j][:], in_=bankap(b), func=AF.Sigmoid),
                                                  r=r_bank[b], w=[r_sigA[j]]), first=True)
                inproj_fm("q", lambda j, b: P.add("act", lambda: A.activation(out=qbf[j][:], in_=bankap(b), func=AF.Copy, scale=128.0 ** -0.5),
                                                  r=r_bank[b], w=[r_qbf[j]]))

                def qk_evac(kind):
                    def ev(j, b):
                        t = tmp32()
                        si = nsq()
                        P.add("act", lambda: A.activation(out=sq[si][:], in_=bankap(b), func=AF.Square), r=r_bank[b], w=[r_sq[si]])
                        P.add("act", lambda: A.activation(out=f32t[t][:], in_=bankap(b), func=AF.Copy), r=r_bank[b], w=[r_f32t[t]])
                        b2 = bank()
                        P.add("pe", lambda: PE.matmul(bankap(b2), lhsT=onesblk[:], rhs=sq[si][:], start=True, stop=True),
                              r=[r_sq[si], r_const], w=r_bank[b2])
                        t2 = tmp32()

                        def f1():
                            if kind == "q":
                                return A.activation(out=f32t[t2][:], in_=bankap(b2), func=AF.Ln, scale=1.0, bias=epsc[:, 1:2])
                            return A.activation(out=f32t[t2][:], in_=bankap(b2), func=AF.Ln, scale=1.0 / 64, bias=epsc[:, 0:1])
                        P.chain("act", [f1, lambda: A.activation(out=f32t[t2][:], in_=f32t[t2][:], func=AF.Exp, scale=-0.5)],
                                r=r_bank[b2] + [r_const], w=[r_f32t[t2]])
                        if kind == "q":
                            def g():
                                V.scalar_tensor_tensor(out=qnz[0:64, 2 * j, :], in0=f32t[t][0:64, :], scalar=cst[0:64, C_QN + l:C_QN + l + 1],
                                                       in1=f32t[t2][0:64, :], op0=ALU.mult, op1=ALU.mult)
                                return V.scalar_tensor_tensor(out=qnz[64:128, 2 * j + 1, :], in0=f32t[t][64:128, :],
                                                              scalar=cst[64:128, C_QN + l:C_QN + l + 1],
                                                              in1=f32t[t2][64:128, :], op0=ALU.mult, op1=ALU.mult)
                            P.add("dve", g, r=[r_f32t[t], r_f32t[t2], r_cst], w=[r_qnz[2 * j], r_qnz[2 * j + 1]])
                        else:
                            P.add("dve", lambda: V.scalar_tensor_tensor(out=Kb[:, j, TB:2 * TB], in0=f32t[t][:], scalar=cst[:, C_KN + l:C_KN + l + 1],
                                                                        in1=f32t[t2][:], op0=ALU.mult, op1=ALU.mult),
                                  r=[r_f32t[t], r_f32t[t2], r_cst], w=[r_Kcur[j]])
                    return ev
                add_fm_fillers("aq", qk_evac("q"))
                add_fm_fillers("ak", qk_evac("k"))
                stv = {}
                for tg in range(4):
                    def flv(tg=tg):
                        if "s" not in stv:
                            stv["s"] = load_group(l, "av")
                        s = stv["s"]
                        wv = ring[:, s, :].rearrange("p (k c) -> p k c", k=8)
                        b = bank()
                        mm_group(b, 0, 512, [(hT[:, k, tg * 128:(tg + 1) * 128], wv[:, k, :]) for k in range(8)], r=r_hT + [r_ring[s]])
                        P.add("act", lambda: A.activation(out=Vb[:, 4 + tg, :, 0:64], in_=bankap(b).rearrange("p (h e) -> p h e", e=64), func=AF.Copy),
                              r=r_bank[b], w=[r_Vcur[tg]])
                    fillers.append(flv)

                def gate_evac(idx0):
                    def ev(j, b):
                        t = tmp32()
                        P.add("act", lambda: A.activation(out=f32t[t][:], in_=bankap(b), func=AF.Exp, scale=-1.0), r=r_bank[b], w=[r_f32t[t]])
                        P.chain("act", [lambda: A.activation(out=f32t[t][:], in_=f32t[t][:], func=AF.Ln, bias=1.0),
                                        lambda: A.activation(out=big[:, idx0 + j, :], in_=f32t[t][:], func=AF.Exp, scale=-1.0)],
                                r=[r_f32t[t]], w=[r_f32t[t], r_big[idx0 + j]])
                    return ev
                for gi_, gn in enumerate(("ga0", "ga1", "gb0", "gb1")):
                    add_fm_fillers(gn, gate_evac(gi_ * 4))

                if STG < 3:
                    return
                def hgrn_head(h):
                    hp = h % 2
                    P.add("dve", lambda h=h: V.tensor_scalar(out=sigA[h][:], in0=sigA[h][:], scalar1=omlt[:, l, h:h + 1], scalar2=lbt[:, l, h:h + 1],
                                                             op0=ALU.mult, op1=ALU.add), r=[r_sigA[h], r_const], w=[r_sigA[h]])
                    yield "s0"
                    P.add("act", lambda h=h: A.activation(out=hL[:], in_=sigA[h][:], func=AF.Ln), r=[r_sigA[h]], w=[r_hL])
                    yield "s0"
                    P.add("dve", lambda h=h: V.tensor_tensor_scan(out=hB[:], data0=scanm[:], data1=hL[:], initial=0.0, op0=ALU.mult, op1=ALU.add),
                          r=[r_hL, r_const], w=[r_hB])
                    bv = hB[:].rearrange("p (c t) -> p c t", t=64)
                    P.add("dve", lambda h=h, bv=bv: V.tensor_tensor(out=hF[:].rearrange("p (c t) -> p c t", t=64), in0=bv,
                                                                    in1=bv[:, :, 63:64].to_broadcast([128, 8, 64]), op=ALU.subtract),
                          r=[r_hB], w=[r_hF])
                    yield "s0"
                    P.add("act", lambda h=h: A.activation(out=sigA[h][:], in_=sigA[h][:], func=AF.Identity, scale=-1.0, bias=1.0),
                          r=[r_sigA[h]], w=[r_sigA[h]])

                    def fexp(h=h):
                        A.activation(out=hL[:], in_=hB[:], func=AF.Exp)
                        return A.activation(out=hE[:], in_=hB[:], func=AF.Exp, scale=-1.0)
                    P.add("act", fexp, r=[r_hB], w=[r_hL, r_hE])
                    P.add("act", lambda h=h: A.activation(out=hF[:], in_=hF[:], func=AF.Exp, scale=-1.0), r=[r_hF], w=[r_hF])
                    yield "s0"

                    P.add("dve", lambda h=h, hp=hp: V.tensor_tensor(out=kdT[hp][:], in0=sigA[h][:], in1=hF[:], op=ALU.mult),
                          r=[r_hF, r_sigA[h]], w=[r_kdT[hp]])
                    P.add("pool", lambda h=h, hp=hp: PO.tensor_tensor(out=keT[hp][:], in0=sigA[h][:], in1=hE[:], op=ALU.mult),
                          r=[r_hE, r_sigA[h]], w=[r_keT[hp]])
                    P.add("dve", lambda h=h, hp=hp: V.tensor_tensor(out=qeT[hp][:], in0=qbf[h][:], in1=hL[:], op=ALU.mult),
                          r=[r_qbf[h], r_hL], w=[r_qeT[hp]])
                    P.add("dve", lambda h=h, hp=hp: V.tensor_copy(out=acol[hp][:], in_=hL[:, 63::64]), r=[r_hL], w=[r_acol[hp]])
                    if ti == 0:
                        P.add("pool", lambda hp=hp: PO.memset(Sall[hp][:, 0, :], 0.0), w=[r_Sall[hp]])
                    else:
                        P.dma("pool", lambda l=l, h=h, hp=hp: PO.dma_start(out=Sall[hp][:, 0, :], in_=sst_d[l, :, h * 128:(h + 1) * 128]),
                              r=[r_sst[l]], w=[r_Sall[hp]], chan="sld", nslots=2)
                    yield "x"
                    for (src, rsrc, dst, rdst) in ((kdT[hp], r_kdT[hp], kdtok[hp], r_kdtok[hp]), (iTb[h], r_iT[h], itok[hp], r_itok[hp])):
                        bb = bbank()

                        def ftr(src=src, bb=bb):
                            ins = None
                            for c in range(8):
                                ins = PE.transpose(PSB[0:64, bb * 1024 + c * 128:bb * 1024 + (c + 1) * 128], src[:, c * 64:(c + 1) * 64], identb[:])
                            return ins
                        P.add("pe", ftr, r=[rsrc, r_const], w=[r_bbank[bb]])
                        P.add("act", lambda dst=dst, bb=bb: A.activation(out=dst[:].rearrange("p c k -> p (c k)"), in_=PSB[0:64, bb * 1024:(bb + 1) * 1024], func=AF.Copy),
                              r=[r_bbank[bb]], w=[rdst])
                    yield "x"
                    B0, B1 = 2 + 2 * hp, 3 + 2 * hp
                    ba = B0

                    def fat(hp=hp, ba=ba):
                        ins = None
                        for c in range(8):
                            ins = PE.matmul(PSF[0:64, ba * 512 + c * 64:ba * 512 + (c + 1) * 64], lhsT=keT[hp][:, c * 64:(c + 1) * 64],
                                            rhs=qeT[hp][:, c * 64:(c + 1) * 64], start=True, stop=True)
                        return ins
                    P.add("pe", fat, r=[r_keT[hp], r_qeT[hp]], w=r_bank[ba])
                    P.add("dve", lambda hp=hp, ba=ba: V.tensor_tensor(out=attm[hp][:], in0=PSF[0:64, ba * 512:(ba + 1) * 512], in1=cmask[:], op=ALU.mult),
                          r=r_bank[ba] + [r_const], w=[r_attm[hp]])
                    for rnd in range(2):
                        b = B1 if rnd == 0 else B0

                        def fdl(hp=hp, b=b, rnd=rnd):
                            ins = None
                            for cc in range(4):
                                c = rnd * 4 + cc
                                ins = PE.matmul(PSF[:, b * 512 + cc * 128:b * 512 + (cc + 1) * 128], lhsT=kdtok[hp][:, c, :], rhs=itok[hp][:, c, :],
                                                start=True, stop=True)
                            return ins
                        P.add("pe", fdl, r=[r_kdtok[hp], r_itok[hp]], w=r_bank[b])
                        yield "x"
                        for cc in range(4):
                            c = rnd * 4 + cc
                            P.add("dve", lambda hp=hp, c=c, cc=cc, b=b: V.scalar_tensor_tensor(
                                out=Sall[hp][:, c + 1, :], in0=Sall[hp][:, c, :], scalar=acol[hp][:, c:c + 1],
                                in1=PSF[:, b * 512 + cc * 128:b * 512 + (cc + 1) * 128], op0=ALU.mult, op1=ALU.add),
                                r=[r_Sall[hp], r_acol[hp]] + r_bank[b], w=[r_Sall[hp]])
                    P.add("act", lambda hp=hp: A.activation(out=Sbf[hp][:], in_=Sall[hp][:, 0:8, :], func=AF.Copy), r=[r_Sall[hp]], w=[r_Sbf[hp]])
                    if ti + 1 < n_tiles:
                        P.dma("pool", lambda l=l, h=h, hp=hp: PO.dma_start(out=sst_d[l, :, h * 128:(h + 1) * 128], in_=Sall[hp][:, 8, :]),
                              r=[r_Sall[hp]], w=[r_sst[l]], chan="sst", nslots=2)
                    yield "x"
                    bo = B1

                    def fo(hp=hp, bo=bo):
                        ins = None
                        for c in range(8):
                            PE.matmul(PSF[:, bo * 512 + c * 64:bo * 512 + (c + 1) * 64], lhsT=Sbf[hp][:, c, :], rhs=qeT[hp][:, c * 64:(c + 1) * 64],
                                      start=True, stop=False)
                            ins = PE.matmul(PSF[:, bo * 512 + c * 64:bo * 512 + (c + 1) * 64], lhsT=itok[hp][:, c, :], rhs=attm[hp][:, c * 64:(c + 1) * 64],
                                            start=False, stop=True)
                        return ins
                    P.add("pe", fo, r=[r_Sbf[hp], r_qeT[hp], r_itok[hp], r_attm[hp]], w=r_bank[bo])
                    yield "x"
                    si = nsq()
                    P.add("act", lambda si=si, bo=bo: A.activation(out=sq[si][:], in_=bankap(bo), func=AF.Square), r=r_bank[bo], w=[r_sq[si]])
                    bn = B0
                    P.add("pe", lambda si=si, bn=bn: PE.matmul(bankap(bn), lhsT=onesb[:], rhs=sq[si][:], start=True, stop=True),
                          r=[r_sq[si], r_const], w=r_bank[bn])
                    t = 4
                    yield "x"

                    P.chain("act", [lambda t=t, bn=bn: A.activation(out=f32t[t][:], in_=bankap(bn), func=AF.Ln, scale=1.0 / 128, bias=epsc[:, 0:1]),
                                    lambda t=t: A.activation(out=f32t[t][:], in_=f32t[t][:], func=AF.Exp, scale=-0.5)],
                            r=r_bank[bn] + [r_const], w=[r_f32t[t]])
                    yield "x"

                    P.chain("dve", [lambda t=t, bo=bo: V.scalar_tensor_tensor(out=f32t[t][:], in0=bankap(bo), scalar=cst[:, C_ON + l:C_ON + l + 1], in1=f32t[t][:],
                                                                              op0=ALU.mult, op1=ALU.mult),
                                    lambda h=h, t=t: V.tensor_tensor(out=oaT[:, h, :], in0=f32t[t][:], in1=sgb16[h][:], op=ALU.mult)],
                            r=r_bank[bo] + [r_f32t[t], r_cst, r_sg[h]], w=[r_f32t[t], r_oaT[h]])

                def inproj_gi():
                    inproj_fm("g", lambda j, b: P.add("act", lambda: A.activation(out=sgb16[j][:], in_=bankap(b), func=AF.Silu),
                                                      r=r_bank[b], w=[r_sg[j]]))
                    inproj_fm("i", lambda j, b: P.add("act", lambda: A.activation(out=iTb[j][:], in_=bankap(b), func=AF.Copy),
                                                      r=r_bank[b], w=[r_iT[j]]))
                gens = [hgrn_head(h) for h in range(4)]
                state["pool_n"] = 2
                state["bank"] = state["bank"] % 2

                def F(n=1):
                    for _ in range(n):
                        if fillers:
                            fillers.pop(0)()
                started, done, in_s0 = set(), set(), [None]
                first = [True]
                stepc = [0]

                def can_start(h):
                    return h not in started and in_s0[0] is None and (h < 2 or (h - 2) in done)
                while len(done) < 4:
                    progressed = False
                    for h in range(4):
                        if h in done:
                            continue
                        if h not in started:
                            if not can_start(h):
                                continue
                            started.add(h)
                            in_s0[0] = h
                        tag = next(gens[h], None)
                        progressed = True
                        if tag is None:
                            done.add(h)
                            if in_s0[0] == h:
                                in_s0[0] = None
                        elif tag == "x" and in_s0[0] == h:
                            in_s0[0] = None
                        if first[0] and 0 in started and in_s0[0] is None:
                            inproj_gi()
                            first[0] = False
                        stepc[0] += 1
                        if stepc[0] % 2 == 0 or stepc[0] <= 6:
                            F(1)
                    assert progressed
                if first[0]:
                    inproj_gi()
                F(100)
                for g_ in gens:
                    for _ in g_:
                        pass
                state["pool_n"] = 6
                if ti + 1 < n_tiles:
                    P.dma("pool", lambda l=l: PO.dma_start(out=kst_d[l].rearrange("p (a t) -> p a t", t=TB), in_=Kb[:, :, TB:2 * TB]),
                          r=r_Kcur, w=[r_kst[l]], chan="kvst", nslots=2)
                    P.dma("pool", lambda l=l: PO.dma_start(out=vst_d[l].rearrange("p (a h e) -> p a h e", h=8, e=VE), in_=Vb[:, 4:8, :, :]),
                          r=r_Vcur, w=[r_vst[l]], chan="kvst", nslots=2)
                s_ab = [load_group(l, "ab0"), load_group(l, "ab1")]
                if STG < 4:
                    return
                if debug and ti == n_tiles - 1 and l == 0:
                    dump("ab0", ring[:, s_ab[0], 0:2560], [128, 2560], BF16, [r_ring[s_ab[0]]])
                abv = [ring[:, s_ab[0], 0:2560].rearrange("p (h j q) -> p h j q", h=4, j=5),
                       ring[:, s_ab[1], 0:2560].rearrange("p (h j q) -> p h j q", h=4, j=5)]
                rk = [r_Kprev] + r_Kcur

                def att_scores(pr, h, pi):
                    hg, hh = h // 4, h % 4
                    j0 = max(0, 4 - (ti * 4 + pr))
                    bs, b4 = h % 2, 2 + h % 2

                    def fsc():
                        ins = None
                        for j in range(j0, 5):
                            dst = PSF[:, bs * 512 + j * 128:bs * 512 + (j + 1) * 128] if j < 4 else PSF[:, b4 * 512:b4 * 512 + 128]
                            PE.matmul(dst, lhsT=Kb[:, h // 2, (pr + j) * 128:(pr + j + 1) * 128], rhs=qnz[:, h, pr * 128:(pr + 1) * 128],
                                      start=True, stop=False)
                            msk = (j == 0) or (j == 4)
                            ins = PE.matmul(dst, lhsT=identb[:], rhs=abv[hg][:, hh, j, :], start=False, stop=not msk)
                            if msk:
                                ins = PE.matmul(dst, lhsT=identb[:], rhs=(mask0 if j == 0 else mask4)[:], start=False, stop=True)
                        return ins
                    P.add("pe", fsc, r=rk + [r_qnz[h], r_ring[s_ab[hg]], r_const], w=r_bank[bs] + r_bank[b4])

                    def fex():
                        if j0 < 4:
                            A.activation(out=Pt[pi][:, j0 * 128:512], in_=PSF[:, bs * 512 + j0 * 128:(bs + 1) * 512], func=AF.Exp)
                        return A.activation(out=Pt[pi][:, 512:640], in_=PSF[:, b4 * 512:b4 * 512 + 128], func=AF.Exp)
                    P.add("act", fex, r=r_bank[bs] + r_bank[b4], w=[r_Pt[pi]])

                def att_pv(pr, h, pi):
                    hg, hh = h // 4, h % 4
                    j0 = max(0, 4 - (ti * 4 + pr))
                    bov = 4 + hg
                    oi = hg

                    def fpv():
                        ins = None
                        for j in range(j0, 5):
                            ins = PE.matmul(PSF[:, bov * 512 + hh * VE:bov * 512 + (hh + 1) * VE], lhsT=Pt[pi][:, j * 128:(j + 1) * 128],
                                            rhs=Vb[:, pr + j, h, :], start=(j == j0), stop=(j == 4))
                        return ins
                    P.add("pe", fpv, r=[r_Pt[pi], r_Vprev] + r_Vcur, w=r_bank[bov])
                    if hh == 3:
                        def fnm1():
                            pv = PSF[:, bov * 512:bov * 512 + 4 * VE].rearrange("p (h e) -> p h e", e=VE)
                            return V.reciprocal(out=rden[oi][:], in_=pv[:, :, 64:65])

                        def fnm2():
                            pv = PSF[:, bov * 512:bov * 512 + 4 * VE].rearrange("p (h e) -> p h e", e=VE)
                            return V.tensor_tensor(out=obt[pr % 2][:, hg * 256:(hg + 1) * 256].rearrange("p (h e) -> p h e", e=64), in0=pv[:, :, 0:64],
                                                   in1=rden[oi][:].to_broadcast([128, 4, 64]), op=ALU.mult)
                        P.chain("dve", [fnm1, fnm2], r=r_bank[bov], w=[r_rden[oi], r_obt[pr % 2]])
                    if h == 7:
                        bb = bbank()

                        def ftr():
                            ins = None
                            for fc in range(4):
                                ins = PE.transpose(PSB[:, bb * 1024 + fc * 128:bb * 1024 + (fc + 1) * 128], obt[pr % 2][:, fc * 128:(fc + 1) * 128], identb[:])
                            return ins
                        P.add("pe", ftr, r=[r_obt[pr % 2], r_const], w=[r_bbank[bb]])
                        P.add("act", lambda: A.activation(out=obT[:, :, pr * 128:(pr + 1) * 128],
                                                          in_=PSB[:, bb * 1024:bb * 1024 + 512].rearrange("p (c t) -> p c t", t=128), func=AF.Copy),
                              r=[r_bbank[bb]], w=r_obT)

                steps = [(pr, h) for pr in range(4) for h in range(8)]
                LAG = 1
                for i in range(len(steps) + LAG):
                    if i < len(steps):
                        att_scores(steps[i][0], steps[i][1], i % 3)
                    if i >= LAG:
                        att_pv(steps[i - LAG][0], steps[i - LAG][1], (i - LAG) % 3)

                if STG < 5:
                    return
                sa = load_group(l, "wa")
                sb_ = load_group(l, "wb")
                wav = ring[:, sa, :].rearrange("p (j k c) -> p j k c", j=8, k=4)
                wbv = ring[:, sb_, :].rearrange("p (j k c) -> p j k c", j=8, k=4)
                for oc in range(8):
                    ba_, bb_ = bank(), bank()
                    mm_group(ba_, 0, 512, [(wav[:, oc, k, :], oaT[:, k, :]) for k in range(4)], r=r_oaT + [r_ring[sa]])
                    mm_group(bb_, 0, 512, [(wbv[:, oc, k, :], obT[:, k, :]) for k in range(4)], r=r_obT + [r_ring[sb_]])
                    t1, t2 = tmp32(), tmp32()
                    P.add("dve", lambda oc=oc, ba_=ba_, t1=t1: V.tensor_tensor(out=f32t[t1][:], in0=bankap(ba_), in1=big[:, oc, :], op=ALU.mult),
                          r=r_bank[ba_] + [r_big[oc]], w=[r_f32t[t1]])
                    P.add("dve", lambda oc=oc, bb_=bb_, t2=t2: V.tensor_tensor(out=f32t[t2][:], in0=bankap(bb_), in1=big[:, 8 + oc, :], op=ALU.mult),
                          r=r_bank[bb_] + [r_big[8 + oc]], w=[r_f32t[t2]])
                    P.add("pool", lambda oc=oc, t1=t1, t2=t2: PO.tensor_tensor(out=hT[:, oc, :], in0=f32t[t1][:], in1=f32t[t2][:], op=ALU.add),
                          r=[r_f32t[t1], r_f32t[t2]], w=[r_hT[oc]])
                fn_ = FusedNorm()
                for half in range(2):
                    s = load_group(l, "wo%d" % half)
                    wv = ring[:, s, :].rearrange("p (j k c) -> p j k c", j=4, k=8)
                    for j in range(4):
                        oc = half * 4 + j
                        b = bank()
                        mm_group(b, 0, 512, [(wv[:, j, k, :], hT[:, k, :]) for k in range(8)], r=r_hT + [r_ring[s]])
                        P.add("dve", lambda oc=oc, b=b: V.tensor_tensor(out=xT[:, oc, :], in0=xT[:, oc, :], in1=bankap(b), op=ALU.add),
                              r=r_bank[b] + [r_xT[oc]], w=[r_xT[oc]])
                        fn_.chunk_done(oc)
                fn_.close()
                if debug and ti == n_tiles - 1 and l == 0:
                    dump("oaT", oaT[:], [128, 4, TB], BF16, r_oaT)
                    dump("qnz", qnz[:], [128, 8, TB], BF16, r_qnz)
                    dump("Kb", Kb[:], [128, 4, 2 * TB], BF16, [r_Kprev] + r_Kcur)
                    dump("Vb", Vb[:], [128, 8, 8, VE], BF16, [r_Vprev] + r_Vcur)
                    dump("obt1", obt[1][:], [128, 512], BF16, [r_obt[1]])
                    dump("Pt1", Pt[1][:], [128, 640], BF16, [r_Pt[1]])
                    dump("obT", obT[:], [128, 4, TB], BF16, r_obT)
                    dump("x1", xT[:], [128, 8, TB], F32, r_xT)

                if STG < 6:
                    return
                rmsnorm(l, C_GFFN)
                for tg in range(4):
                    b = bank()

                    def ftp(tg=tg, b=b):
                        PE.transpose(PSF[:, b * 512:b * 512 + 128], stg[p_si][:, tg * 256:tg * 256 + 128], identf[:])
                        return PE.transpose(PSF[:, b * 512 + 128:b * 512 + 256], stg[p_si][:, tg * 256 + 128:tg * 256 + 256], identf[:])
                    P.add("pe", ftp, r=[r_stg[p_si], r_const], w=r_bank[b])
                    P.add("act", lambda tg=tg, b=b: A.activation(out=pTb[:, :, tg * 128:(tg + 1) * 128],
                                                                 in_=PSF[:, b * 512:b * 512 + 256].rearrange("p (c t) -> p c t", t=128), func=AF.Copy),
                          r=r_bank[b], w=[r_pT])
                for i in range(11):
                    s = load_group(l, "ff%d" % i)
                    wv = ring[:, s, :].rearrange("p (j k c) -> p j k c", j=4, k=8)
                    for jj in range(2):
                        j = 2 * i + jj
                        bg, bu = bank(), bank()
                        if j == 0:
                            mm_group_split(bg, [(wv[:, 2 * jj, k, :], hT[:, k, :]) for k in range(8)], r_hT, [r_ring[s]])
                        else:
                            mm_group(bg, 0, 512, [(wv[:, 2 * jj, k, :], hT[:, k, :]) for k in range(8)], r=r_hT + [r_ring[s]])
                        mm_group(bu, 0, 512, [(wv[:, 2 * jj + 1, k, :], hT[:, k, :]) for k in range(8)], r=r_hT + [r_ring[s]])
                        t = tmp32()
                        P.add("act", lambda bg=bg, t=t: A.activation(out=f32t[t][:], in_=bankap(bg), func=AF.Silu), r=r_bank[bg], w=[r_f32t[t]])
                        P.add("dve", lambda j=j, bu=bu, t=t: V.tensor_tensor(out=big[:, j, :], in0=bankap(bu), in1=f32t[t][:], op=ALU.mult),
                              r=r_bank[bu] + [r_f32t[t]], w=[r_big[j]])
                fn_ = FusedNorm()
                for oc in range(8):
                    s = load_group(l, "dn%d" % oc)
                    wv = ring[:, s, 0:2816].rearrange("p (k c) -> p k c", k=22)
                    b = bank()
                    mm_group(b, 0, 512, [(wv[:, k, :], big[:, k, :]) for k in range(22)], r=r_big + [r_ring[s]])
                    P.add("dve", lambda oc=oc, b=b: V.tensor_tensor(out=xT[:, oc, :], in0=xT[:, oc, :], in1=bankap(b), op=ALU.add),
                          r=r_bank[b] + [r_xT[oc]], w=[r_xT[oc]])
                    fn_.chunk_done(oc)
                fn_.close()

                if STG < 7:
                    return
                rmsnorm(l, C_GPLE)
                spp = None
                fn_ = FusedNorm(enable=(l + 1 < n_layers))
                for half in range(2):
                    s = load_group(l, "pg%d" % half)
                    if half == 0:
                        spp = load_group(l, "pp")
                    wv = ring[:, s, :].rearrange("p (j k c) -> p j k c", j=4, k=8)
                    ppv = ring[:, spp, 0:2048].rearrange("p (j k c) -> p j k c", j=8, k=2)
                    for j in range(4):
                        oc = half * 4 + j
                        bg, bp = bank(), bank()
                        if oc == 0:
                            mm_group_split(bg, [(wv[:, j, k, :], hT[:, k, :]) for k in range(8)], r_hT, [r_ring[s]])
                        else:
                            mm_group(bg, 0, 512, [(wv[:, j, k, :], hT[:, k, :]) for k in range(8)], r=r_hT + [r_ring[s]])
                        mm_group(bp, 0, 512, [(ppv[:, oc, k, :], pTb[:, k, :]) for k in range(2)], r=[r_pT, r_ring[spp]])
                        t = tmp32()
                        P.add("act", lambda bg=bg, t=t: A.activation(out=f32t[t][:], in_=bankap(bg), func=AF.Sigmoid), r=r_bank[bg], w=[r_f32t[t]])
                        P.add("dve", lambda bp=bp, t=t: V.tensor_tensor(out=f32t[t][:], in0=bankap(bp), in1=f32t[t][:], op=ALU.mult),
                              r=r_bank[bp] + [r_f32t[t]], w=[r_f32t[t]])
                        P.add("pool", lambda oc=oc, t=t: PO.tensor_tensor(out=xT[:, oc, :], in0=xT[:, oc, :], in1=f32t[t][:], op=ALU.add),
                              r=[r_f32t[t], r_xT[oc]], w=[r_xT[oc]])
                        fn_.chunk_done(oc)
                fn_.close()

            for l in range(n_layers):
                layer(ti, l, t0)
            for tg in range(4):
                si = nstg()
                for half in range(2):
                    b = bank()

                    def ftr(tg=tg, half=half, b=b):
                        ins = None
                        for cc in range(4):
                            c = half * 4 + cc
                            ins = PE.transpose(PSF[:, b * 512 + cc * 128:b * 512 + (cc + 1) * 128], xT[:, c, tg * 128:(tg + 1) * 128], identf[:])
                        return ins
                    P.add("pe", ftr, r=r_xT[half * 4:half * 4 + 4] + [r_const], w=r_bank[b])
                    P.add("act", lambda si=si, half=half, b=b: A.activation(out=stg[si][:, half * 512:(half + 1) * 512], in_=bankap(b), func=AF.Copy),
                          r=r_bank[b], w=[r_stg[si]])
                op = P.dma("pool", lambda tg=tg, si=si, t0=t0: PO.dma_start(out=o_d[t0 + tg * 128:t0 + (tg + 1) * 128, :], in_=stg[si][:]),
                           r=[r_stg[si]], w=[r_out], chan="out", nslots=2)
                op.is_out = True

        nwait = P.emit()
    return nc, dbg_d


_CACHE = {}


def _run(inp, n_tiles, n_layers, ncores, debug=False, trace=False):
    wl, cst = _host_layout(inp)
    key = (n_tiles, n_layers, debug)
    if key not in _CACHE:
        _CACHE[key] = build(n_tiles, n_layers, debug)
    nc, dbg = _CACHE[key]
    S = n_tiles * TB
    in_maps = []
    for b in range(ncores):
        in_maps.append({"x": np.ascontiguousarray(inp["x"][b, :S]),
                        "p": np.ascontiguousarray(inp["p"][:, b, :S]),
                        "w": wl, "cst": cst})
    res = run_bass_kernel_spmd(nc, in_maps, core_ids=list(range(ncores)), trace=trace)
    return res


def kernel(**inputs):
    inp = {k: np.asarray(v) for k, v in inputs.items()}
    res = _run(inp, SEQ // TB, DEPTH, 8)
    out = np.stack([np.asarray(r["out"]) for r in res.results], axis=0)
    return out.astype(np.float32)
```
